# Optimizing a Trainium2 kernel written in Bass

```python
import math
import jax, jax.numpy as jnp
from jax import lax
import numpy as np

D_MODEL = 2048
BATCH = 4
SEQ = 2048
DEPTH = 4
DEC_BATCH = 128
DEC_SEQ = 1
PAST_LEN = 16384
PAGE_SIZE = 128

MIX_WIDTH = D_MODEL
DN_WIDTH = MIX_WIDTH // 2
DN_HEADS = 8
DN_HEAD_DIM = DN_WIDTH // DN_HEADS
CONV_W = 4
CHUNK = 64
POOL_WIDTH = MIX_WIDTH - DN_WIDTH
POOL_WINDOWS = (2, 4, 8, 16)
N_POOL_GROUPS = len(POOL_WINDOWS)
POOL_GROUP = POOL_WIDTH // N_POOL_GROUPS
POOL_BUF = max(POOL_WINDOWS) - 1
D_FF = -(-8 * D_MODEL // (3 * 256)) * 256
IN_COLS = 4 * DN_WIDTH + 2 * DN_HEADS + POOL_WIDTH
EPS = 1e-6

kernel_name = "hymba_gdn_pool_decoder_step"


def rmsnorm(x, w):
    xf = x.astype(jnp.float32)
    y = xf * lax.rsqrt(jnp.mean(xf * xf, axis=-1, keepdims=True) + EPS)
    return (y * w.astype(jnp.float32)).astype(x.dtype)


def l2norm(x):
    xf = x.astype(jnp.float32)
    return xf * lax.rsqrt(jnp.sum(xf * xf, axis=-1, keepdims=True) + EPS)


def short_conv(buf, x, w):
    ext = jnp.concatenate([buf.astype(x.dtype), x], axis=1)
    T = x.shape[1]
    y = ext[:, 0:T] * w[0]
    for i in range(1, CONV_W):
        y = y + ext[:, i:i + T] * w[i]
    return jax.nn.silu(y), ext[:, -(CONV_W - 1):]


def gated_delta_chunked(q, k, v, g, beta, S0):
    B, T, H, K = k.shape
    n = T // CHUNK

    def to_chunks(a):
        a = a.reshape((B, n, CHUNK, H) + a.shape[3:])
        return jnp.moveaxis(a, 3, 1)

    qc, kc, vc, bc = to_chunks(q), to_chunks(k), to_chunks(v), to_chunks(beta)
    gc = jnp.cumsum(to_chunks(g), axis=-1)
    tril = jnp.tril(jnp.ones((CHUNK, CHUNK), bool))
    strict = jnp.tril(jnp.ones((CHUNK, CHUNK), bool), -1)
    diff = gc[..., :, None] - gc[..., None, :]
    decay = jnp.where(tril, jnp.exp(jnp.where(tril, diff, 0.0)), 0.0)
    kb = kc * bc[..., None]
    m = jnp.where(strict, jnp.einsum('bhncd,bhnsd->bhncs', kb, kc) * decay, 0.0)
    a_mat = m + jnp.eye(CHUNK, dtype=m.dtype)
    u = lax.linalg.triangular_solve(a_mat, vc * bc[..., None], left_side=True,
                                    lower=True, unit_diagonal=True)
    w = lax.linalg.triangular_solve(a_mat, kb * jnp.exp(gc)[..., None], left_side=True,
                                    lower=True, unit_diagonal=True)
    qk = jnp.where(tril, jnp.einsum('bhncd,bhnsd->bhncs', qc, kc) * decay, 0.0)

    def step(S, xs):
        q_i, k_i, u_i, w_i, g_i, qk_i = xs
        v_new = u_i - jnp.einsum('bhck,bhkv->bhcv', w_i, S)
        o = (jnp.einsum('bhck,bhkv->bhcv', q_i * jnp.exp(g_i)[..., None], S)
             + jnp.einsum('bhcs,bhsv->bhcv', qk_i, v_new))
        g_last = g_i[..., -1]
        k_dec = k_i * jnp.exp(g_last[..., None] - g_i)[..., None]
        S = S * jnp.exp(g_last)[..., None, None] + jnp.einsum('bhck,bhcv->bhkv', k_dec, v_new)
        return S, o

    xs = tuple(jnp.moveaxis(a, 2, 0) for a in (qc, kc, u, w, gc, qk))
    S, o = lax.scan(step, S0, xs)
    o = jnp.transpose(o, (1, 0, 3, 2, 4)).reshape(B, T, H, -1)
    return o, S


def gated_delta_recurrent(q, k, v, g, beta, S0):
    def step(S, xs):
        q_t, k_t, v_t, g_t, b_t = xs
        S = S * jnp.exp(g_t)[..., None, None]
        kv = jnp.einsum('bhk,bhkv->bhv', k_t, S)
        S = S + jnp.einsum('bhk,bhv->bhkv', k_t, (v_t - kv) * b_t[..., None])
        o = jnp.einsum('bhk,bhkv->bhv', q_t, S)
        return S, o

    xs = tuple(jnp.moveaxis(a, 1, 0) for a in (q, k, v, g, beta))
    S, o = lax.scan(step, S0, xs)
    return jnp.moveaxis(o, 0, 1), S


def multiscale_pool(buf, p, start_pos, w_pool, pool_scale):
    B, T, C = p.shape
    ext = jnp.concatenate([buf.astype(p.dtype), p], axis=1)
    extf = ext.astype(jnp.float32)
    cs = jnp.concatenate([jnp.zeros((B, 1, C), jnp.float32), jnp.cumsum(extf, axis=1)], axis=1)
    pos = start_pos + jnp.arange(T)
    means = []
    for gi, win in enumerate(POOL_WINDOWS):
        lo, hi = gi * POOL_GROUP, (gi + 1) * POOL_GROUP
        s = (cs[:, POOL_BUF + 1:POOL_BUF + 1 + T, lo:hi]
             - cs[:, POOL_BUF + 1 - win:POOL_BUF + 1 - win + T, lo:hi])
        cnt = jnp.minimum(pos + 1, win).astype(jnp.float32)
        means.append(s / cnt[None, :, None])
    d = jnp.concatenate(means, axis=-1) - extf[:, POOL_BUF:]
    d = d.reshape(B, T, N_POOL_GROUPS, POOL_GROUP).astype(p.dtype)
    y = jnp.einsum('btgc,gcd->btgd', d, w_pool).reshape(B, T, POOL_WIDTH) * pool_scale
    return y, ext[:, -POOL_BUF:]


def decoder_layer(h, conv_buf, S0, pool_buf, start_pos, chunked,
                  norm_mix, w_in, conv_w, a_log, dt_bias, dn_norm, w_pool, pool_scale,
                  w_out, norm_ffn, w_gate_up, w_down):
    B, T, _ = h.shape
    xn = rmsnorm(h, norm_mix)
    proj = xn @ w_in
    o0 = 3 * DN_WIDTH
    o1 = 4 * DN_WIDTH
    qkv = proj[..., :o0]
    z = proj[..., o0:o1]
    b_raw = proj[..., o1:o1 + DN_HEADS]
    a_raw = proj[..., o1 + DN_HEADS:o1 + 2 * DN_HEADS]
    p = proj[..., o1 + 2 * DN_HEADS:]

    qkv_c, new_conv = short_conv(conv_buf, qkv, conv_w)
    q = qkv_c[..., :DN_WIDTH].reshape(B, T, DN_HEADS, DN_HEAD_DIM)
    k = qkv_c[..., DN_WIDTH:2 * DN_WIDTH].reshape(B, T, DN_HEADS, DN_HEAD_DIM)
    v = qkv_c[..., 2 * DN_WIDTH:].reshape(B, T, DN_HEADS, DN_HEAD_DIM).astype(jnp.float32)
    q = l2norm(q) * (DN_HEAD_DIM ** -0.5)
    k = l2norm(k)
    beta = jax.nn.sigmoid(b_raw.astype(jnp.float32))
    g = -jnp.exp(a_log.astype(jnp.float32)) * jax.nn.softplus(
        a_raw.astype(jnp.float32) + dt_bias.astype(jnp.float32))
    S0f = S0.astype(jnp.float32)
    if chunked:
        o, S = gated_delta_chunked(q, k, v, g, beta, S0f)
    else:
        o, S = gated_delta_recurrent(q, k, v, g, beta, S0f)
    zf = z.astype(jnp.float32).reshape(B, T, DN_HEADS, DN_HEAD_DIM)
    dn_out = (rmsnorm(o, dn_norm) * jax.nn.silu(zf)).reshape(B, T, DN_WIDTH).astype(h.dtype)

    pool_out, new_pool = multiscale_pool(pool_buf, p, start_pos, w_pool, pool_scale)

    h = h + jnp.concatenate([dn_out, pool_out.astype(h.dtype)], axis=-1) @ w_out

    xn2 = rmsnorm(h, norm_ffn)
    gu = xn2 @ w_gate_up
    h = h + (jax.nn.silu(gu[..., :D_FF]) * gu[..., D_FF:]) @ w_down
    return h, S.astype(h.dtype), new_conv, new_pool


def setup_inputs(seed: int = 0) -> dict:
    key = jax.random.key(seed)
    ks = jax.random.split(key, 20)
    f32 = jnp.float32

    def nrm(k, shape, scale):
        return jax.random.normal(k, shape, f32) * scale

    x_prompt = nrm(ks[0], (BATCH, SEQ, D_MODEL), 1.0)
    x_sample = nrm(ks[1], (DEC_BATCH, DEC_SEQ, D_MODEL), 1.0)
    state_delta = nrm(ks[2], (DEPTH, DEC_BATCH, DN_HEADS, DN_HEAD_DIM, DN_HEAD_DIM), DN_HEAD_DIM ** -0.5)
    state_conv = nrm(ks[3], (DEPTH, DEC_BATCH, CONV_W - 1, 3 * DN_WIDTH), 1.0)
    state_pool = nrm(ks[4], (DEPTH, DEC_BATCH, POOL_BUF, POOL_WIDTH), 1.0)
    norm_mix = 1.0 + nrm(ks[5], (DEPTH, D_MODEL), 0.02)
    w_in = nrm(ks[6], (DEPTH, D_MODEL, IN_COLS), D_MODEL ** -0.5)
    conv_w = nrm(ks[7], (DEPTH, CONV_W, 3 * DN_WIDTH), CONV_W ** -0.5)
    a_log = jnp.log(jax.random.uniform(ks[8], (DEPTH, DN_HEADS), f32, 1.0, 16.0))
    dt = jnp.exp(jax.random.uniform(ks[9], (DEPTH, DN_HEADS), f32, math.log(1e-3), math.log(1e-1)))
    dt_bias = dt + jnp.log(-jnp.expm1(-dt))
    dn_norm = 1.0 + nrm(ks[10], (DEPTH, DN_HEAD_DIM), 0.02)
    w_pool = nrm(ks[11], (DEPTH, N_POOL_GROUPS, POOL_GROUP, POOL_GROUP), POOL_GROUP ** -0.5)
    pool_scale = 1.0 + nrm(ks[12], (DEPTH, POOL_WIDTH), 0.1)
    w_out = nrm(ks[13], (DEPTH, MIX_WIDTH, D_MODEL), (2 * DEPTH * MIX_WIDTH) ** -0.5)
    norm_ffn = 1.0 + nrm(ks[14], (DEPTH, D_MODEL), 0.02)
    w_gate_up = nrm(ks[15], (DEPTH, D_MODEL, 2 * D_FF), D_MODEL ** -0.5)
    w_down = nrm(ks[16], (DEPTH, D_FF, D_MODEL), (2 * DEPTH * D_FF) ** -0.5)
    norm_final = 1.0 + nrm(ks[17], (D_MODEL,), 0.02)
    return {"x_prompt": x_prompt, "x_sample": x_sample,
            "state_delta": state_delta, "state_conv": state_conv, "state_pool": state_pool,
            "norm_mix": norm_mix, "w_in": w_in, "conv_w": conv_w, "a_log": a_log,
            "dt_bias": dt_bias, "dn_norm": dn_norm, "w_pool": w_pool, "pool_scale": pool_scale,
            "w_out": w_out, "norm_ffn": norm_ffn, "w_gate_up": w_gate_up, "w_down": w_down,
            "norm_final": norm_final}


def reference(x_prompt, x_sample, state_delta, state_conv, state_pool, norm_mix, w_in, conv_w,
              a_log, dt_bias, dn_norm, w_pool, pool_scale, w_out, norm_ffn, w_gate_up, w_down,
              norm_final):
    hp, hs = x_prompt, x_sample
    dtp = x_prompt.dtype
    zero_S = jnp.zeros((BATCH, DN_HEADS, DN_HEAD_DIM, DN_HEAD_DIM), jnp.float32)
    zero_conv = jnp.zeros((BATCH, CONV_W - 1, 3 * DN_WIDTH), dtp)
    zero_pool = jnp.zeros((BATCH, POOL_BUF, POOL_WIDTH), dtp)
    dp, cp, pp, ds, cs, ps = [], [], [], [], [], []
    for l in range(DEPTH):
        lw = (norm_mix[l], w_in[l], conv_w[l], a_log[l], dt_bias[l], dn_norm[l], w_pool[l],
              pool_scale[l], w_out[l], norm_ffn[l], w_gate_up[l], w_down[l])
        hp, S_p, c_p, p_p = decoder_layer(hp, zero_conv, zero_S, zero_pool, 0, True, *lw)
        hs, S_s, c_s, p_s = decoder_layer(hs, state_conv[l], state_delta[l], state_pool[l],
                                          PAST_LEN, False, *lw)
        dp.append(S_p); cp.append(c_p); pp.append(p_p)
        ds.append(S_s); cs.append(c_s); ps.append(p_s)
    y_prompt = rmsnorm(hp, norm_final)
    y_sample = rmsnorm(hs, norm_final)
    return (y_prompt, y_sample, jnp.stack(dp), jnp.stack(cp), jnp.stack(pp),
            jnp.stack(ds), jnp.stack(cs), jnp.stack(ps))
```

```python
import contextlib
import numpy as np
import concourse.bass as bass
import concourse.mybir as mybir
from concourse.bass_utils import run_bass_kernel_spmd

F32 = mybir.dt.float32
BF16 = mybir.dt.bfloat16
ALU = mybir.AluOpType
AF = mybir.ActivationFunctionType
AX = mybir.AxisListType

P = 128
D = 2048
KC = 16
H = 8
HB = 4
CH = 128
DFF = 5632
FC = 44
EPS = 1e-6
NV = 137
O_NM, O_NF, O_CW, O_PS, O_DN = 0, 16, 32, 128, 136
POOL_WINDOWS = (2, 4, 8, 16)
ENGS = ("pe", "act", "dve", "pool", "sp")


class Buf:
    __slots__ = ("name", "lw", "rd", "tok")

    def __init__(self, name="", tok=()):
        self.name = name
        self.lw = None
        self.rd = {}
        self.tok = tuple(tok)


class Sched:
    def __init__(self, nc, stack):
        self.nc = nc
        self.stack = stack
        self.ops = {e: [] for e in ENGS}
        self.cnt = {e: 0 for e in ENGS}
        self.clock = {e: {} for e in ENGS}
        self.iclock = {}
        self.sems = {}
        self.dcnt = {}
        for e in ("pe", "act", "dve", "pool"):
            self.sems[e] = stack.enter_context(nc.semaphore("s_" + e))

    def dsem(self, key):
        if key not in self.sems:
            self.sems[key] = self.stack.enter_context(self.nc.semaphore("d_" + key))
            self.dcnt[key] = 0
        return key

    def _deps(self, reads, writes):
        deps = {}
        for b in reads:
            if b.lw is not None and deps.get(b.lw[0], 0) < b.lw[1]:
                deps[b.lw[0]] = b.lw[1]
        for b in writes:
            if b.lw is not None and deps.get(b.lw[0], 0) < b.lw[1]:
                deps[b.lw[0]] = b.lw[1]
            for k, i in b.rd.items():
                if deps.get(k, 0) < i:
                    deps[k] = i
        return deps

    def _emit_waits(self, eng, deps, skip_self=False):
        clk = self.clock[eng]
        for k, i in deps.items():
            if k == eng and skip_self:
                continue
            if clk.get(k, 0) >= i:
                continue
            self.ops[eng].append(("wait", self.sems[k], i))
            oc = self.iclock.get((k, i))
            if oc:
                for kk, vv in oc.items():
                    if clk.get(kk, 0) < vv:
                        clk[kk] = vv
            clk[k] = max(clk.get(k, 0), i)

    def _record(self, key, idx, eng, reads, writes):
        c = dict(self.clock[eng])
        c[key] = idx
        self.iclock[(key, idx)] = c
        for b in reads:
            if b.rd.get(key, 0) < idx:
                b.rd[key] = idx
        for b in writes:
            b.lw = (key, idx)
            b.rd = {}

    @staticmethod
    def _with_tok(reads, writes):
        extra = [t for b in list(reads) + list(writes) for t in b.tok]
        return (list(reads) + extra) if extra else reads

    def op(self, eng, fn, reads=(), writes=(), pe_acc=False):
        reads = self._with_tok(reads, writes)
        deps = self._deps(reads, writes)
        self._emit_waits(eng, deps, skip_self=pe_acc)
        self.cnt[eng] += 1
        idx = self.cnt[eng]
        self.ops[eng].append(("op", fn, self.sems[eng], 1))
        self._record(eng, idx, eng, reads, writes)

    def dma(self, q, dsem, fn, reads=(), writes=()):
        self.dsem(dsem)
        reads = self._with_tok(reads, writes)
        deps = self._deps(reads, writes)
        self._emit_waits(q, deps)
        self.dcnt[dsem] += 16
        idx = self.dcnt[dsem]
        self.ops[q].append(("op", fn, self.sems[dsem], 16))
        self._record(dsem, idx, q, reads, writes)

    def replay(self, eng, e):
        for item in self.ops[eng]:
            if item[0] == "wait":
                e.wait_ge(item[1], item[2])
            else:
                item[1](e).then_inc(item[2], item[3])


def build_program(SEQ, DEPTH, NS, with_prompt=True, with_sample=True):
    TT = 512
    NT = SEQ // TT
    NJ = TT // CH
    nc = bass.Bass("TRN2", target_bir_lowering=False)

    def din(name, shape, dt=F32):
        return nc.dram_tensor(name, list(shape), dt, kind="ExternalInput").ap()

    def dout(name, shape, dt=F32):
        return nc.dram_tensor(name, list(shape), dt, kind="ExternalOutput").ap()

    xpT = din("xpT", [P, KC, SEQ])
    xsT = din("xsT", [P, KC, NS])
    sdS = din("sdS", [DEPTH, NS, H, P, P])
    scS = din("scS", [DEPTH, P, 24, NS, 3])
    spS = din("spS", [DEPTH, P, 8, NS, 15])
    WIN = din("WIN", [DEPTH, 20, P, KC, 256])
    WBA = din("WBA", [DEPTH, P, KC, 16])
    WOUT = din("WOUT", [DEPTH, 4, 2, P, 4, 1024])
    WG = din("WG", [DEPTH, 22, P, KC, 256])
    WU = din("WU", [DEPTH, 22, P, KC, 256])
    WD = din("WD", [DEPTH, 8, 4, P, 11, 256])
    WP = din("WP", [DEPTH, P, 4, 2, 256])
    VEC = din("VEC", [P, DEPTH, NV])
    VECF = din("VECF", [P, KC])
    AB = din("AB", [8, DEPTH, 2])
    CONST = din("CONST", [P, 1088])

    ypT = dout("ypT", [P, KC, SEQ])
    ysT = dout("ysT", [P, KC, NS])
    ndp = dout("ndp", [DEPTH, H, P, P])
    ncpT = dout("ncpT", [DEPTH, P, 24, 3])
    nppT = dout("nppT", [DEPTH, P, 8, 15])
    nds = dout("nds", [DEPTH, NS, H, P, P])
    ncsT = dout("ncsT", [DEPTH, P, 24, NS, 3])
    npsT = dout("npsT", [DEPTH, P, 8, NS, 15])

    with contextlib.ExitStack() as st:
        S = Sched(nc, st)
        _n = [0]

        def sb(shape, dt=F32, name=None):
            _n[0] += 1
            return st.enter_context(nc.sbuf_tensor(name or f"t{_n[0]}", list(shape), dt))

        PS = st.enter_context(nc.psum_tensor("PS", [P, 7 * 512], F32))
        PSB = st.enter_context(nc.psum_tensor("PSB", [P, 1024], BF16))
        pbufs = [Buf(f"ps{i}") for i in range(7)]
        psb_buf = Buf("psb")
        _pb = [0]

        def pbank():
            i = _pb[0] % 7
            _pb[0] += 1
            return PS[:, i * 512:(i + 1) * 512], pbufs[i]

        def mm(out, lhsT, rhs, start=True, stop=True, reads=(), writes=()):
            S.op("pe", lambda e: e.matmul(out, lhsT, rhs, start=start, stop=stop), reads=reads, writes=writes,
                 pe_acc=not start)

        def tr(out, in_, ident, reads=(), writes=()):
            S.op("pe", lambda e: e.transpose(out, in_, ident), reads=reads, writes=writes)

        def act(out, in_, func, reads=(), writes=(), bias=0.0, scale=1.0):
            S.op("act", lambda e: e.activation(out=out, in_=in_, func=func, bias=bias, scale=scale),
                 reads=reads, writes=writes)

        def tt(eng, out, in0, in1, op, reads=(), writes=()):
            S.op(eng, lambda e: e.tensor_tensor(out=out, in0=in0, in1=in1, op=op), reads=reads, writes=writes)

        def ts(eng, out, in0, s1, op0, reads=(), writes=()):
            S.op(eng, lambda e: e.tensor_scalar(out=out, in0=in0, scalar1=s1, scalar2=None, op0=op0),
                 reads=reads, writes=writes)

        def stt(eng, out, in0, scalar, in1, op0, op1, reads=(), writes=()):
            S.op(eng, lambda e: e.scalar_tensor_tensor(out=out, in0=in0, scalar=scalar, in1=in1, op0=op0, op1=op1),
                 reads=reads, writes=writes)

        def cp(eng, out, in_, reads=(), writes=()):
            if eng == "act":
                S.op("act", lambda e: e.copy(out, in_), reads=reads, writes=writes)
            else:
                S.op(eng, lambda e: e.tensor_copy(out=out, in_=in_), reads=reads, writes=writes)

        def recip(out, in_, reads=(), writes=()):
            S.op("dve", lambda e: e.reciprocal(out=out, in_=in_), reads=reads, writes=writes)

        def memset(eng, ap, val, writes=()):
            S.op(eng, lambda e: e.memset(ap, val), writes=writes)

        def dma(q, key, out, in_, reads=(), writes=()):
            S.dma(q, key, lambda e: e.dma_start(out=out, in_=in_), reads=reads, writes=writes)

        const = sb([P, 1088], F32, "const")
        b_const = Buf("const")
        dma("sp", "c0", const[:], CONST[:, :], writes=[b_const])
        ident = const[:, 0:128]
        maskT = const[:, 128:256]
        strict01 = const[:, 256:384]
        bd16 = const[:, 384:512]
        Lm = [const[:, 512 + i * 128:640 + i * 128] for i in range(3)]
        invc = const[:, 896:960]
        ones_f = const[:, 960:1088]
        vec = sb([P, DEPTH, NV], F32, "vec")
        vecf = sb([P, KC], F32, "vecf")
        ab = sb([8, DEPTH, 2], F32, "ab")
        nexpA = sb([8, DEPTH], F32, "nexpA")
        scratch = sb([P, 8], F32, "scratch")
        dma("sp", "c1", vec[:], VEC[:, :, :], writes=[b_const])
        dma("sp", "c2", vecf[:], VECF[:, :], writes=[b_const])
        dma("sp", "c3", ab[:], AB[:, :, :], writes=[b_const])
        ident_b = sb([P, P], BF16, "ident_b")
        onesD_b = sb([P, P], BF16, "onesD_b")
        onesV_b = sb([P, P], BF16, "onesV_b")
        ones_b = sb([P, P], BF16, "ones_b")
        cp("dve", ident_b[:], ident, reads=[b_const], writes=[b_const])
        ts("dve", onesD_b[:], ones_f, 1.0 / D, ALU.mult, reads=[b_const], writes=[b_const])
        ts("dve", onesV_b[:], ones_f, 1.0 / P, ALU.mult, reads=[b_const], writes=[b_const])
        cp("dve", ones_b[:], ones_f, reads=[b_const], writes=[b_const])
        act(nexpA[:], ab[:, :, 0], AF.Exp, reads=[b_const], writes=[b_const])
        ts("dve", nexpA[:], nexpA[:], -1.0, ALU.mult, reads=[b_const], writes=[b_const])
        CB = [b_const]

        tokU = Buf("tokU")
        tokA = Buf("tokA")

        b_scr = Buf("scratch")

        def switch(tok):
            memset("dve", scratch[:, 0:1], 0.0, writes=[tok, b_scr])

        NSLOT = 3
        SLW = 4096
        wsl = [sb([P, SLW], BF16, f"wsl{i}") for i in range(NSLOT)]
        wbuf = [Buf(f"w{i}") for i in range(NSLOT)]
        _ws = [0]

        def wload(src, a, b):
            i = _ws[0] % NSLOT
            _ws[0] += 1
            view = wsl[i][:, 0:a * b].rearrange("p (a b) -> p a b", a=a)
            dma("pool", f"w{i}", view, src, writes=[wbuf[i]])
            return view, wbuf[i]

        wsm = sb([P, KC * 16 + 4 * 2 * 256], BF16, "wsm")
        b_wsm = Buf("wsm")
        wba = wsm[:, 0:KC * 16].rearrange("p (a b) -> p a b", a=KC)
        wp = wsm[:, KC * 16:].rearrange("p (g k n) -> p g k n", g=4, k=2)

        def load_small_w(l):
            dma("pool", "wsm", wba, WBA[l], writes=[b_wsm])
            dma("pool", "wsm", wp, WP[l], writes=[b_wsm])

        UW = 20736
        U = sb([P, UW], BF16, "U")

        def uview(off, n_bf16, dt, pattern=None, **kw):
            v = U[:, off:off + n_bf16]
            if dt == F32:
                v = v.bitcast(F32)
            if pattern:
                v = v.rearrange(pattern, **kw)
            return v

        b_sq2 = [Buf("sq0"), Buf("sq1")]
        b_rstd_t = Buf("rstd")

        def UB(name, tokA_=False):
            return Buf(name, tok=(tokU, tokA) if tokA_ else (tokU,))

        def make_act(N, hT, xn, sqs, rstd, actv, gtmps, b_act, b_gt):
            return dict(N=N, hT=hT, b_h=Buf("hT"), xn=xn, b_xn=Buf("xn"), sq=sqs, b_sq=b_sq2,
                        rstd=rstd, b_rstd=b_rstd_t, act=actv, b_act=b_act, gtmp=gtmps, b_gtmp=b_gt)

        def sumsq_rstd(A):
            N = A["N"]
            bank, bb = pbank()
            for kc in range(KC):
                sq, bsq = A["sq"][kc % 2], A["b_sq"][kc % 2]
                act(sq[:, 0:N], A["hT"][:, kc, :], AF.Square, reads=[A["b_h"]], writes=[bsq])
                mm(bank[:, 0:N], onesD_b[:], sq[:, 0:N], start=(kc == 0), stop=(kc == KC - 1), reads=[bsq] + CB, writes=[bb])
            act(A["rstd"][:, 0:N], bank[:, 0:N], AF.Ln, bias=EPS, reads=[bb], writes=[A["b_rstd"]])
            act(A["rstd"][:, 0:N], A["rstd"][:, 0:N], AF.Exp, scale=-0.5, reads=[A["b_rstd"]], writes=[A["b_rstd"]])

        def rmsnorm_to_xn(A, wcol):
            N = A["N"]
            sumsq_rstd(A)
            for kc in range(KC):
                stt("dve", A["xn"][:, kc, :], A["hT"][:, kc, :], wcol[:, kc:kc + 1], A["rstd"][:, 0:N], ALU.mult, ALU.mult,
                    reads=[A["b_h"], A["b_rstd"]] + CB, writes=[A["b_xn"]])

        def proj_chunks(A, wview, wb, nch, kcs, rhs_fn, rhs_bufs, handler):
            N = A["N"]
            for m in range(nch):
                bank, bb = pbank()
                for kc in range(kcs):
                    mm(bank[:, 0:N], wview[:, kc, m * 128:(m + 1) * 128], rhs_fn(kc), start=(kc == 0), stop=(kc == kcs - 1),
                       reads=[wb] + rhs_bufs, writes=[bb])
                handler(m, bank[:, 0:N], bb)

        def ffn(A, l):
            N = A["N"]
            rmsnorm_to_xn(A, vec[:, l, O_NF:O_NF + KC])
            xn, bxn = A["xn"], A["b_xn"]
            for g in range(2):
                for bi in range(11):
                    blk = g * 11 + bi
                    gt, bgt = A["gtmp"][bi % 2], A["b_gtmp"][bi % 2]
                    wv, wb = wload(WG[l, blk], KC, 256)

                    def h_gate(m, ps, bb, gt=gt, bgt=bgt):
                        act(gt[:, m, :], ps, AF.Silu, reads=[bb], writes=[bgt])
                    proj_chunks(A, wv, wb, 2, KC, lambda kc: xn[:, kc, :], [bxn], h_gate)
                    wv, wb = wload(WU[l, blk], KC, 256)

                    def h_up(m, ps, bb, gt=gt, bgt=bgt, bi=bi):
                        tt("dve", A["act"][:, bi * 2 + m, :], gt[:, m, :], ps, ALU.mult, reads=[bb, bgt], writes=[A["b_act"]])
                    proj_chunks(A, wv, wb, 2, KC, lambda kc: xn[:, kc, :], [bxn], h_up)
                for cb in range(8):
                    banks = [pbank() for _ in range(2)]
                    for kb in range(2):
                        wv, wb = wload(WD[l, cb, g * 2 + kb], 11, 256)
                        for m in range(2):
                            for kc in range(11):
                                mm(banks[m][0][:, 0:N], wv[:, kc, m * 128:(m + 1) * 128], A["act"][:, kb * 11 + kc, :],
                                   start=(kb == 0 and kc == 0), stop=(kb == 1 and kc == 10),
                                   reads=[wb, A["b_act"]], writes=[banks[m][1]])
                    for m in range(2):
                        oc = cb * 2 + m
                        tt("dve", A["hT"][:, oc, :], A["hT"][:, oc, :], banks[m][0][:, 0:N], ALU.add,
                           reads=[banks[m][1], A["b_h"]], writes=[A["b_h"]])

        def out_proj_part(A, l, q, mixh, bmix):
            for nb in range(2):
                wv, wb = wload(WOUT[l, q, nb], 4, 1024)

                def h_out(m, ps, bb, nb=nb):
                    oc = nb * 8 + m
                    tt("dve", A["hT"][:, oc, :], A["hT"][:, oc, :], ps, ALU.add, reads=[bb, A["b_h"]], writes=[A["b_h"]])
                proj_chunks(A, wv, wb, 8, 4, lambda kc: mixh[:, kc, :], [bmix], h_out)

        def final_norm_out(A, dst, key):
            N = A["N"]
            sumsq_rstd(A)
            for kc in range(KC):
                stt("dve", A["hT"][:, kc, :], A["hT"][:, kc, :], vecf[:, kc:kc + 1], A["rstd"][:, 0:N], ALU.mult, ALU.mult,
                    reads=[A["b_h"], A["b_rstd"]] + CB, writes=[A["b_h"]])
            dma("sp", key, dst, A["hT"][:], reads=[A["b_h"]])

        def gates_fm(A, l, braw, graw, b_g):
            N = A["N"]
            xn, bxn = A["xn"], A["b_xn"]
            bank, bb = pbank()
            for kc in range(KC):
                mm(bank[0:8, 0:N], wba[:, kc, 0:8], xn[:, kc, :], start=(kc == 0), stop=(kc == KC - 1), reads=[b_wsm, bxn], writes=[bb])
            act(braw, bank[0:8, 0:N], AF.Sigmoid, reads=[bb], writes=[b_g])
            bank, bb = pbank()
            for kc in range(KC):
                mm(bank[0:8, 0:N], wba[:, kc, 8:16], xn[:, kc, :], start=(kc == 0), stop=(kc == KC - 1), reads=[b_wsm, bxn], writes=[bb])
            act(graw, bank[0:8, 0:N], AF.Exp, bias=ab[:, l, 1:2], reads=[bb] + CB, writes=[b_g])
            act(graw, graw, AF.Ln, bias=1.0, reads=[b_g], writes=[b_g])
            ts("dve", graw, graw, nexpA[:, l:l + 1], ALU.mult, reads=[b_g] + CB, writes=[b_g])

        sq2 = [sb([P, TT], BF16) for _ in range(2)]
        rstd_t = sb([P, TT], F32)
        mixh = sb([P, 4, TT], BF16); b_mixh = Buf("mixh")
        g8a = sb([8, TT], F32); g8b = sb([8, TT], F32); g8c = sb([8, TT], F32); b_g8 = Buf("g8")

        if with_prompt:
            N = TT
            hT = sb([P, KC, N], F32, "hT")
            xn_t = sb([P, KC, N], BF16, "xn")
            pT = uview(0, 8432, F32, "p (c t) -> p c t", c=8); b_p = UB("p", True)
            OB = 8448
            qT = uview(OB, 2048, BF16, "p (h t) -> p h t", h=HB); b_q = UB("q")
            kT = uview(OB + 2048, 2048, BF16, "p (h t) -> p h t", h=HB); b_k = UB("k")
            vT = uview(OB + 4096, 2048, BF16, "p (h t) -> p h t", h=HB); b_v = UB("v")
            zT = uview(OB + 6144, 2048, BF16, "p (h t) -> p h t", h=HB); b_z = UB("z")
            OC = 16640
            itoff = [i * 1024 for i in range(8)] + [OC + i * 1024 for i in range(4)]

            def itmp(i, name):
                return uview(itoff[i], 512, BF16, "p (h c) -> p h c", h=HB), UB(name, True)
            Nn, b_N = itmp(0, "Nn"); N0, b_N0 = itmp(1, "N0"); M0, b_M0 = itmp(2, "M0")
            Ml = [itmp(3 + i, f"Ml{i}") for i in range(3)]
            Ra, b_Ra = itmp(6, "Ra"); Rb_, b_Rb = itmp(7, "Rb")
            N2, b_N2 = itmp(8, "N2"); M2, b_M2 = itmp(9, "M2"); Ta, b_Ta = itmp(10, "Ta"); Tb, b_Tb = itmp(11, "Tb")
            N4, b_N4, M4, b_M4 = N0, b_N0, M0, b_M0
            M8, b_M8 = N2, b_N2
            Yt, b_Y = Nn, b_N
            actv = uview(0, 11264, BF16, "p (c t) -> p c t", c=22); b_act = UB("act")
            gtmps = [uview(11264 + i * 2048, 2048, F32, "p (c t) -> p c t", c=2) for i in range(2)]
            b_gt = [UB("g0"), UB("g1")]
            A = make_act(N, hT, xn_t, sq2, rstd_t, actv, gtmps, b_act, b_gt)

            pre = [sb([P, 3 + N], F32) for _ in range(2)]; b_pre = [Buf("pre0"), Buf("pre1")]
            ycv = [sb([P, 15 + N], F32) for _ in range(2)]; b_ycv = [Buf("y0"), Buf("y1")]
            for _i in range(2):
                memset("dve", ycv[_i][:], 0.0, writes=[b_ycv[_i]])
            rn = [sb([P, N], F32) for _ in range(2)]; b_rn = [Buf("rn0"), Buf("rn1")]
            chalo = sb([P, DEPTH, 24, 3], F32); b_chalo = Buf("chalo")
            phalo = sb([P, DEPTH, 8, 15], F32); b_phalo = Buf("phalo")
            memset("dve", chalo[:], 0.0, writes=[b_chalo])
            memset("dve", phalo[:], 0.0, writes=[b_phalo])
            Sst = [sb([P, HB, P], F32) for _ in range(2)]; b_S = [Buf("S0"), Buf("S1")]
            Sb = sb([P, HB, P], BF16); b_Sb = Buf("Sb")
            Xg = sb([8, HB, CH], F32); b_X = Buf("X")
            gc_tok = sb([P, NJ, H], F32); be_tok = sb([P, NJ, H], F32); b_tok = Buf("tok")
            bg_tok = sb([P, NJ, H], F32); kd_tok = sb([P, NJ, H], F32); el_tok = sb([P, NJ, H], F32)
            glb = sb([8, P], F32); b_glb = Buf("glb")
            kbT = sb([P, HB, CH], BF16); qgT = sb([P, HB, CH], BF16); b_kq = Buf("kbqg")
            dm = sb([P, HB, CH], F32); decTs = sb([P, HB, CH], F32); b_dec = Buf("dec")
            kbg = sb([P, HB, CH], BF16); kdec = sb([P, HB, CH], BF16); vb = sb([P, HB, CH], BF16); b_tokm = Buf("tokm")
            qkTb = sb([P, HB, CH], BF16); b_qk = Buf("qk")
            oTc = sb([P, HB, CH], F32); b_oTc = Buf("oTc")
            sqo = sb([P, HB, CH], BF16); rno = sb([P, HB, CH], F32); b_no = Buf("no")
            dg = [sb([P, 2, N], BF16) for _ in range(2)]; b_dg = [Buf("dg0"), Buf("dg1")]
            Rbf = sb([P, HB, CH], BF16); b_Rbf = Buf("Rbf")
            wTn = sb([P, HB, CH], BF16); b_wTn = Buf("wTn")
            vnew = sb([P, HB, CH], BF16); b_vnew = Buf("vnew")

            def bc_h(ap2d, nh):
                return ap2d.unsqueeze(1).to_broadcast([P, nh, CH])
            b_ndp = [[Buf(f"ndp{l}_{hb}") for hb in range(2)] for l in range(DEPTH)]

            v3 = lambda x: x.rearrange("p (h c) -> p h c", h=HB)
            f2 = lambda x: x[:].rearrange("p j h -> p (j h)")

            for t in range(NT):
                tok0 = t * TT
                dma("sp", "xin", hT[:], xpT[:, :, tok0:tok0 + TT], writes=[A["b_h"]])
                for l in range(DEPTH):
                    load_small_w(l)
                    rmsnorm_to_xn(A, vec[:, l, O_NM:O_NM + KC])
                    xn, bxn = A["xn"], A["b_xn"]
                    switch(tokU)
                    switch(tokA)
                    braw, gcA, gcB = g8a, g8b, g8c
                    gates_fm(A, l, braw[:], gcA[:], b_g8)
                    src, dst = gcA, gcB
                    sh = 1
                    while sh < CH:
                        s3 = src[:].rearrange("p (j c) -> p j c", c=CH)
                        d3 = dst[:].rearrange("p (j c) -> p j c", c=CH)
                        cp("dve", d3[:, :, 0:sh], s3[:, :, 0:sh], reads=[b_g8], writes=[b_g8])
                        tt("dve", d3[:, :, sh:], s3[:, :, sh:], s3[:, :, 0:CH - sh], ALU.add, reads=[b_g8], writes=[b_g8])
                        src, dst = dst, src
                        sh *= 2
                    gcT = src
                    bank, bb = pbank()
                    for j in range(NJ):
                        tr(bank[:, j * 8:(j + 1) * 8], gcT[:, j * CH:(j + 1) * CH], ident[0:8, 0:8], reads=[b_g8] + CB, writes=[bb])
                        tr(bank[:, 64 + j * 8:64 + (j + 1) * 8], braw[:, j * CH:(j + 1) * CH], ident[0:8, 0:8], reads=[b_g8] + CB, writes=[bb])
                    cp("dve", f2(gc_tok), bank[:, 0:NJ * 8], reads=[bb], writes=[b_tok])
                    cp("dve", f2(be_tok), bank[:, 64:64 + NJ * 8], reads=[bb], writes=[b_tok])
                    bank, bb = pbank()
                    for j in range(NJ):
                        col = j * CH + CH - 1
                        ts("dve", glb[:], ones_f[0:8, :], gcT[:, col:col + 1], ALU.mult, reads=[b_g8] + CB, writes=[b_glb])
                        mm(bank[:, j * 8:(j + 1) * 8], glb[:], ident[0:8, 0:8], reads=[b_glb] + CB, writes=[bb])
                    tt("dve", f2(kd_tok), bank[:, 0:NJ * 8], f2(gc_tok), ALU.subtract, reads=[bb, b_tok], writes=[b_tok])
                    act(f2(kd_tok), f2(kd_tok), AF.Exp, reads=[b_tok], writes=[b_tok])
                    act(f2(el_tok), bank[:, 0:NJ * 8], AF.Exp, reads=[bb], writes=[b_tok])
                    act(f2(bg_tok), f2(gc_tok), AF.Exp, reads=[b_tok], writes=[b_tok])
                    tt("dve", f2(bg_tok), f2(bg_tok), f2(be_tok), ALU.mult, reads=[b_tok], writes=[b_tok])

                    for pb in range(4):
                        wv, wb = wload(WIN[l, 16 + pb], KC, 256)

                        def h_p(m, ps, bb, pb=pb):
                            ci = pb * 2 + m
                            cp("dve", pT[:, ci, 0:15], phalo[:, l, ci, :], reads=[b_phalo], writes=[b_p])
                            cp("act", pT[:, ci, 15:15 + N], ps, reads=[bb], writes=[b_p])
                            cp("dve", phalo[:, l, ci, :], pT[:, ci, N:N + 15], reads=[b_p], writes=[b_phalo])
                        proj_chunks(A, wv, wb, 2, KC, lambda kc: xn[:, kc, :], [bxn], h_p)
                    if t == NT - 1:
                        dma("sp", "npp", nppT[l], phalo[:, l, :, :], reads=[b_phalo])
                    for half in range(2):
                        for gg in range(2):
                            g = half * 2 + gg
                            win = POOL_WINDOWS[g]
                            dgt, bdg = dg[g % 2], b_dg[g % 2]
                            L = 15 + N
                            for c2 in range(2):
                                ci = g * 2 + c2
                                cur, curb = pT[:, ci, :], b_p
                                sh = 1
                                k2 = 0
                                while sh < win:
                                    nxt, nb_ = ycv[k2 % 2], b_ycv[k2 % 2]
                                    tt("dve", nxt[:, sh:L], cur[:, sh:L], cur[:, 0:L - sh], ALU.add, reads=[curb], writes=[nb_])
                                    cur, curb = nxt[:, :], nb_
                                    k2 += 1
                                    sh *= 2
                                stt("dve", dgt[:, c2, :], cur[:, 15:15 + N], 1.0 / win, pT[:, ci, 15:15 + N], ALU.mult, ALU.subtract,
                                    reads=[curb, b_p], writes=[bdg])
                                if t == 0:
                                    o_, ob_ = ycv[k2 % 2], b_ycv[k2 % 2]
                                    tt("dve", o_[:, 0:15], cur[:, 15:30], invc[:, g * 16:g * 16 + 15], ALU.mult, reads=[curb] + CB, writes=[ob_])
                                    tt("dve", dgt[:, c2, 0:15], o_[:, 0:15], pT[:, ci, 15:30], ALU.subtract, reads=[ob_, b_p], writes=[bdg])
                            for m in range(2):
                                bank, bb = pbank()
                                for kc in range(2):
                                    mm(bank[:, 0:N], wp[:, g, kc, m * 128:(m + 1) * 128], dgt[:, kc, :], start=(kc == 0), stop=(kc == 1),
                                       reads=[b_wsm, bdg], writes=[bb])
                                ci = g * 2 + m
                                ts("dve", mixh[:, gg * 2 + m, :], bank[:, 0:N], vec[:, l, O_PS + ci:O_PS + ci + 1], ALU.mult,
                                   reads=[bb] + CB, writes=[b_mixh])
                        out_proj_part(A, l, 2 + half, mixh, b_mixh)
                    switch(tokA)

                    for hb in range(2):
                        cwv = vec[:, l, O_CW:O_CW + 96].rearrange("p (c i) -> p c i", i=4)
                        _ci = [0]
                        for typ in range(4):
                            for sub in range(2):
                                wv, wb = wload(WIN[l, hb * 8 + typ * 2 + sub], KC, 256)

                                def h_in(m, ps, bb, typ=typ, sub=sub):
                                    h4 = sub * 2 + m
                                    if typ == 3:
                                        act(zT[:, h4, :], ps, AF.Silu, reads=[bb], writes=[b_z])
                                        return
                                    ci = typ * 8 + hb * 4 + h4
                                    i2 = _ci[0] % 2
                                    _ci[0] += 1
                                    pr, bpr = pre[i2], b_pre[i2]
                                    cp("dve", pr[:, 0:3], chalo[:, l, ci, :], reads=[b_chalo], writes=[bpr])
                                    cp("act", pr[:, 3:3 + N], ps, reads=[bb], writes=[bpr])
                                    cp("dve", chalo[:, l, ci, :], pr[:, N:N + 3], reads=[bpr], writes=[b_chalo])
                                    y, by = ycv[i2], b_ycv[i2]
                                    ts("dve", y[:, 0:N], pr[:, 0:N], cwv[:, ci, 0:1], ALU.mult, reads=[bpr] + CB, writes=[by])
                                    for i in range(1, 4):
                                        stt("dve", y[:, 0:N], pr[:, i:i + N], cwv[:, ci, i:i + 1], y[:, 0:N], ALU.mult, ALU.add,
                                            reads=[bpr, by] + CB, writes=[by])
                                    dstT, bdst = ((qT, b_q), (kT, b_k), (vT, b_v))[typ]
                                    act(dstT[:, h4, :], y[:, 0:N], AF.Silu, reads=[by], writes=[bdst])
                                proj_chunks(A, wv, wb, 2, KC, lambda kc: xn[:, kc, :], [bxn], h_in)
                        for typ in (1, 0):
                            dstT, bdst = (qT, b_q) if typ == 0 else (kT, b_k)
                            nbanks = []
                            for h4 in range(HB):
                                sqb, bsqb = A["sq"][h4 % 2], A["b_sq"][h4 % 2]
                                act(sqb[:, 0:N], dstT[:, h4, :], AF.Square, reads=[bdst], writes=[bsqb])
                                bk2, bb2 = pbank()
                                mm(bk2[:, 0:N], ones_b[:], sqb[:, 0:N], reads=[bsqb] + CB, writes=[bb2])
                                nbanks.append((bk2, bb2))

                            def scale(h4, typ=typ, dstT=dstT, bdst=bdst):
                                i2 = h4 % 2
                                if typ == 0:
                                    stt("dve", dstT[:, h4, :], dstT[:, h4, :], float(P) ** -0.5, rn[i2][:], ALU.mult, ALU.mult,
                                        reads=[bdst, b_rn[i2]], writes=[bdst])
                                else:
                                    tt("dve", dstT[:, h4, :], dstT[:, h4, :], rn[i2][:], ALU.mult, reads=[bdst, b_rn[i2]], writes=[bdst])
                            for h4 in range(HB):
                                i2 = h4 % 2
                                bk2, bb2 = nbanks[h4]
                                act(rn[i2][:], bk2[:, 0:N], AF.Ln, bias=EPS, reads=[bb2], writes=[b_rn[i2]])
                                act(rn[i2][:], rn[i2][:], AF.Exp, scale=-0.5, reads=[b_rn[i2]], writes=[b_rn[i2]])
                                if h4 >= 1:
                                    scale(h4 - 1)
                            scale(HB - 1)
                        if t == NT - 1 and hb == 1:
                            dma("sp", "ncp", ncpT[l], chalo[:, l, :, :], reads=[b_chalo])
                        Sh, bSh = Sst[hb], b_S[hb]
                        if t == 0:
                            memset("dve", Sh[:], 0.0, writes=[bSh])
                            memset("dve", Sb[:], 0.0, writes=[b_Sb])
                        else:
                            dma("sp", f"sld{hb}", Sh[:], ndp[l, hb * HB:(hb + 1) * HB].rearrange("h d v -> d h v"), reads=[b_ndp[l][hb]], writes=[bSh])
                            cp("act", Sb[:], Sh[:], reads=[bSh], writes=[b_Sb])
                        hs = slice(hb * HB, (hb + 1) * HB)
                        for j in range(NJ):
                            cs = slice(j * CH, (j + 1) * CH)
                            i8b = ident[0:8, hb * HB:(hb + 1) * HB].unsqueeze(2).to_broadcast([8, HB, CH])
                            tt("dve", Xg[:], gcT[:, cs].unsqueeze(1).to_broadcast([8, HB, CH]), i8b, ALU.mult, reads=[b_g8] + CB, writes=[b_X])
                            GBk, GBb = pbank()
                            mm(GBk, ones_f[0:8, :], Xg[:].rearrange("p h c -> p (h c)"), reads=[b_X] + CB, writes=[GBb])
                            g3 = v3(GBk)
                            act(dm[:], g3, AF.Exp, reads=[GBb], writes=[b_dec])
                            tt("dve", qgT[:], qT[:, :, cs], dm[:], ALU.mult, reads=[b_q, b_dec], writes=[b_kq])
                            for h4 in range(HB):
                                h = hb * HB + h4
                                stt("dve", dm[:, h4, :], g3[:, h4, :], gc_tok[:, j, h:h + 1], maskT, ALU.subtract, ALU.min,
                                    reads=[GBb, b_tok] + CB, writes=[b_dec])
                            act(dm[:], dm[:], AF.Exp, reads=[b_dec], writes=[b_dec])
                            tt("dve", decTs[:], dm[:], bc_h(strict01, HB), ALU.mult, reads=[b_dec] + CB, writes=[b_dec])
                            tt("dve", Xg[:], braw[:, cs].unsqueeze(1).to_broadcast([8, HB, CH]), i8b, ALU.mult, reads=[b_g8] + CB, writes=[b_X])
                            BBk, BBb = pbank()
                            mm(BBk, ones_f[0:8, :], Xg[:].rearrange("p h c -> p (h c)"), reads=[b_X] + CB, writes=[BBb])
                            tt("dve", kbT[:], kT[:, :, cs], v3(BBk), ALU.mult, reads=[b_k, BBb], writes=[b_kq])
                            k3 = PSB[:, 0:512].rearrange("p (h c) -> p h c", h=HB)
                            v3b = PSB[:, 512:1024].rearrange("p (h c) -> p h c", h=HB)
                            for h4 in range(HB):
                                tr(k3[:, h4, :], kT[:, h4, cs], ident_b[:], reads=[b_k] + CB, writes=[psb_buf])
                                tr(v3b[:, h4, :], vT[:, h4, cs], ident_b[:], reads=[b_v] + CB, writes=[psb_buf])
                            tt("dve", kbg[:], k3, bg_tok[:, j, hs].unsqueeze(2).to_broadcast([P, HB, CH]), ALU.mult, reads=[psb_buf, b_tok], writes=[b_tokm])
                            tt("dve", kdec[:], k3, kd_tok[:, j, hs].unsqueeze(2).to_broadcast([P, HB, CH]), ALU.mult, reads=[psb_buf, b_tok], writes=[b_tokm])
                            tt("dve", vb[:], v3b, be_tok[:, j, hs].unsqueeze(2).to_broadcast([P, HB, CH]), ALU.mult, reads=[psb_buf, b_tok], writes=[b_tokm])
                            bank, bb = pbank()
                            for h4 in range(HB):
                                mm(bank[:, h4 * CH:(h4 + 1) * CH], kT[:, h4, cs], qT[:, h4, cs], reads=[b_k, b_q], writes=[bb])
                            tt("dve", qkTb[:], v3(bank), dm[:], ALU.mult, reads=[bb, b_dec], writes=[b_qk])
                            bank, bb = pbank()
                            for h4 in range(HB):
                                mm(bank[:, h4 * CH:(h4 + 1) * CH], kT[:, h4, cs], kbT[:, h4, :], reads=[b_k, b_kq], writes=[bb])
                            tt("dve", Nn[:], v3(bank), decTs[:], ALU.mult, reads=[bb, b_dec], writes=[b_N])
                            tt("dve", N0[:], Nn[:], bc_h(bd16, HB), ALU.mult, reads=[b_N] + CB, writes=[b_N0])
                            for h4 in range(HB):
                                tr(k3[:, h4, :], Nn[:, h4, :], ident_b[:], reads=[b_N] + CB, writes=[psb_buf])
                            tt("dve", M0[:], k3, bc_h(bd16, HB), ALU.mult, reads=[psb_buf] + CB, writes=[b_M0])
                            for i in range(3):
                                tt("dve", Ml[i][0][:], k3, bc_h(Lm[i], HB), ALU.mult, reads=[psb_buf] + CB, writes=[Ml[i][1]])
                            tt("dve", Ra[:], bc_h(ident, HB), N0[:], ALU.subtract, reads=[b_N0] + CB, writes=[b_Ra])

                            def mm4(lh, blh, rh, brh):
                                bank, bb = pbank()
                                for h4 in range(HB):
                                    mm(bank[:, h4 * CH:(h4 + 1) * CH], lh[:, h4, :], rh[:, h4, :], reads=[blh, brh], writes=[bb])
                                return v3(bank), bb
                            Rc, bRc, Rn, bRn = Ra, b_Ra, Rb_, b_Rb
                            p1, bb1 = mm4(M0, b_M0, N0, b_N0)
                            p2, bb2 = mm4(N0, b_N0, M0, b_M0)
                            cp("act", N2[:], p1, reads=[bb1], writes=[b_N2])
                            cp("act", M2[:], p2, reads=[bb2], writes=[b_M2])
                            p3, bb3 = mm4(M2, b_M2, Rc, bRc)
                            tt("dve", Rn[:], Rc[:], p3, ALU.add, reads=[bb3, bRc], writes=[bRn])
                            Rc, bRc, Rn, bRn = Rn, bRn, Rc, bRc
                            p1, bb1 = mm4(M2, b_M2, N2, b_N2)
                            p2, bb2 = mm4(N2, b_N2, M2, b_M2)
                            cp("act", N4[:], p1, reads=[bb1], writes=[b_N4])
                            cp("act", M4[:], p2, reads=[bb2], writes=[b_M4])
                            p3, bb3 = mm4(M4, b_M4, Rc, bRc)
                            tt("dve", Rn[:], Rc[:], p3, ALU.add, reads=[bb3, bRc], writes=[bRn])
                            Rc, bRc, Rn, bRn = Rn, bRn, Rc, bRc
                            p2, bb2 = mm4(N4, b_N4, M4, b_M4)
                            cp("act", M8[:], p2, reads=[bb2], writes=[b_M8])
                            p3, bb3 = mm4(M8, b_M8, Rc, bRc)
                            tt("dve", Rn[:], Rc[:], p3, ALU.add, reads=[bb3, bRc], writes=[bRn])
                            Rc, bRc, Rn, bRn = Rn, bRn, Rc, bRc
                            for h4 in range(HB):
                                tr(k3[:, h4, :], Rc[:, h4, :], ident_b[:], reads=[bRc] + CB, writes=[psb_buf])
                            Tc, bTc, Tn, bTn = Ta, b_Ta, Tb, b_Tb
                            cp("act", Tc[:], k3, reads=[psb_buf], writes=[bTc])
                            for i in range(3):
                                py, bby = mm4(Ml[i][0], Ml[i][1], Rc, bRc)
                                cp("act", Yt[:], py, reads=[bby], writes=[b_Y])
                                pz, bbz = mm4(Tc, bTc, Yt, b_Y)
                                if i < 2:
                                    pzt, bbzt = mm4(Yt, b_Y, Tc, bTc)
                                    tt("dve", Rn[:], Rc[:], pz, ALU.subtract, reads=[bbz, bRc], writes=[bRn])
                                    tt("dve", Tn[:], Tc[:], pzt, ALU.subtract, reads=[bbzt, bTc], writes=[bTn])
                                    Rc, bRc, Rn, bRn = Rn, bRn, Rc, bRc
                                    Tc, bTc, Tn, bTn = Tn, bTn, Tc, bTc
                                else:
                                    tt("dve", Rbf[:], Rc[:], pz, ALU.subtract, reads=[bbz, bRc], writes=[b_Rbf])
                            bank, bb = pbank()
                            for h4 in range(HB):
                                mm(bank[:, h4 * CH:(h4 + 1) * CH], kbg[:, h4, :], Rbf[:, h4, :], reads=[b_tokm, b_Rbf], writes=[bb])
                            act(wTn[:], v3(bank), AF.Copy, scale=-1.0, reads=[bb], writes=[b_wTn])
                            bank, bb = pbank()
                            for h4 in range(HB):
                                mm(bank[:, h4 * CH:(h4 + 1) * CH], Rbf[:, h4, :], vb[:, h4, :], start=True, stop=False, reads=[b_Rbf, b_tokm], writes=[bb])
                                mm(bank[:, h4 * CH:(h4 + 1) * CH], wTn[:, h4, :], Sb[:, h4, :], start=False, stop=True, reads=[b_wTn, b_Sb], writes=[bb])
                            cp("act", vnew[:], v3(bank), reads=[bb], writes=[b_vnew])
                            bank, bb = pbank()
                            for h4 in range(HB):
                                mm(bank[:, h4 * CH:(h4 + 1) * CH], Sb[:, h4, :], qgT[:, h4, :], start=True, stop=False, reads=[b_Sb, b_kq], writes=[bb])
                                mm(bank[:, h4 * CH:(h4 + 1) * CH], vnew[:, h4, :], qkTb[:, h4, :], start=False, stop=True, reads=[b_vnew, b_qk], writes=[bb])
                            cp("act", oTc[:], v3(bank), reads=[bb], writes=[b_oTc])
                            bank, bb = pbank()
                            for h4 in range(HB):
                                mm(bank[:, h4 * CH:(h4 + 1) * CH], kdec[:, h4, :], vnew[:, h4, :], reads=[b_tokm, b_vnew], writes=[bb])
                            tt("dve", Sh[:], Sh[:], el_tok[:, j, hs].unsqueeze(2).to_broadcast([P, HB, CH]), ALU.mult, reads=[bSh, b_tok], writes=[bSh])
                            tt("dve", Sh[:], Sh[:], v3(bank), ALU.add, reads=[bb, bSh], writes=[bSh])
                            cp("act", Sb[:], Sh[:], reads=[bSh], writes=[b_Sb])
                            act(sqo[:], oTc[:], AF.Square, reads=[b_oTc], writes=[b_no])
                            bank, bb = pbank()
                            mm(bank, onesV_b[:], sqo[:].rearrange("p h c -> p (h c)"), reads=[b_no] + CB, writes=[bb])
                            act(rno[:], v3(bank), AF.Ln, bias=EPS, reads=[bb], writes=[b_no])
                            act(rno[:], rno[:], AF.Exp, scale=-0.5, reads=[b_no], writes=[b_no])
                            stt("dve", rno[:], oTc[:], vec[:, l, O_DN:O_DN + 1], rno[:], ALU.mult, ALU.mult, reads=[b_oTc, b_no] + CB, writes=[b_no])
                            tt("dve", mixh[:, :, cs], rno[:], zT[:, :, cs], ALU.mult, reads=[b_no, b_z], writes=[b_mixh])
                        dma("sp", f"sst{hb}", ndp[l, hb * HB:(hb + 1) * HB].rearrange("h d v -> d h v"), Sh[:], reads=[bSh], writes=[b_ndp[l][hb]])
                        out_proj_part(A, l, hb, mixh, b_mixh)
                    switch(tokU)
                    ffn(A, l)
                final_norm_out(A, ypT[:, :, tok0:tok0 + TT], "yout")

        if with_sample:
            N = NS
            NP = NS * H
            switch(tokU)
            hTs = sb([P, KC, N], F32, "hTs"); xns = sb([P, KC, N], BF16, "xns")
            acts = sb([P, 22, N], BF16, "acts"); gts = [sb([P, 2, N], F32) for _ in range(2)]
            B = make_act(N, hTs, xns, sq2, rstd_t, acts, gts, Buf("acts"), [Buf("gs0"), Buf("gs1")])
            Sin = [uview(i * 2048, 2048, F32, "p (h v) -> p h v", h=H) for i in range(2)]; b_Sin = [UB("sin0"), UB("sin1")]
            Sout = [uview(4096 + i * 2048, 2048, F32, "p (h v) -> p h v", h=H) for i in range(2)]; b_Sout = [UB("so0"), UB("so1")]
            s_pre = uview(8192, 3072, F32, "p (c b i) -> p c b i", c=24, i=4); b_spre = UB("spre")
            s_p = uview(11264, 4096, F32, "p (c b i) -> p c b i", c=8, i=16); b_sp = UB("spool")
            s_y = uview(15360, 768, F32, "p (c b) -> p c b", c=24); s_y2 = uview(16128, 768, F32, "p (c b) -> p c b", c=24); b_sy = UB("sy")
            o = [16896]

            def u128(n=128):
                v = uview(o[0], 2 * n, F32)
                o[0] += 2 * n
                return v
            kqf = u128(256); kq = kqf.rearrange("p (j t) -> p j t", t=2); b_kqs = UB("kq")
            kvqsf = u128(256); kvqs = kvqsf.rearrange("p (j t) -> p j t", t=2); b_kvqs = UB("kvqs")
            vTs = u128(); b_vs = UB("vs")
            zTs = u128(); b_zs = UB("zs")
            BBs = u128(); EGs = u128(); QKs = u128(); b_bc = UB("bcs")
            dvT = u128(); oTs = u128(); b_dv = UB("dv")
            s_rno = u128(); b_sno = UB("sno")
            assert o[0] <= UW
            Ktok = sb([P, P], F32); DVtok = sb([P, P], F32); b_ktok = Buf("ktok")
            Km = [sb([P, P], F32) for _ in range(2)]; b_Km = [Buf("km0"), Buf("km1")]
            s_sq = sb([P, 16 * NS], BF16); s_rn = sb([P, 16 * NS], F32); b_srn = Buf("srn")
            s_sum = sb([P, 8, NS], F32); s_d = sb([P, 8, NS], BF16); b_sd = Buf("sd")
            s_sqo = sb([P, NP], BF16)
            Xs = sb([8, NS, H], F32); b_Xs = Buf("Xs")
            mixs = sb([P, 4, N], BF16); b_mixs = Buf("mixs")

            dma("sp", "xin_s", hTs[:], xsT[:, :, :], writes=[B["b_h"]])
            for l in range(DEPTH):
                load_small_w(l)
                rmsnorm_to_xn(B, vec[:, l, O_NM:O_NM + KC])
                xn, bxn = B["xn"], B["b_xn"]
                dma("sp", "scl", s_pre[:, :, :, 0:3], scS[l], writes=[b_spre])
                dma("sp", "spl", s_p[:, :, :, 0:15], spS[l], writes=[b_sp])
                s_b8, s_g8, s_eg8 = g8a[:, 0:N], g8b[:, 0:N], g8c[:, 0:N]
                gates_fm(B, l, s_b8, s_g8, b_g8)
                act(s_eg8, s_g8, AF.Exp, reads=[b_g8], writes=[b_g8])
                i8 = ident[0:8, 0:8].unsqueeze(1).to_broadcast([8, NS, H])
                for srcv, dstv in ((s_b8, BBs), (s_eg8, EGs)):
                    tt("dve", Xs[:], srcv.unsqueeze(2).to_broadcast([8, NS, H]), i8, ALU.mult, reads=[b_g8] + CB, writes=[b_Xs])
                    bank, bb = pbank()
                    mm(bank[:, 0:NP], ones_f[0:8, :], Xs[:].rearrange("p b h -> p (b h)"), reads=[b_Xs] + CB, writes=[bb])
                    cp("act", dstv, bank[:, 0:NP], reads=[bb], writes=[b_bc])
                for pb in range(4):
                    wv, wb = wload(WIN[l, 16 + pb], KC, 256)

                    def h_ps(m, ps, bb, pb=pb):
                        cp("act", s_p[:, pb * 2 + m, :, 15], ps, reads=[bb], writes=[b_sp])
                    proj_chunks(B, wv, wb, 2, KC, lambda kc: xn[:, kc, :], [bxn], h_ps)
                dma("sp", "spo", npsT[l], s_p[:, :, :, 1:16], reads=[b_sp])
                for g in range(4):
                    win = POOL_WINDOWS[g]
                    S.op("dve", (lambda e, g=g, win=win: e.tensor_reduce(out=s_sum[:, 2 * g:2 * g + 2, :], in_=s_p[:, 2 * g:2 * g + 2, :, 16 - win:16],
                                                                      axis=AX.X, op=ALU.add)), reads=[b_sp], writes=[b_sd])
                    stt("dve", s_d[:, 2 * g:2 * g + 2, :], s_sum[:, 2 * g:2 * g + 2, :], 1.0 / win, s_p[:, 2 * g:2 * g + 2, :, 15], ALU.mult, ALU.subtract,
                        reads=[b_sd, b_sp], writes=[b_sd])
                for half in range(2):
                    for gg in range(2):
                        g = half * 2 + gg
                        for m in range(2):
                            bank, bb = pbank()
                            for kc in range(2):
                                mm(bank[:, 0:N], wp[:, g, kc, m * 128:(m + 1) * 128], s_d[:, g * 2 + kc, :], start=(kc == 0), stop=(kc == 1),
                                   reads=[b_wsm, b_sd], writes=[bb])
                            ci = g * 2 + m
                            ts("dve", mixs[:, gg * 2 + m, :], bank[:, 0:N], vec[:, l, O_PS + ci:O_PS + ci + 1], ALU.mult, reads=[bb] + CB, writes=[b_mixs])
                    out_proj_part(B, l, 2 + half, mixs, b_mixs)
                zT3 = zTs.rearrange("p (b h) -> p b h", h=H)
                for hb in range(2):
                    for typ in range(4):
                        for sub in range(2):
                            wv, wb = wload(WIN[l, hb * 8 + typ * 2 + sub], KC, 256)

                            def h_in_s(m, ps, bb, typ=typ, sub=sub, hb=hb):
                                hh = hb * 4 + sub * 2 + m
                                if typ == 3:
                                    act(zT3[:, :, hh], ps, AF.Silu, reads=[bb], writes=[b_zs])
                                else:
                                    cp("act", s_pre[:, typ * 8 + hh, :, 3], ps, reads=[bb], writes=[b_spre])
                            proj_chunks(B, wv, wb, 2, KC, lambda kc: xn[:, kc, :], [bxn], h_in_s)
                dma("sp", "sco", ncsT[l], s_pre[:, :, :, 1:4], reads=[b_spre])
                cwv = vec[:, l, O_CW:O_CW + 96].rearrange("p (c i) -> p c i", i=4)
                cwb = lambda i: cwv[:, :, i:i + 1].to_broadcast([P, 24, NS])
                tt("dve", s_y, s_pre[:, :, :, 0], cwb(0), ALU.mult, reads=[b_spre] + CB, writes=[b_sy])
                for i in range(1, 4):
                    tt("dve", s_y2, s_pre[:, :, :, i], cwb(i), ALU.mult, reads=[b_spre] + CB, writes=[b_sy])
                    tt("dve", s_y, s_y, s_y2, ALU.add, reads=[b_sy], writes=[b_sy])
                act(s_y, s_y, AF.Silu, reads=[b_sy], writes=[b_sy])
                act(s_sq[:].rearrange("p (c b) -> p c b", b=NS), s_y[:, 0:16, :], AF.Square, reads=[b_sy], writes=[b_srn])
                bank, bb = pbank()
                mm(bank[:, 0:16 * NS], ones_b[:], s_sq[:], reads=[b_srn] + CB, writes=[bb])
                act(s_rn[:], bank[:, 0:16 * NS], AF.Ln, bias=EPS, reads=[bb], writes=[b_srn])
                act(s_rn[:], s_rn[:], AF.Exp, scale=-0.5, reads=[b_srn], writes=[b_srn])
                rn3 = s_rn[:].rearrange("p (c b) -> p c b", b=NS)
                kq4 = kq.rearrange("p (b h) t -> p b h t", h=H)
                vT3 = vTs.rearrange("p (b h) -> p b h", h=H)
                for hh in range(H):
                    stt("dve", kq4[:, :, hh, 1], s_y[:, hh, :], float(P) ** -0.5, rn3[:, hh, :], ALU.mult, ALU.mult, reads=[b_sy, b_srn], writes=[b_kqs])
                    tt("dve", kq4[:, :, hh, 0], s_y[:, 8 + hh, :], rn3[:, 8 + hh, :], ALU.mult, reads=[b_sy, b_srn], writes=[b_kqs])
                    cp("dve", vT3[:, :, hh], s_y[:, 16 + hh, :], reads=[b_sy], writes=[b_vs])
                tt("dve", dvT, kq[:, :, 0], kq[:, :, 1], ALU.mult, reads=[b_kqs], writes=[b_dv])
                bank, bb = pbank()
                mm(bank[:, 0:NP], ones_f, dvT, reads=[b_dv] + CB, writes=[bb])
                cp("act", QKs, bank[:, 0:NP], reads=[bb], writes=[b_bc])
                kvbank, kvbb = pbank()
                for b in range(NS):
                    si = b % 2
                    dma(("sp", "act")[si], f"sin{si}", Sin[si], sdS[l, b].rearrange("h d v -> d h v"), writes=[b_Sin[si]])
                    for hh in range(H):
                        jj = b * H + hh
                        mm(kvbank[:, jj * 2:jj * 2 + 2], Sin[si][:, hh, :], kq[:, jj, :], reads=[b_Sin[si], b_kqs], writes=[kvbb])
                cp("act", kvqsf, kvbank[:, 0:NP * 2], reads=[kvbb], writes=[b_kvqs])
                tt("dve", dvT, EGs, kvqs[:, :, 0], ALU.mult, reads=[b_bc, b_kvqs], writes=[b_dv])
                tt("dve", dvT, vTs, dvT, ALU.subtract, reads=[b_vs, b_dv], writes=[b_dv])
                tt("dve", dvT, dvT, BBs, ALU.mult, reads=[b_bc, b_dv], writes=[b_dv])
                tt("dve", oTs, EGs, kvqs[:, :, 1], ALU.mult, reads=[b_bc, b_kvqs], writes=[b_dv])
                tt("dve", s_rno, QKs, dvT, ALU.mult, reads=[b_bc, b_dv], writes=[b_sno])
                tt("dve", oTs, oTs, s_rno, ALU.add, reads=[b_sno, b_dv], writes=[b_dv])
                bank, bb = pbank()
                tr(bank[:, 0:P], kq[:, :, 0], ident, reads=[b_kqs] + CB, writes=[bb])
                tr(bank[:, P:2 * P], dvT, ident, reads=[b_dv] + CB, writes=[bb])
                cp("act", Ktok[:], bank[:, 0:P], reads=[bb], writes=[b_ktok])
                cp("act", DVtok[:], bank[:, P:2 * P], reads=[bb], writes=[b_ktok])
                for b in range(NS):
                    si = b % 2
                    dma("act", f"sin{si}", Sin[si], sdS[l, b].rearrange("h d v -> d h v"), writes=[b_Sin[si]])
                    for hh in range(H):
                        jj = b * H + hh
                        ki = jj % 2
                        ts("dve", Km[ki][:], Ktok[:], ident[:, jj:jj + 1], ALU.mult, reads=[b_ktok] + CB, writes=[b_Km[ki]])
                        bank, bb = pbank()
                        mm(bank[:, 0:P], Km[ki][:], DVtok[:], reads=[b_Km[ki], b_ktok], writes=[bb])
                        stt("dve", Sout[si][:, hh, :], Sin[si][:, hh, :], EGs[:, jj:jj + 1], bank[:, 0:P], ALU.mult, ALU.add,
                            reads=[b_Sin[si], b_bc, bb], writes=[b_Sout[si]])
                    dma("sp", f"sout{si}", nds[l, b].rearrange("h d v -> d h v"), Sout[si], reads=[b_Sout[si]])
                act(s_sqo[:], oTs, AF.Square, reads=[b_dv], writes=[b_sno])
                bank, bb = pbank()
                mm(bank[:, 0:NP], onesV_b[:], s_sqo[:], reads=[b_sno] + CB, writes=[bb])
                act(s_rno, bank[:, 0:NP], AF.Ln, bias=EPS, reads=[bb], writes=[b_sno])
                act(s_rno, s_rno, AF.Exp, scale=-0.5, reads=[b_sno], writes=[b_sno])
                stt("dve", s_rno, oTs, vec[:, l, O_DN:O_DN + 1], s_rno, ALU.mult, ALU.mult, reads=[b_dv, b_sno] + CB, writes=[b_sno])
                tt("dve", s_rno, s_rno, zTs, ALU.mult, reads=[b_sno, b_zs], writes=[b_sno])
                r3 = s_rno.rearrange("p (b h) -> p b h", h=H)
                for hb in range(2):
                    for h4 in range(HB):
                        cp("dve", mixs[:, h4, :], r3[:, :, hb * HB + h4], reads=[b_sno], writes=[b_mixs])
                    out_proj_part(B, l, hb, mixs, b_mixs)
                ffn(B, l)
            final_norm_out(B, ysT[:, :, :], "yout_s")

        for key, cntv in S.dcnt.items():
            if cntv > 0:
                S.ops["sp"].append(("wait", S.sems[key], cntv))
        build_program.stats = {e: len(S.ops[e]) for e in ENGS}

        with nc.Block() as block:
            @block.tensor
            def _(e):
                S.replay("pe", e)

            @block.scalar
            def _(e):
                S.replay("act", e)

            @block.vector
            def _(e):
                S.replay("dve", e)

            @block.gpsimd
            def _(e):
                S.replay("pool", e)

            @block.sync
            def _(e):
                S.replay("sp", e)
    return nc


def make_consts():
    c = np.zeros((P, 1088), np.float32)
    i = np.arange(P)
    s, cc = i[:, None], i[None, :]
    c[:, 0:128] = np.eye(P)
    c[:, 128:256] = np.where(cc >= s, 0.0, -1e4)
    c[:, 256:384] = (cc > s)
    c[:, 384:512] = (s // 16 == cc // 16)
    cp_, sf = i[:, None], i[None, :]
    c[:, 512:640] = (cp_ // 32 == sf // 32) & (cp_ % 32 >= 16) & (sf % 32 < 16)
    c[:, 640:768] = (cp_ // 64 == sf // 64) & (cp_ % 64 >= 32) & (sf % 64 < 32)
    c[:, 768:896] = (cp_ >= 64) & (sf < 64)
    for g, w in enumerate(POOL_WINDOWS):
        tpos = np.arange(16)
        c[:, 896 + g * 16:896 + (g + 1) * 16] = 1.0 / np.minimum(tpos + 1, w)
    c[:, 960:1088] = 1.0
    return c


def prep_weights(w_in, w_out, w_gate_up, w_down, w_pool, norm_mix, norm_ffn, conv_w, pool_scale, dn_norm,
                 a_log, dt_bias, norm_final):
    DEPTH = w_in.shape[0]
    f = np.float32

    def tile_cols(w, col0, ncols, kcs=KC):
        return np.ascontiguousarray(w[:, :, col0:col0 + ncols].reshape(DEPTH, kcs, P, ncols).transpose(0, 2, 1, 3))

    cols = []
    for hb in range(2):
        for typ in range(4):
            for sub in range(2):
                cols.append(typ * 1024 + hb * 512 + sub * 256)
    cols += [4112 + i * 256 for i in range(4)]
    WIN = np.stack([tile_cols(w_in, c0, 256) for c0 in cols], axis=1)
    WBA = tile_cols(w_in, 4096, 16)
    wo = w_out.reshape(DEPTH, 4, 4, P, 2, 1024)
    WOUT = np.ascontiguousarray(wo.transpose(0, 1, 4, 3, 2, 5))
    WG = np.stack([tile_cols(w_gate_up, b * 256, 256) for b in range(22)], axis=1)
    WU = np.stack([tile_cols(w_gate_up, DFF + b * 256, 256) for b in range(22)], axis=1)
    wd = w_down.reshape(DEPTH, 4, 11, P, 8, 256)
    WD = np.ascontiguousarray(wd.transpose(0, 4, 1, 3, 2, 5))
    WP = np.ascontiguousarray(w_pool.reshape(DEPTH, 4, 2, P, 256).transpose(0, 3, 1, 2, 4))
    VEC = np.zeros((P, DEPTH, NV), f)
    VEC[:, :, O_NM:O_NM + 16] = norm_mix.reshape(DEPTH, KC, P).transpose(2, 0, 1)
    VEC[:, :, O_NF:O_NF + 16] = norm_ffn.reshape(DEPTH, KC, P).transpose(2, 0, 1)
    VEC[:, :, O_CW:O_CW + 96] = conv_w.reshape(DEPTH, 4, 24, P).transpose(3, 0, 2, 1).reshape(P, DEPTH, 96)
    VEC[:, :, O_PS:O_PS + 8] = pool_scale.reshape(DEPTH, 8, P).transpose(2, 0, 1)
    VEC[:, :, O_DN] = dn_norm.T
    VECF = np.ascontiguousarray(norm_final.reshape(KC, P).T)
    AB = np.ascontiguousarray(np.stack([a_log, dt_bias], axis=-1).transpose(1, 0, 2))
    return dict(WIN=WIN, WBA=WBA, WOUT=WOUT, WG=WG, WU=WU, WD=WD, WP=WP, VEC=VEC, VECF=VECF, AB=AB,
                CONST=make_consts())


_NC_CACHE = {}


def run(x_prompt, x_sample, state_delta, state_conv, state_pool, norm_mix, w_in, conv_w, a_log, dt_bias, dn_norm,
        w_pool, pool_scale, w_out, norm_ffn, w_gate_up, w_down, norm_final, n_cores=8):
    asf = lambda a: np.asarray(a, np.float32)
    x_prompt, x_sample, state_delta, state_conv, state_pool = map(asf, (x_prompt, x_sample, state_delta, state_conv, state_pool))
    BATCH, SEQ, _ = x_prompt.shape
    DEPTH = w_in.shape[0]
    DECB = x_sample.shape[0]
    NS = DECB // n_cores
    wts = prep_weights(*map(asf, (w_in, w_out, w_gate_up, w_down, w_pool, norm_mix, norm_ffn, conv_w, pool_scale, dn_norm,
                                  a_log, dt_bias, norm_final)))
    key = (SEQ, DEPTH, NS)
    if key not in _NC_CACHE:
        _NC_CACHE[key] = build_program(SEQ, DEPTH, NS)
    nc = _NC_CACHE[key]
    in_maps = []
    for c in range(n_cores):
        b = c % BATCH
        sl = slice(c * NS, (c + 1) * NS)
        m = dict(wts)
        m["xpT"] = np.ascontiguousarray(x_prompt[b].reshape(SEQ, KC, P).transpose(2, 1, 0))
        m["xsT"] = np.ascontiguousarray(x_sample[sl, 0].reshape(NS, KC, P).transpose(2, 1, 0))
        m["sdS"] = np.ascontiguousarray(state_delta[:, sl])
        m["scS"] = np.ascontiguousarray(state_conv[:, sl].reshape(DEPTH, NS, 3, 24, P).transpose(0, 4, 3, 1, 2))
        m["spS"] = np.ascontiguousarray(state_pool[:, sl].reshape(DEPTH, NS, 15, 8, P).transpose(0, 4, 3, 1, 2))
        in_maps.append(m)
    res = run_bass_kernel_spmd(nc, in_maps, core_ids=list(range(n_cores)))
    R = res.results
    y_prompt = np.stack([R[b]["ypT"].transpose(2, 1, 0).reshape(SEQ, D) for b in range(BATCH)])
    y_sample = np.concatenate([R[c]["ysT"].transpose(2, 1, 0).reshape(NS, 1, D) for c in range(n_cores)])
    ndp = np.stack([R[b]["ndp"] for b in range(BATCH)], axis=1)
    ncp = np.stack([R[b]["ncpT"].transpose(0, 3, 2, 1).reshape(DEPTH, 3, 3072) for b in range(BATCH)], axis=1)
    npp = np.stack([R[b]["nppT"].transpose(0, 3, 2, 1).reshape(DEPTH, 15, 1024) for b in range(BATCH)], axis=1)
    nds = np.concatenate([R[c]["nds"] for c in range(n_cores)], axis=1)
    ncs = np.concatenate([R[c]["ncsT"].transpose(0, 3, 4, 2, 1).reshape(DEPTH, NS, 3, 3072) for c in range(n_cores)], axis=1)
    nps = np.concatenate([R[c]["npsT"].transpose(0, 3, 4, 2, 1).reshape(DEPTH, NS, 15, 1024) for c in range(n_cores)], axis=1)
    f = np.float32
    return tuple(np.ascontiguousarray(a, dtype=f) for a in (y_prompt, y_sample, ndp, ncp, npp, nds, ncs, nps))


def kernel(**inputs):
    return run(**inputs)
```

```python
import contextlib
import numpy as np
import concourse.bass as bass
import concourse.mybir as mybir
from concourse.bass_utils import run_bass_kernel_spmd

F32 = mybir.dt.float32
BF16 = mybir.dt.bfloat16
ALU = mybir.AluOpType
AF = mybir.ActivationFunctionType
AX = mybir.AxisListType

P = 128
D = 2048
KC = 16
H = 8
HB = 4
CH = 128
DFF = 5632
FC = 44
EPS = 1e-6
NV = 137
O_NM, O_NF, O_CW, O_PS, O_DN = 0, 16, 32, 128, 136
POOL_WINDOWS = (2, 4, 8, 16)
ENGS = ("pe", "act", "dve", "pool", "sp")


class Buf:
    __slots__ = ("name", "lw", "rd", "tok")

    def __init__(self, name="", tok=()):
        self.name = name
        self.lw = None
        self.rd = {}
        self.tok = tuple(tok)


class Sched:
    def __init__(self, nc, stack):
        self.nc = nc
        self.stack = stack
        self.ops = {e: [] for e in ENGS}
        self.cnt = {e: 0 for e in ENGS}
        self.clock = {e: {} for e in ENGS}
        self.iclock = {}
        self.sems = {}
        self.dcnt = {}
        for e in ("pe", "act", "dve", "pool"):
            self.sems[e] = stack.enter_context(nc.semaphore("s_" + e))

    def dsem(self, key):
        if key not in self.sems:
            self.sems[key] = self.stack.enter_context(self.nc.semaphore("d_" + key))
            self.dcnt[key] = 0
        return key

    def _deps(self, reads, writes):
        deps = {}
        for b in reads:
            if b.lw is not None and deps.get(b.lw[0], 0) < b.lw[1]:
                deps[b.lw[0]] = b.lw[1]
        for b in writes:
            if b.lw is not None and deps.get(b.lw[0], 0) < b.lw[1]:
                deps[b.lw[0]] = b.lw[1]
            for k, i in b.rd.items():
                if deps.get(k, 0) < i:
                    deps[k] = i
        return deps

    def _emit_waits(self, eng, deps, skip_self=False):
        clk = self.clock[eng]
        for k, i in deps.items():
            if k == eng and skip_self:
                continue
            if clk.get(k, 0) >= i:
                continue
            self.ops[eng].append(("wait", self.sems[k], i))
            oc = self.iclock.get((k, i))
            if oc:
                for kk, vv in oc.items():
                    if clk.get(kk, 0) < vv:
                        clk[kk] = vv
            clk[k] = max(clk.get(k, 0), i)

    def _record(self, key, idx, eng, reads, writes):
        c = dict(self.clock[eng])
        c[key] = idx
        self.iclock[(key, idx)] = c
        for b in reads:
            if b.rd.get(key, 0) < idx:
                b.rd[key] = idx
        for b in writes:
            b.lw = (key, idx)
            b.rd = {}

    @staticmethod
    def _with_tok(reads, writes):
        extra = [t for b in list(reads) + list(writes) for t in b.tok]
        return (list(reads) + extra) if extra else reads

    def op(self, eng, fn, reads=(), writes=(), pe_acc=False):
        reads = self._with_tok(reads, writes)
        deps = self._deps(reads, writes)
        self._emit_waits(eng, deps, skip_self=pe_acc)
        self.cnt[eng] += 1
        idx = self.cnt[eng]
        self.ops[eng].append(("op", fn, self.sems[eng], 1))
        self._record(eng, idx, eng, reads, writes)

    def dma(self, q, dsem, fn, reads=(), writes=()):
        self.dsem(dsem)
        reads = self._with_tok(reads, writes)
        deps = self._deps(reads, writes)
        self._emit_waits(q, deps)
        self.dcnt[dsem] += 16
        idx = self.dcnt[dsem]
        self.ops[q].append(("op", fn, self.sems[dsem], 16))
        self._record(dsem, idx, q, reads, writes)

    def replay(self, eng, e):
        for item in self.ops[eng]:
            if item[0] == "wait":
                e.wait_ge(item[1], item[2])
            else:
                item[1](e).then_inc(item[2], item[3])


def build_program(SEQ, DEPTH, NS, with_prompt=True, with_sample=True):
    TT = 512
    NT = SEQ // TT
    NJ = TT // CH
    nc = bass.Bass("TRN2", target_bir_lowering=False)

    def din(name, shape, dt=F32):
        return nc.dram_tensor(name, list(shape), dt, kind="ExternalInput").ap()

    def dout(name, shape, dt=F32):
        return nc.dram_tensor(name, list(shape), dt, kind="ExternalOutput").ap()

    xpT = din("xpT", [P, KC, SEQ])
    xsT = din("xsT", [P, KC, NS])
    sdS = din("sdS", [DEPTH, NS, H, P, P])
    scS = din("scS", [DEPTH, P, 24, NS, 3])
    spS = din("spS", [DEPTH, P, 8, NS, 15])
    WIN = din("WIN", [DEPTH, 20, P, KC, 256])
    WBA = din("WBA", [DEPTH, P, KC, 16])
    WOUT = din("WOUT", [DEPTH, 4, 2, P, 4, 1024])
    WG = din("WG", [DEPTH, 22, P, KC, 256])
    WU = din("WU", [DEPTH, 22, P, KC, 256])
    WD = din("WD", [DEPTH, 8, 4, P, 11, 256])
    WP = din("WP", [DEPTH, P, 4, 2, 256])
    VEC = din("VEC", [P, DEPTH, NV])
    VECF = din("VECF", [P, KC])
    AB = din("AB", [8, DEPTH, 2])
    CONST = din("CONST", [P, 1088])

    ypT = dout("ypT", [P, KC, SEQ])
    ysT = dout("ysT", [P, KC, NS])
    ndp = dout("ndp", [DEPTH, H, P, P])
    ncpT = dout("ncpT", [DEPTH, P, 24, 3])
    nppT = dout("nppT", [DEPTH, P, 8, 15])
    nds = dout("nds", [DEPTH, NS, H, P, P])
    ncsT = dout("ncsT", [DEPTH, P, 24, NS, 3])
    npsT = dout("npsT", [DEPTH, P, 8, NS, 15])

    with contextlib.ExitStack() as st:
        S = Sched(nc, st)
        _n = [0]

        def sb(shape, dt=F32, name=None):
            _n[0] += 1
            return st.enter_context(nc.sbuf_tensor(name or f"t{_n[0]}", list(shape), dt))

        PS = st.enter_context(nc.psum_tensor("PS", [P, 7 * 512], F32))
        PSB = st.enter_context(nc.psum_tensor("PSB", [P, 1024], BF16))
        pbufs = [Buf(f"ps{i}") for i in range(7)]
        psb_buf = Buf("psb")
        _pb = [0]

        def pbank():
            i = _pb[0] % 7
            _pb[0] += 1
            return PS[:, i * 512:(i + 1) * 512], pbufs[i]

        def mm(out, lhsT, rhs, start=True, stop=True, reads=(), writes=()):
            S.op("pe", lambda e: e.matmul(out, lhsT, rhs, start=start, stop=stop), reads=reads, writes=writes,
                 pe_acc=not start)

        def tr(out, in_, ident, reads=(), writes=()):
            S.op("pe", lambda e: e.transpose(out, in_, ident), reads=reads, writes=writes)

        def act(out, in_, func, reads=(), writes=(), bias=0.0, scale=1.0):
            S.op("act", lambda e: e.activation(out=out, in_=in_, func=func, bias=bias, scale=scale),
                 reads=reads, writes=writes)

        def tt(eng, out, in0, in1, op, reads=(), writes=()):
            S.op(eng, lambda e: e.tensor_tensor(out=out, in0=in0, in1=in1, op=op), reads=reads, writes=writes)

        def ts(eng, out, in0, s1, op0, reads=(), writes=()):
            S.op(eng, lambda e: e.tensor_scalar(out=out, in0=in0, scalar1=s1, scalar2=None, op0=op0),
                 reads=reads, writes=writes)

        def stt(eng, out, in0, scalar, in1, op0, op1, reads=(), writes=()):
            S.op(eng, lambda e: e.scalar_tensor_tensor(out=out, in0=in0, scalar=scalar, in1=in1, op0=op0, op1=op1),
                 reads=reads, writes=writes)

        def cp(eng, out, in_, reads=(), writes=()):
            if eng == "act":
                S.op("act", lambda e: e.copy(out, in_), reads=reads, writes=writes)
            else:
                S.op(eng, lambda e: e.tensor_copy(out=out, in_=in_), reads=reads, writes=writes)

        def recip(out, in_, reads=(), writes=()):
            S.op("dve", lambda e: e.reciprocal(out=out, in_=in_), reads=reads, writes=writes)

        def memset(eng, ap, val, writes=()):
            S.op(eng, lambda e: e.memset(ap, val), writes=writes)

        def dma(q, key, out, in_, reads=(), writes=()):
            S.dma(q, key, lambda e: e.dma_start(out=out, in_=in_), reads=reads, writes=writes)

        const = sb([P, 1088], F32, "const")
        b_const = Buf("const")
        dma("sp", "c0", const[:], CONST[:, :], writes=[b_const])
        ident = const[:, 0:128]
        maskT = const[:, 128:256]
        strict01 = const[:, 256:384]
        bd16 = const[:, 384:512]
        Lm = [const[:, 512 + i * 128:640 + i * 128] for i in range(3)]
        invc = const[:, 896:960]
        ones_f = const[:, 960:1088]
        vec = sb([P, DEPTH, NV], F32, "vec")
        vecf = sb([P, KC], F32, "vecf")
        ab = sb([8, DEPTH, 2], F32, "ab")
        nexpA = sb([8, DEPTH], F32, "nexpA")
        scratch = sb([P, 8], F32, "scratch")
        dma("sp", "c1", vec[:], VEC[:, :, :], writes=[b_const])
        dma("sp", "c2", vecf[:], VECF[:, :], writes=[b_const])
        dma("sp", "c3", ab[:], AB[:, :, :], writes=[b_const])
        ident_b = sb([P, P], BF16, "ident_b")
        onesD_b = sb([P, P], BF16, "onesD_b")
        onesV_b = sb([P, P], BF16, "onesV_b")
        ones_b = sb([P, P], BF16, "ones_b")
        cp("dve", ident_b[:], ident, reads=[b_const], writes=[b_const])
        ts("dve", onesD_b[:], ones_f, 1.0 / D, ALU.mult, reads=[b_const], writes=[b_const])
        ts("dve", onesV_b[:], ones_f, 1.0 / P, ALU.mult, reads=[b_const], writes=[b_const])
        cp("dve", ones_b[:], ones_f, reads=[b_const], writes=[b_const])
        act(nexpA[:], ab[:, :, 0], AF.Exp, reads=[b_const], writes=[b_const])
        ts("dve", nexpA[:], nexpA[:], -1.0, ALU.mult, reads=[b_const], writes=[b_const])
        CB = [b_const]

        tokU = Buf("tokU")
        tokA = Buf("tokA")

        b_scr = Buf("scratch")

        def switch(tok):
            memset("dve", scratch[:, 0:1], 0.0, writes=[tok, b_scr])

        NSLOT = 3
        SLW = 4096
        wsl = [sb([P, SLW], BF16, f"wsl{i}") for i in range(NSLOT)]
        wbuf = [Buf(f"w{i}") for i in range(NSLOT)]
        _ws = [0]

        def wload(src, a, b):
            i = _ws[0] % NSLOT
            _ws[0] += 1
            view = wsl[i][:, 0:a * b].rearrange("p (a b) -> p a b", a=a)
            dma("pool", f"w{i}", view, src, writes=[wbuf[i]])
            return view, wbuf[i]

        wsm = sb([P, KC * 16 + 4 * 2 * 256], BF16, "wsm")
        b_wsm = Buf("wsm")
        wba = wsm[:, 0:KC * 16].rearrange("p (a b) -> p a b", a=KC)
        wp = wsm[:, KC * 16:].rearrange("p (g k n) -> p g k n", g=4, k=2)

        def load_small_w(l):
            dma("pool", "wsm", wba, WBA[l], writes=[b_wsm])
            dma("pool", "wsm", wp, WP[l], writes=[b_wsm])

        UW = 20736
        U = sb([P, UW], BF16, "U")

        def uview(off, n_bf16, dt, pattern=None, **kw):
            v = U[:, off:off + n_bf16]
            if dt == F32:
                v = v.bitcast(F32)
            if pattern:
                v = v.rearrange(pattern, **kw)
            return v

        b_sq2 = [Buf("sq0"), Buf("sq1")]
        b_rstd_t = Buf("rstd")

        def UB(name, tokA_=False):
            return Buf(name, tok=(tokU, tokA) if tokA_ else (tokU,))

        def make_act(N, hT, xn, sqs, rstd, actv, gtmps, b_act, b_gt):
            return dict(N=N, hT=hT, b_h=Buf("hT"), xn=xn, b_xn=Buf("xn"), sq=sqs, b_sq=b_sq2,
                        rstd=rstd, b_rstd=b_rstd_t, act=actv, b_act=b_act, gtmp=gtmps, b_gtmp=b_gt)

        def sumsq_rstd(A):
            N = A["N"]
            bank, bb = pbank()
            for kc in range(KC):
                sq, bsq = A["sq"][kc % 2], A["b_sq"][kc % 2]
                act(sq[:, 0:N], A["hT"][:, kc, :], AF.Square, reads=[A["b_h"]], writes=[bsq])
                mm(bank[:, 0:N], onesD_b[:], sq[:, 0:N], start=(kc == 0), stop=(kc == KC - 1), reads=[bsq] + CB, writes=[bb])
            act(A["rstd"][:, 0:N], bank[:, 0:N], AF.Ln, bias=EPS, reads=[bb], writes=[A["b_rstd"]])
            act(A["rstd"][:, 0:N], A["rstd"][:, 0:N], AF.Exp, scale=-0.5, reads=[A["b_rstd"]], writes=[A["b_rstd"]])

        def rmsnorm_to_xn(A, wcol):
            N = A["N"]
            sumsq_rstd(A)
            for kc in range(KC):
                stt("dve", A["xn"][:, kc, :], A["hT"][:, kc, :], wcol[:, kc:kc + 1], A["rstd"][:, 0:N], ALU.mult, ALU.mult,
                    reads=[A["b_h"], A["b_rstd"]] + CB, writes=[A["b_xn"]])

        def proj_chunks(A, wview, wb, nch, kcs, rhs_fn, rhs_bufs, handler):
            N = A["N"]
            for m in range(nch):
                bank, bb = pbank()
                for kc in range(kcs):
                    mm(bank[:, 0:N], wview[:, kc, m * 128:(m + 1) * 128], rhs_fn(kc), start=(kc == 0), stop=(kc == kcs - 1),
                       reads=[wb] + rhs_bufs, writes=[bb])
                handler(m, bank[:, 0:N], bb)

        def ffn(A, l):
            N = A["N"]
            rmsnorm_to_xn(A, vec[:, l, O_NF:O_NF + KC])
            xn, bxn = A["xn"], A["b_xn"]
            for g in range(2):
                for bi in range(11):
                    blk = g * 11 + bi
                    gt, bgt = A["gtmp"][bi % 2], A["b_gtmp"][bi % 2]
                    wv, wb = wload(WG[l, blk], KC, 256)

                    def h_gate(m, ps, bb, gt=gt, bgt=bgt):
                        act(gt[:, m, :], ps, AF.Silu, reads=[bb], writes=[bgt])
                    proj_chunks(A, wv, wb, 2, KC, lambda kc: xn[:, kc, :], [bxn], h_gate)
                    wv, wb = wload(WU[l, blk], KC, 256)

                    def h_up(m, ps, bb, gt=gt, bgt=bgt, bi=bi):
                        tt("dve", A["act"][:, bi * 2 + m, :], gt[:, m, :], ps, ALU.mult, reads=[bb, bgt], writes=[A["b_act"]])
                    proj_chunks(A, wv, wb, 2, KC, lambda kc: xn[:, kc, :], [bxn], h_up)
                for cb in range(8):
                    banks = [pbank() for _ in range(2)]
                    for kb in range(2):
                        wv, wb = wload(WD[l, cb, g * 2 + kb], 11, 256)
                        for m in range(2):
                            for kc in range(11):
                                mm(banks[m][0][:, 0:N], wv[:, kc, m * 128:(m + 1) * 128], A["act"][:, kb * 11 + kc, :],
                                   start=(kb == 0 and kc == 0), stop=(kb == 1 and kc == 10),
                                   reads=[wb, A["b_act"]], writes=[banks[m][1]])
                    for m in range(2):
                        oc = cb * 2 + m
                        tt("dve", A["hT"][:, oc, :], A["hT"][:, oc, :], banks[m][0][:, 0:N], ALU.add,
                           reads=[banks[m][1], A["b_h"]], writes=[A["b_h"]])

        def out_proj_part(A, l, q, mixh, bmix):
            for nb in range(2):
                wv, wb = wload(WOUT[l, q, nb], 4, 1024)

                def h_out(m, ps, bb, nb=nb):
                    oc = nb * 8 + m
                    tt("dve", A["hT"][:, oc, :], A["hT"][:, oc, :], ps, ALU.add, reads=[bb, A["b_h"]], writes=[A["b_h"]])
                proj_chunks(A, wv, wb, 8, 4, lambda kc: mixh[:, kc, :], [bmix], h_out)

        def final_norm_out(A, dst, key):
            N = A["N"]
            sumsq_rstd(A)
            for kc in range(KC):
                stt("dve", A["hT"][:, kc, :], A["hT"][:, kc, :], vecf[:, kc:kc + 1], A["rstd"][:, 0:N], ALU.mult, ALU.mult,
                    reads=[A["b_h"], A["b_rstd"]] + CB, writes=[A["b_h"]])
            dma("sp", key, dst, A["hT"][:], reads=[A["b_h"]])

        def gates_fm(A, l, braw, graw, b_g):
            N = A["N"]
            xn, bxn = A["xn"], A["b_xn"]
            bank, bb = pbank()
            for kc in range(KC):
                mm(bank[0:8, 0:N], wba[:, kc, 0:8], xn[:, kc, :], start=(kc == 0), stop=(kc == KC - 1), reads=[b_wsm, bxn], writes=[bb])
            act(braw, bank[0:8, 0:N], AF.Sigmoid, reads=[bb], writes=[b_g])
            bank, bb = pbank()
            for kc in range(KC):
                mm(bank[0:8, 0:N], wba[:, kc, 8:16], xn[:, kc, :], start=(kc == 0), stop=(kc == KC - 1), reads=[b_wsm, bxn], writes=[bb])
            act(graw, bank[0:8, 0:N], AF.Exp, bias=ab[:, l, 1:2], reads=[bb] + CB, writes=[b_g])
            act(graw, graw, AF.Ln, bias=1.0, reads=[b_g], writes=[b_g])
            ts("dve", graw, graw, nexpA[:, l:l + 1], ALU.mult, reads=[b_g] + CB, writes=[b_g])

        sq2 = [sb([P, TT], BF16) for _ in range(2)]
        rstd_t = sb([P, TT], F32)
        mixh = sb([P, 4, TT], BF16); b_mixh = Buf("mixh")
        g8a = sb([8, TT], F32); g8b = sb([8, TT], F32); g8c = sb([8, TT], F32); b_g8 = Buf("g8")

        if with_prompt:
            N = TT
            hT = sb([P, KC, N], F32, "hT")
            xn_t = sb([P, KC, N], BF16, "xn")
            pT = uview(0, 8432, F32, "p (c t) -> p c t", c=8); b_p = UB("p", True)
            OB = 8448
            qT = uview(OB, 2048, BF16, "p (h t) -> p h t", h=HB); b_q = UB("q")
            kT = uview(OB + 2048, 2048, BF16, "p (h t) -> p h t", h=HB); b_k = UB("k")
            vT = uview(OB + 4096, 2048, BF16, "p (h t) -> p h t", h=HB); b_v = UB("v")
            zT = uview(OB + 6144, 2048, BF16, "p (h t) -> p h t", h=HB); b_z = UB("z")
            OC = 16640
            itoff = [i * 1024 for i in range(8)] + [OC + i * 1024 for i in range(4)]

            def itmp(i, name):
                return uview(itoff[i], 512, BF16, "p (h c) -> p h c", h=HB), UB(name, True)
            Nn, b_N = itmp(0, "Nn"); N0, b_N0 = itmp(1, "N0"); M0, b_M0 = itmp(2, "M0")
            Ml = [itmp(3 + i, f"Ml{i}") for i in range(3)]
            Ra, b_Ra = itmp(6, "Ra"); Rb_, b_Rb = itmp(7, "Rb")
            N2, b_N2 = itmp(8, "N2"); M2, b_M2 = itmp(9, "M2"); Ta, b_Ta = itmp(10, "Ta"); Tb, b_Tb = itmp(11, "Tb")
            N4, b_N4, M4, b_M4 = N0, b_N0, M0, b_M0
            M8, b_M8 = N2, b_N2
            Yt, b_Y = Nn, b_N
            actv = uview(0, 11264, BF16, "p (c t) -> p c t", c=22); b_act = UB("act")
            gtmps = [uview(11264 + i * 2048, 2048, F32, "p (c t) -> p c t", c=2) for i in range(2)]
            b_gt = [UB("g0"), UB("g1")]
            A = make_act(N, hT, xn_t, sq2, rstd_t, actv, gtmps, b_act, b_gt)

            pre = [sb([P, 3 + N], F32) for _ in range(2)]; b_pre = [Buf("pre0"), Buf("pre1")]
            ycv = [sb([P, 15 + N], F32) for _ in range(2)]; b_ycv = [Buf("y0"), Buf("y1")]
            for _i in range(2):
                memset("dve", ycv[_i][:], 0.0, writes=[b_ycv[_i]])
            rn = [sb([P, N], F32) for _ in range(2)]; b_rn = [Buf("rn0"), Buf("rn1")]
            chalo = sb([P, DEPTH, 24, 3], F32); b_chalo = Buf("chalo")
            phalo = sb([P, DEPTH, 8, 15], F32); b_phalo = Buf("phalo")
            memset("dve", chalo[:], 0.0, writes=[b_chalo])
            memset("dve", phalo[:], 0.0, writes=[b_phalo])
            Sst = [sb([P, HB, P], F32) for _ in range(2)]; b_S = [Buf("S0"), Buf("S1")]
            Sb = sb([P, HB, P], BF16); b_Sb = Buf("Sb")
            Xg = sb([8, HB, CH], F32); b_X = Buf("X")
            gc_tok = sb([P, NJ, H], F32); be_tok = sb([P, NJ, H], F32); b_tok = Buf("tok")
            bg_tok = sb([P, NJ, H], F32); kd_tok = sb([P, NJ, H], F32); el_tok = sb([P, NJ, H], F32)
            glb = sb([8, P], F32); b_glb = Buf("glb")
            kbT = sb([P, HB, CH], BF16); qgT = sb([P, HB, CH], BF16); b_kq = Buf("kbqg")
            dm = sb([P, HB, CH], F32); decTs = sb([P, HB, CH], F32); b_dec = Buf("dec")
            kbg = sb([P, HB, CH], BF16); kdec = sb([P, HB, CH], BF16); vb = sb([P, HB, CH], BF16); b_tokm = Buf("tokm")
            qkTb = sb([P, HB, CH], BF16); b_qk = Buf("qk")
            oTc = sb([P, HB, CH], F32); b_oTc = Buf("oTc")
            sqo = sb([P, HB, CH], BF16); rno = sb([P, HB, CH], F32); b_no = Buf("no")
            dg = [sb([P, 2, N], BF16) for _ in range(2)]; b_dg = [Buf("dg0"), Buf("dg1")]
            Rbf = sb([P, HB, CH], BF16); b_Rbf = Buf("Rbf")
            wTn = sb([P, HB, CH], BF16); b_wTn = Buf("wTn")
            vnew = sb([P, HB, CH], BF16); b_vnew = Buf("vnew")

            HT = []
            for _sl in range(2):
                if _sl == 0:
                    _t = dict(dm=dm, decTs=decTs, kbT=kbT, qgT=qgT, kbg=kbg, kdec=kdec, vb=vb, qkTb=qkTb, Rbf=Rbf)
                else:
                    _t = dict(dm=sb([P, HB, CH], F32), decTs=sb([P, HB, CH], F32))
                    for _k in ("kbT", "qgT", "kbg", "kdec", "vb", "qkTb", "Rbf"):
                        _t[_k] = sb([P, HB, CH], BF16)
                _t.update(b_dec=Buf(f"dec{_sl}"), b_kq=Buf(f"kbqg{_sl}"), b_tokm=Buf(f"tokm{_sl}"), b_qk=Buf(f"qk{_sl}"), b_Rbf=Buf(f"Rbf{_sl}"))
                _t["it"] = [(uview(itoff[i] + 512 * _sl, 512, BF16, "p (h c) -> p h c", h=HB), UB(f"it{_sl}_{i}", True)) for i in range(12)]
                HT.append(_t)

            def bc_h(ap2d, nh):
                return ap2d.unsqueeze(1).to_broadcast([P, nh, CH])
            b_ndp = [[Buf(f"ndp{l}_{hb}") for hb in range(2)] for l in range(DEPTH)]

            v3 = lambda x: x.rearrange("p (h c) -> p h c", h=HB)
            f2 = lambda x: x[:].rearrange("p j h -> p (j h)")

            for t in range(NT):
                tok0 = t * TT
                dma("sp", "xin", hT[:], xpT[:, :, tok0:tok0 + TT], writes=[A["b_h"]])
                for l in range(DEPTH):
                    load_small_w(l)
                    rmsnorm_to_xn(A, vec[:, l, O_NM:O_NM + KC])
                    xn, bxn = A["xn"], A["b_xn"]
                    switch(tokU)
                    switch(tokA)
                    braw, gcA, gcB = g8a, g8b, g8c
                    gates_fm(A, l, braw[:], gcA[:], b_g8)
                    src, dst = gcA, gcB
                    sh = 1
                    while sh < CH:
                        s3 = src[:].rearrange("p (j c) -> p j c", c=CH)
                        d3 = dst[:].rearrange("p (j c) -> p j c", c=CH)
                        cp("dve", d3[:, :, 0:sh], s3[:, :, 0:sh], reads=[b_g8], writes=[b_g8])
                        tt("dve", d3[:, :, sh:], s3[:, :, sh:], s3[:, :, 0:CH - sh], ALU.add, reads=[b_g8], writes=[b_g8])
                        src, dst = dst, src
                        sh *= 2
                    gcT = src
                    bank, bb = pbank()
                    for j in range(NJ):
                        tr(bank[:, j * 8:(j + 1) * 8], gcT[:, j * CH:(j + 1) * CH], ident[0:8, 0:8], reads=[b_g8] + CB, writes=[bb])
                        tr(bank[:, 64 + j * 8:64 + (j + 1) * 8], braw[:, j * CH:(j + 1) * CH], ident[0:8, 0:8], reads=[b_g8] + CB, writes=[bb])
                    cp("dve", f2(gc_tok), bank[:, 0:NJ * 8], reads=[bb], writes=[b_tok])
                    cp("dve", f2(be_tok), bank[:, 64:64 + NJ * 8], reads=[bb], writes=[b_tok])
                    bank, bb = pbank()
                    for j in range(NJ):
                        col = j * CH + CH - 1
                        ts("dve", glb[:], ones_f[0:8, :], gcT[:, col:col + 1], ALU.mult, reads=[b_g8] + CB, writes=[b_glb])
                        mm(bank[:, j * 8:(j + 1) * 8], glb[:], ident[0:8, 0:8], reads=[b_glb] + CB, writes=[bb])
                    tt("dve", f2(kd_tok), bank[:, 0:NJ * 8], f2(gc_tok), ALU.subtract, reads=[bb, b_tok], writes=[b_tok])
                    act(f2(kd_tok), f2(kd_tok), AF.Exp, reads=[b_tok], writes=[b_tok])
                    act(f2(el_tok), bank[:, 0:NJ * 8], AF.Exp, reads=[bb], writes=[b_tok])
                    act(f2(bg_tok), f2(gc_tok), AF.Exp, reads=[b_tok], writes=[b_tok])
                    tt("dve", f2(bg_tok), f2(bg_tok), f2(be_tok), ALU.mult, reads=[b_tok], writes=[b_tok])

                    for pb in range(4):
                        wv, wb = wload(WIN[l, 16 + pb], KC, 256)

                        def h_p(m, ps, bb, pb=pb):
                            ci = pb * 2 + m
                            cp("dve", pT[:, ci, 0:15], phalo[:, l, ci, :], reads=[b_phalo], writes=[b_p])
                            cp("act", pT[:, ci, 15:15 + N], ps, reads=[bb], writes=[b_p])
                            cp("dve", phalo[:, l, ci, :], pT[:, ci, N:N + 15], reads=[b_p], writes=[b_phalo])
                        proj_chunks(A, wv, wb, 2, KC, lambda kc: xn[:, kc, :], [bxn], h_p)
                    if t == NT - 1:
                        dma("sp", "npp", nppT[l], phalo[:, l, :, :], reads=[b_phalo])
                    for half in range(2):
                        for gg in range(2):
                            g = half * 2 + gg
                            win = POOL_WINDOWS[g]
                            dgt, bdg = dg[g % 2], b_dg[g % 2]
                            L = 15 + N
                            for c2 in range(2):
                                ci = g * 2 + c2
                                cur, curb = pT[:, ci, :], b_p
                                sh = 1
                                k2 = 0
                                while sh < win:
                                    nxt, nb_ = ycv[k2 % 2], b_ycv[k2 % 2]
                                    tt("dve", nxt[:, sh:L], cur[:, sh:L], cur[:, 0:L - sh], ALU.add, reads=[curb], writes=[nb_])
                                    cur, curb = nxt[:, :], nb_
                                    k2 += 1
                                    sh *= 2
                                stt("dve", dgt[:, c2, :], cur[:, 15:15 + N], 1.0 / win, pT[:, ci, 15:15 + N], ALU.mult, ALU.subtract,
                                    reads=[curb, b_p], writes=[bdg])
                                if t == 0:
                                    o_, ob_ = ycv[k2 % 2], b_ycv[k2 % 2]
                                    tt("dve", o_[:, 0:15], cur[:, 15:30], invc[:, g * 16:g * 16 + 15], ALU.mult, reads=[curb] + CB, writes=[ob_])
                                    tt("dve", dgt[:, c2, 0:15], o_[:, 0:15], pT[:, ci, 15:30], ALU.subtract, reads=[ob_, b_p], writes=[bdg])
                            for m in range(2):
                                bank, bb = pbank()
                                for kc in range(2):
                                    mm(bank[:, 0:N], wp[:, g, kc, m * 128:(m + 1) * 128], dgt[:, kc, :], start=(kc == 0), stop=(kc == 1),
                                       reads=[b_wsm, bdg], writes=[bb])
                                ci = g * 2 + m
                                ts("dve", mixh[:, gg * 2 + m, :], bank[:, 0:N], vec[:, l, O_PS + ci:O_PS + ci + 1], ALU.mult,
                                   reads=[bb] + CB, writes=[b_mixh])
                        out_proj_part(A, l, 2 + half, mixh, b_mixh)
                    switch(tokA)

                    for hb in range(2):
                        cwv = vec[:, l, O_CW:O_CW + 96].rearrange("p (c i) -> p c i", i=4)
                        _ci = [0]
                        for typ in range(4):
                            for sub in range(2):
                                wv, wb = wload(WIN[l, hb * 8 + typ * 2 + sub], KC, 256)

                                def h_in(m, ps, bb, typ=typ, sub=sub):
                                    h4 = sub * 2 + m
                                    if typ == 3:
                                        act(zT[:, h4, :], ps, AF.Silu, reads=[bb], writes=[b_z])
                                        return
                                    ci = typ * 8 + hb * 4 + h4
                                    i2 = _ci[0] % 2
                                    _ci[0] += 1
                                    pr, bpr = pre[i2], b_pre[i2]
                                    cp("dve", pr[:, 0:3], chalo[:, l, ci, :], reads=[b_chalo], writes=[bpr])
                                    cp("act", pr[:, 3:3 + N], ps, reads=[bb], writes=[bpr])
                                    cp("dve", chalo[:, l, ci, :], pr[:, N:N + 3], reads=[bpr], writes=[b_chalo])
                                    y, by = ycv[i2], b_ycv[i2]
                                    ts("dve", y[:, 0:N], pr[:, 0:N], cwv[:, ci, 0:1], ALU.mult, reads=[bpr] + CB, writes=[by])
                                    for i in range(1, 4):
                                        stt("dve", y[:, 0:N], pr[:, i:i + N], cwv[:, ci, i:i + 1], y[:, 0:N], ALU.mult, ALU.add,
                                            reads=[bpr, by] + CB, writes=[by])
                                    dstT, bdst = ((qT, b_q), (kT, b_k), (vT, b_v))[typ]
                                    act(dstT[:, h4, :], y[:, 0:N], AF.Silu, reads=[by], writes=[bdst])
                                proj_chunks(A, wv, wb, 2, KC, lambda kc: xn[:, kc, :], [bxn], h_in)
                        for typ in (1, 0):
                            dstT, bdst = (qT, b_q) if typ == 0 else (kT, b_k)
                            nbanks = []
                            for h4 in range(HB):
                                sqb, bsqb = A["sq"][h4 % 2], A["b_sq"][h4 % 2]
                                act(sqb[:, 0:N], dstT[:, h4, :], AF.Square, reads=[bdst], writes=[bsqb])
                                bk2, bb2 = pbank()
                                mm(bk2[:, 0:N], ones_b[:], sqb[:, 0:N], reads=[bsqb] + CB, writes=[bb2])
                                nbanks.append((bk2, bb2))

                            def scale(h4, typ=typ, dstT=dstT, bdst=bdst):
                                i2 = h4 % 2
                                if typ == 0:
                                    stt("dve", dstT[:, h4, :], dstT[:, h4, :], float(P) ** -0.5, rn[i2][:], ALU.mult, ALU.mult,
                                        reads=[bdst, b_rn[i2]], writes=[bdst])
                                else:
                                    tt("dve", dstT[:, h4, :], dstT[:, h4, :], rn[i2][:], ALU.mult, reads=[bdst, b_rn[i2]], writes=[bdst])
                            for h4 in range(HB):
                                i2 = h4 % 2
                                bk2, bb2 = nbanks[h4]
                                act(rn[i2][:], bk2[:, 0:N], AF.Ln, bias=EPS, reads=[bb2], writes=[b_rn[i2]])
                                act(rn[i2][:], rn[i2][:], AF.Exp, scale=-0.5, reads=[b_rn[i2]], writes=[b_rn[i2]])
                                if h4 >= 1:
                                    scale(h4 - 1)
                            scale(HB - 1)
                        if t == NT - 1 and hb == 1:
                            dma("sp", "ncp", ncpT[l], chalo[:, l, :, :], reads=[b_chalo])
                        Sh, bSh = Sst[hb], b_S[hb]
                        if t == 0:
                            memset("dve", Sh[:], 0.0, writes=[bSh])
                            memset("dve", Sb[:], 0.0, writes=[b_Sb])
                        else:
                            dma("sp", f"sld{hb}", Sh[:], ndp[l, hb * HB:(hb + 1) * HB].rearrange("h d v -> d h v"), reads=[b_ndp[l][hb]], writes=[bSh])
                            cp("act", Sb[:], Sh[:], reads=[bSh], writes=[b_Sb])
                        hs = slice(hb * HB, (hb + 1) * HB)
                        i8b = ident[0:8, hb * HB:(hb + 1) * HB].unsqueeze(2).to_broadcast([8, HB, CH])
                        k3 = PSB[:, 0:512].rearrange("p (h c) -> p h c", h=HB)
                        v3b = PSB[:, 512:1024].rearrange("p (h c) -> p h c", h=HB)

                        def mm4(lh, blh, rh, brh):
                            bank, bb = pbank()
                            for h4 in range(HB):
                                mm(bank[:, h4 * CH:(h4 + 1) * CH], lh[:, h4, :], rh[:, h4, :], reads=[blh, brh], writes=[bb])
                            return v3(bank), bb

                        def head(j, T):
                            cs = slice(j * CH, (j + 1) * CH)
                            dm, decTs, b_dec = T["dm"], T["decTs"], T["b_dec"]
                            kbT, qgT, b_kq = T["kbT"], T["qgT"], T["b_kq"]
                            kbg, kdec, vb, b_tokm = T["kbg"], T["kdec"], T["vb"], T["b_tokm"]
                            qkTb, b_qk, Rbf, b_Rbf = T["qkTb"], T["b_qk"], T["Rbf"], T["b_Rbf"]
                            (Nn, b_N), (N0, b_N0), (M0, b_M0) = T["it"][0], T["it"][1], T["it"][2]
                            Ml = [T["it"][3 + i] for i in range(3)]
                            (Ra, b_Ra), (Rb_, b_Rb) = T["it"][6], T["it"][7]
                            (N2, b_N2), (M2, b_M2), (Ta, b_Ta), (Tb, b_Tb) = T["it"][8], T["it"][9], T["it"][10], T["it"][11]
                            N4, b_N4, M4, b_M4 = N0, b_N0, M0, b_M0
                            M8, b_M8 = N2, b_N2
                            Yt, b_Y = Nn, b_N
                            tt("dve", Xg[:], gcT[:, cs].unsqueeze(1).to_broadcast([8, HB, CH]), i8b, ALU.mult, reads=[b_g8] + CB, writes=[b_X])
                            GBk, GBb = pbank()
                            mm(GBk, ones_f[0:8, :], Xg[:].rearrange("p h c -> p (h c)"), reads=[b_X] + CB, writes=[GBb])
                            g3 = v3(GBk)
                            act(dm[:], g3, AF.Exp, reads=[GBb], writes=[b_dec])
                            tt("dve", qgT[:], qT[:, :, cs], dm[:], ALU.mult, reads=[b_q, b_dec], writes=[b_kq])
                            for h4 in range(HB):
                                h = hb * HB + h4
                                stt("dve", dm[:, h4, :], g3[:, h4, :], gc_tok[:, j, h:h + 1], maskT, ALU.subtract, ALU.min,
                                    reads=[GBb, b_tok] + CB, writes=[b_dec])
                            yield
                            act(dm[:], dm[:], AF.Exp, reads=[b_dec], writes=[b_dec])
                            tt("dve", decTs[:], dm[:], bc_h(strict01, HB), ALU.mult, reads=[b_dec] + CB, writes=[b_dec])
                            yield
                            tt("dve", Xg[:], braw[:, cs].unsqueeze(1).to_broadcast([8, HB, CH]), i8b, ALU.mult, reads=[b_g8] + CB, writes=[b_X])
                            BBk, BBb = pbank()
                            mm(BBk, ones_f[0:8, :], Xg[:].rearrange("p h c -> p (h c)"), reads=[b_X] + CB, writes=[BBb])
                            tt("dve", kbT[:], kT[:, :, cs], v3(BBk), ALU.mult, reads=[b_k, BBb], writes=[b_kq])
                            yield
                            bank, bb = pbank()
                            for h4 in range(HB):
                                mm(bank[:, h4 * CH:(h4 + 1) * CH], kT[:, h4, cs], kbT[:, h4, :], reads=[b_k, b_kq], writes=[bb])
                            tt("dve", Nn[:], v3(bank), decTs[:], ALU.mult, reads=[bb, b_dec], writes=[b_N])
                            tt("dve", N0[:], Nn[:], bc_h(bd16, HB), ALU.mult, reads=[b_N] + CB, writes=[b_N0])
                            yield
                            for h4 in range(HB):
                                tr(k3[:, h4, :], Nn[:, h4, :], ident_b[:], reads=[b_N] + CB, writes=[psb_buf])
                            tt("dve", M0[:], k3, bc_h(bd16, HB), ALU.mult, reads=[psb_buf] + CB, writes=[b_M0])
                            for i in range(3):
                                tt("dve", Ml[i][0][:], k3, bc_h(Lm[i], HB), ALU.mult, reads=[psb_buf] + CB, writes=[Ml[i][1]])
                            tt("dve", Ra[:], bc_h(ident, HB), N0[:], ALU.subtract, reads=[b_N0] + CB, writes=[b_Ra])
                            yield
                            Rc, bRc, Rn, bRn = Ra, b_Ra, Rb_, b_Rb
                            p1, bb1 = mm4(M0, b_M0, N0, b_N0)
                            p2, bb2 = mm4(N0, b_N0, M0, b_M0)
                            cp("act", N2[:], p1, reads=[bb1], writes=[b_N2])
                            cp("act", M2[:], p2, reads=[bb2], writes=[b_M2])
                            yield
                            p3, bb3 = mm4(M2, b_M2, Rc, bRc)
                            tt("dve", Rn[:], Rc[:], p3, ALU.add, reads=[bb3, bRc], writes=[bRn])
                            Rc, bRc, Rn, bRn = Rn, bRn, Rc, bRc
                            p1, bb1 = mm4(M2, b_M2, N2, b_N2)
                            p2, bb2 = mm4(N2, b_N2, M2, b_M2)
                            cp("act", N4[:], p1, reads=[bb1], writes=[b_N4])
                            cp("act", M4[:], p2, reads=[bb2], writes=[b_M4])
                            yield
                            p3, bb3 = mm4(M4, b_M4, Rc, bRc)
                            tt("dve", Rn[:], Rc[:], p3, ALU.add, reads=[bb3, bRc], writes=[bRn])
                            Rc, bRc, Rn, bRn = Rn, bRn, Rc, bRc
                            p2, bb2 = mm4(N4, b_N4, M4, b_M4)
                            cp("act", M8[:], p2, reads=[bb2], writes=[b_M8])
                            yield
                            p3, bb3 = mm4(M8, b_M8, Rc, bRc)
                            tt("dve", Rn[:], Rc[:], p3, ALU.add, reads=[bb3, bRc], writes=[bRn])
                            Rc, bRc, Rn, bRn = Rn, bRn, Rc, bRc
                            yield
                            for h4 in range(HB):
                                tr(k3[:, h4, :], Rc[:, h4, :], ident_b[:], reads=[bRc] + CB, writes=[psb_buf])
                            Tc, bTc, Tn, bTn = Ta, b_Ta, Tb, b_Tb
                            cp("act", Tc[:], k3, reads=[psb_buf], writes=[bTc])
                            yield
                            for i in range(3):
                                py, bby = mm4(Ml[i][0], Ml[i][1], Rc, bRc)
                                cp("act", Yt[:], py, reads=[bby], writes=[b_Y])
                                yield
                                pz, bbz = mm4(Tc, bTc, Yt, b_Y)
                                if i < 2:
                                    pzt, bbzt = mm4(Yt, b_Y, Tc, bTc)
                                    tt("dve", Rn[:], Rc[:], pz, ALU.subtract, reads=[bbz, bRc], writes=[bRn])
                                    tt("dve", Tn[:], Tc[:], pzt, ALU.subtract, reads=[bbzt, bTc], writes=[bTn])
                                    Rc, bRc, Rn, bRn = Rn, bRn, Rc, bRc
                                    Tc, bTc, Tn, bTn = Tn, bTn, Tc, bTc
                                else:
                                    tt("dve", Rbf[:], Rc[:], pz, ALU.subtract, reads=[bbz, bRc], writes=[b_Rbf])
                                yield
                            for h4 in range(HB):
                                tr(k3[:, h4, :], kT[:, h4, cs], ident_b[:], reads=[b_k] + CB, writes=[psb_buf])
                                tr(v3b[:, h4, :], vT[:, h4, cs], ident_b[:], reads=[b_v] + CB, writes=[psb_buf])
                            tt("dve", kbg[:], k3, bg_tok[:, j, hs].unsqueeze(2).to_broadcast([P, HB, CH]), ALU.mult, reads=[psb_buf, b_tok], writes=[b_tokm])
                            tt("dve", kdec[:], k3, kd_tok[:, j, hs].unsqueeze(2).to_broadcast([P, HB, CH]), ALU.mult, reads=[psb_buf, b_tok], writes=[b_tokm])
                            tt("dve", vb[:], v3b, be_tok[:, j, hs].unsqueeze(2).to_broadcast([P, HB, CH]), ALU.mult, reads=[psb_buf, b_tok], writes=[b_tokm])
                            bank, bb = pbank()
                            for h4 in range(HB):
                                mm(bank[:, h4 * CH:(h4 + 1) * CH], kT[:, h4, cs], qT[:, h4, cs], reads=[b_k, b_q], writes=[bb])
                            tt("dve", qkTb[:], v3(bank), dm[:], ALU.mult, reads=[bb, b_dec], writes=[b_qk])

                        def tail(j, T):
                            cs = slice(j * CH, (j + 1) * CH)
                            qgT, b_kq = T["qgT"], T["b_kq"]
                            kbg, kdec, vb, b_tokm = T["kbg"], T["kdec"], T["vb"], T["b_tokm"]
                            qkTb, b_qk, Rbf, b_Rbf = T["qkTb"], T["b_qk"], T["Rbf"], T["b_Rbf"]
                            bank, bb = pbank()
                            for h4 in range(HB):
                                mm(bank[:, h4 * CH:(h4 + 1) * CH], kbg[:, h4, :], Rbf[:, h4, :], reads=[b_tokm, b_Rbf], writes=[bb])
                            act(wTn[:], v3(bank), AF.Copy, scale=-1.0, reads=[bb], writes=[b_wTn])
                            bank, bb = pbank()
                            for h4 in range(HB):
                                mm(bank[:, h4 * CH:(h4 + 1) * CH], Rbf[:, h4, :], vb[:, h4, :], start=True, stop=False, reads=[b_Rbf, b_tokm], writes=[bb])
                                mm(bank[:, h4 * CH:(h4 + 1) * CH], wTn[:, h4, :], Sb[:, h4, :], start=False, stop=True, reads=[b_wTn, b_Sb], writes=[bb])
                            cp("act", vnew[:], v3(bank), reads=[bb], writes=[b_vnew])
                            bank, bb = pbank()
                            for h4 in range(HB):
                                mm(bank[:, h4 * CH:(h4 + 1) * CH], Sb[:, h4, :], qgT[:, h4, :], start=True, stop=False, reads=[b_Sb, b_kq], writes=[bb])
                                mm(bank[:, h4 * CH:(h4 + 1) * CH], vnew[:, h4, :], qkTb[:, h4, :], start=False, stop=True, reads=[b_vnew, b_qk], writes=[bb])
                            cp("act", oTc[:], v3(bank), reads=[bb], writes=[b_oTc])
                            bank, bb = pbank()
                            for h4 in range(HB):
                                mm(bank[:, h4 * CH:(h4 + 1) * CH], kdec[:, h4, :], vnew[:, h4, :], reads=[b_tokm, b_vnew], writes=[bb])
                            tt("dve", Sh[:], Sh[:], el_tok[:, j, hs].unsqueeze(2).to_broadcast([P, HB, CH]), ALU.mult, reads=[bSh, b_tok], writes=[bSh])
                            tt("dve", Sh[:], Sh[:], v3(bank), ALU.add, reads=[bb, bSh], writes=[bSh])
                            cp("act", Sb[:], Sh[:], reads=[bSh], writes=[b_Sb])
                            act(sqo[:], oTc[:], AF.Square, reads=[b_oTc], writes=[b_no])
                            bank, bb = pbank()
                            mm(bank, onesV_b[:], sqo[:].rearrange("p h c -> p (h c)"), reads=[b_no] + CB, writes=[bb])
                            act(rno[:], v3(bank), AF.Ln, bias=EPS, reads=[bb], writes=[b_no])
                            act(rno[:], rno[:], AF.Exp, scale=-0.5, reads=[b_no], writes=[b_no])
                            stt("dve", rno[:], oTc[:], vec[:, l, O_DN:O_DN + 1], rno[:], ALU.mult, ALU.mult, reads=[b_oTc, b_no] + CB, writes=[b_no])
                            tt("dve", mixh[:, :, cs], rno[:], zT[:, :, cs], ALU.mult, reads=[b_no, b_z], writes=[b_mixh])
                        for jp in range(0, NJ, 2):
                            gens = [head(jp, HT[0]), head(jp + 1, HT[1])]
                            while gens:
                                for g in list(gens):
                                    try:
                                        next(g)
                                    except StopIteration:
                                        gens.remove(g)
                            tail(jp, HT[0])
                            tail(jp + 1, HT[1])
                        dma("sp", f"sst{hb}", ndp[l, hb * HB:(hb + 1) * HB].rearrange("h d v -> d h v"), Sh[:], reads=[bSh], writes=[b_ndp[l][hb]])
                        out_proj_part(A, l, hb, mixh, b_mixh)
                    switch(tokU)
                    ffn(A, l)
                final_norm_out(A, ypT[:, :, tok0:tok0 + TT], "yout")

        if with_sample:
            N = NS
            NP = NS * H
            switch(tokU)
            hTs = sb([P, KC, N], F32, "hTs"); xns = sb([P, KC, N], BF16, "xns")
            acts = sb([P, 22, N], BF16, "acts"); gts = [sb([P, 2, N], F32) for _ in range(2)]
            B = make_act(N, hTs, xns, sq2, rstd_t, acts, gts, Buf("acts"), [Buf("gs0"), Buf("gs1")])
            Sin = [uview(i * 2048, 2048, F32, "p (h v) -> p h v", h=H) for i in range(2)]; b_Sin = [UB("sin0"), UB("sin1")]
            Sout = [uview(4096 + i * 2048, 2048, F32, "p (h v) -> p h v", h=H) for i in range(2)]; b_Sout = [UB("so0"), UB("so1")]
            s_pre = uview(8192, 3072, F32, "p (c b i) -> p c b i", c=24, i=4); b_spre = UB("spre")
            s_p = uview(11264, 4096, F32, "p (c b i) -> p c b i", c=8, i=16); b_sp = UB("spool")
            s_y = uview(15360, 768, F32, "p (c b) -> p c b", c=24); s_y2 = uview(16128, 768, F32, "p (c b) -> p c b", c=24); b_sy = UB("sy")
            o = [16896]

            def u128(n=128):
                v = uview(o[0], 2 * n, F32)
                o[0] += 2 * n
                return v
            kqf = u128(256); kq = kqf.rearrange("p (j t) -> p j t", t=2); b_kqs = UB("kq")
            kvqsf = u128(256); kvqs = kvqsf.rearrange("p (j t) -> p j t", t=2); b_kvqs = UB("kvqs")
            vTs = u128(); b_vs = UB("vs")
            zTs = u128(); b_zs = UB("zs")
            BBs = u128(); EGs = u128(); QKs = u128(); b_bc = UB("bcs")
            dvT = u128(); oTs = u128(); b_dv = UB("dv")
            s_rno = u128(); b_sno = UB("sno")
            assert o[0] <= UW
            Ktok = sb([P, P], F32); DVtok = sb([P, P], F32); b_ktok = Buf("ktok")
            Km = [sb([P, P], F32) for _ in range(2)]; b_Km = [Buf("km0"), Buf("km1")]
            s_sq = sb([P, 16 * NS], BF16); s_rn = sb([P, 16 * NS], F32); b_srn = Buf("srn")
            s_sum = sb([P, 8, NS], F32); s_d = sb([P, 8, NS], BF16); b_sd = Buf("sd")
            s_sqo = sb([P, NP], BF16)
            Xs = sb([8, NS, H], F32); b_Xs = Buf("Xs")
            mixs = sb([P, 4, N], BF16); b_mixs = Buf("mixs")

            dma("sp", "xin_s", hTs[:], xsT[:, :, :], writes=[B["b_h"]])
            for l in range(DEPTH):
                load_small_w(l)
                rmsnorm_to_xn(B, vec[:, l, O_NM:O_NM + KC])
                xn, bxn = B["xn"], B["b_xn"]
                dma("sp", "scl", s_pre[:, :, :, 0:3], scS[l], writes=[b_spre])
                dma("sp", "spl", s_p[:, :, :, 0:15], spS[l], writes=[b_sp])
                s_b8, s_g8, s_eg8 = g8a[:, 0:N], g8b[:, 0:N], g8c[:, 0:N]
                gates_fm(B, l, s_b8, s_g8, b_g8)
                act(s_eg8, s_g8, AF.Exp, reads=[b_g8], writes=[b_g8])
                i8 = ident[0:8, 0:8].unsqueeze(1).to_broadcast([8, NS, H])
                for srcv, dstv in ((s_b8, BBs), (s_eg8, EGs)):
                    tt("dve", Xs[:], srcv.unsqueeze(2).to_broadcast([8, NS, H]), i8, ALU.mult, reads=[b_g8] + CB, writes=[b_Xs])
                    bank, bb = pbank()
                    mm(bank[:, 0:NP], ones_f[0:8, :], Xs[:].rearrange("p b h -> p (b h)"), reads=[b_Xs] + CB, writes=[bb])
                    cp("act", dstv, bank[:, 0:NP], reads=[bb], writes=[b_bc])
                for pb in range(4):
                    wv, wb = wload(WIN[l, 16 + pb], KC, 256)

                    def h_ps(m, ps, bb, pb=pb):
                        cp("act", s_p[:, pb * 2 + m, :, 15], ps, reads=[bb], writes=[b_sp])
                    proj_chunks(B, wv, wb, 2, KC, lambda kc: xn[:, kc, :], [bxn], h_ps)
                dma("sp", "spo", npsT[l], s_p[:, :, :, 1:16], reads=[b_sp])
                for g in range(4):
                    win = POOL_WINDOWS[g]
                    S.op("dve", (lambda e, g=g, win=win: e.tensor_reduce(out=s_sum[:, 2 * g:2 * g + 2, :], in_=s_p[:, 2 * g:2 * g + 2, :, 16 - win:16],
                                                                      axis=AX.X, op=ALU.add)), reads=[b_sp], writes=[b_sd])
                    stt("dve", s_d[:, 2 * g:2 * g + 2, :], s_sum[:, 2 * g:2 * g + 2, :], 1.0 / win, s_p[:, 2 * g:2 * g + 2, :, 15], ALU.mult, ALU.subtract,
                        reads=[b_sd, b_sp], writes=[b_sd])
                for half in range(2):
                    for gg in range(2):
                        g = half * 2 + gg
                        for m in range(2):
                            bank, bb = pbank()
                            for kc in range(2):
                                mm(bank[:, 0:N], wp[:, g, kc, m * 128:(m + 1) * 128], s_d[:, g * 2 + kc, :], start=(kc == 0), stop=(kc == 1),
                                   reads=[b_wsm, b_sd], writes=[bb])
                            ci = g * 2 + m
                            ts("dve", mixs[:, gg * 2 + m, :], bank[:, 0:N], vec[:, l, O_PS + ci:O_PS + ci + 1], ALU.mult, reads=[bb] + CB, writes=[b_mixs])
                    out_proj_part(B, l, 2 + half, mixs, b_mixs)
                zT3 = zTs.rearrange("p (b h) -> p b h", h=H)
                for hb in range(2):
                    for typ in range(4):
                        for sub in range(2):
                            wv, wb = wload(WIN[l, hb * 8 + typ * 2 + sub], KC, 256)

                            def h_in_s(m, ps, bb, typ=typ, sub=sub, hb=hb):
                                hh = hb * 4 + sub * 2 + m
                                if typ == 3:
                                    act(zT3[:, :, hh], ps, AF.Silu, reads=[bb], writes=[b_zs])
                                else:
                                    cp("act", s_pre[:, typ * 8 + hh, :, 3], ps, reads=[bb], writes=[b_spre])
                            proj_chunks(B, wv, wb, 2, KC, lambda kc: xn[:, kc, :], [bxn], h_in_s)
                dma("sp", "sco", ncsT[l], s_pre[:, :, :, 1:4], reads=[b_spre])
                cwv = vec[:, l, O_CW:O_CW + 96].rearrange("p (c i) -> p c i", i=4)
                cwb = lambda i: cwv[:, :, i:i + 1].to_broadcast([P, 24, NS])
                tt("dve", s_y, s_pre[:, :, :, 0], cwb(0), ALU.mult, reads=[b_spre] + CB, writes=[b_sy])
                for i in range(1, 4):
                    tt("dve", s_y2, s_pre[:, :, :, i], cwb(i), ALU.mult, reads=[b_spre] + CB, writes=[b_sy])
                    tt("dve", s_y, s_y, s_y2, ALU.add, reads=[b_sy], writes=[b_sy])
                act(s_y, s_y, AF.Silu, reads=[b_sy], writes=[b_sy])
                act(s_sq[:].rearrange("p (c b) -> p c b", b=NS), s_y[:, 0:16, :], AF.Square, reads=[b_sy], writes=[b_srn])
                bank, bb = pbank()
                mm(bank[:, 0:16 * NS], ones_b[:], s_sq[:], reads=[b_srn] + CB, writes=[bb])
                act(s_rn[:], bank[:, 0:16 * NS], AF.Ln, bias=EPS, reads=[bb], writes=[b_srn])
                act(s_rn[:], s_rn[:], AF.Exp, scale=-0.5, reads=[b_srn], writes=[b_srn])
                rn3 = s_rn[:].rearrange("p (c b) -> p c b", b=NS)
                kq4 = kq.rearrange("p (b h) t -> p b h t", h=H)
                vT3 = vTs.rearrange("p (b h) -> p b h", h=H)
                for hh in range(H):
                    stt("dve", kq4[:, :, hh, 1], s_y[:, hh, :], float(P) ** -0.5, rn3[:, hh, :], ALU.mult, ALU.mult, reads=[b_sy, b_srn], writes=[b_kqs])
                    tt("dve", kq4[:, :, hh, 0], s_y[:, 8 + hh, :], rn3[:, 8 + hh, :], ALU.mult, reads=[b_sy, b_srn], writes=[b_kqs])
                    cp("dve", vT3[:, :, hh], s_y[:, 16 + hh, :], reads=[b_sy], writes=[b_vs])
                tt("dve", dvT, kq[:, :, 0], kq[:, :, 1], ALU.mult, reads=[b_kqs], writes=[b_dv])
                bank, bb = pbank()
                mm(bank[:, 0:NP], ones_f, dvT, reads=[b_dv] + CB, writes=[bb])
                cp("act", QKs, bank[:, 0:NP], reads=[bb], writes=[b_bc])
                kvbank, kvbb = pbank()
                for b in range(NS):
                    si = b % 2
                    dma(("sp", "act")[si], f"sin{si}", Sin[si], sdS[l, b].rearrange("h d v -> d h v"), writes=[b_Sin[si]])
                    for hh in range(H):
                        jj = b * H + hh
                        mm(kvbank[:, jj * 2:jj * 2 + 2], Sin[si][:, hh, :], kq[:, jj, :], reads=[b_Sin[si], b_kqs], writes=[kvbb])
                cp("act", kvqsf, kvbank[:, 0:NP * 2], reads=[kvbb], writes=[b_kvqs])
                tt("dve", dvT, EGs, kvqs[:, :, 0], ALU.mult, reads=[b_bc, b_kvqs], writes=[b_dv])
                tt("dve", dvT, vTs, dvT, ALU.subtract, reads=[b_vs, b_dv], writes=[b_dv])
                tt("dve", dvT, dvT, BBs, ALU.mult, reads=[b_bc, b_dv], writes=[b_dv])
                tt("dve", oTs, EGs, kvqs[:, :, 1], ALU.mult, reads=[b_bc, b_kvqs], writes=[b_dv])
                tt("dve", s_rno, QKs, dvT, ALU.mult, reads=[b_bc, b_dv], writes=[b_sno])
                tt("dve", oTs, oTs, s_rno, ALU.add, reads=[b_sno, b_dv], writes=[b_dv])
                bank, bb = pbank()
                tr(bank[:, 0:P], kq[:, :, 0], ident, reads=[b_kqs] + CB, writes=[bb])
                tr(bank[:, P:2 * P], dvT, ident, reads=[b_dv] + CB, writes=[bb])
                cp("act", Ktok[:], bank[:, 0:P], reads=[bb], writes=[b_ktok])
                cp("act", DVtok[:], bank[:, P:2 * P], reads=[bb], writes=[b_ktok])
                for b in range(NS):
                    si = b % 2
                    dma("act", f"sin{si}", Sin[si], sdS[l, b].rearrange("h d v -> d h v"), writes=[b_Sin[si]])
                    for hh in range(H):
                        jj = b * H + hh
                        ki = jj % 2
                        ts("dve", Km[ki][:], Ktok[:], ident[:, jj:jj + 1], ALU.mult, reads=[b_ktok] + CB, writes=[b_Km[ki]])
                        bank, bb = pbank()
                        mm(bank[:, 0:P], Km[ki][:], DVtok[:], reads=[b_Km[ki], b_ktok], writes=[bb])
                        stt("dve", Sout[si][:, hh, :], Sin[si][:, hh, :], EGs[:, jj:jj + 1], bank[:, 0:P], ALU.mult, ALU.add,
                            reads=[b_Sin[si], b_bc, bb], writes=[b_Sout[si]])
                    dma("sp", f"sout{si}", nds[l, b].rearrange("h d v -> d h v"), Sout[si], reads=[b_Sout[si]])
                act(s_sqo[:], oTs, AF.Square, reads=[b_dv], writes=[b_sno])
                bank, bb = pbank()
                mm(bank[:, 0:NP], onesV_b[:], s_sqo[:], reads=[b_sno] + CB, writes=[bb])
                act(s_rno, bank[:, 0:NP], AF.Ln, bias=EPS, reads=[bb], writes=[b_sno])
                act(s_rno, s_rno, AF.Exp, scale=-0.5, reads=[b_sno], writes=[b_sno])
                stt("dve", s_rno, oTs, vec[:, l, O_DN:O_DN + 1], s_rno, ALU.mult, ALU.mult, reads=[b_dv, b_sno] + CB, writes=[b_sno])
                tt("dve", s_rno, s_rno, zTs, ALU.mult, reads=[b_sno, b_zs], writes=[b_sno])
                r3 = s_rno.rearrange("p (b h) -> p b h", h=H)
                for hb in range(2):
                    for h4 in range(HB):
                        cp("dve", mixs[:, h4, :], r3[:, :, hb * HB + h4], reads=[b_sno], writes=[b_mixs])
                    out_proj_part(B, l, hb, mixs, b_mixs)
                ffn(B, l)
            final_norm_out(B, ysT[:, :, :], "yout_s")

        for key, cntv in S.dcnt.items():
            if cntv > 0:
                S.ops["sp"].append(("wait", S.sems[key], cntv))
        build_program.stats = {e: len(S.ops[e]) for e in ENGS}

        with nc.Block() as block:
            @block.tensor
            def _(e):
                S.replay("pe", e)

            @block.scalar
            def _(e):
                S.replay("act", e)

            @block.vector
            def _(e):
                S.replay("dve", e)

            @block.gpsimd
            def _(e):
                S.replay("pool", e)

            @block.sync
            def _(e):
                S.replay("sp", e)
    return nc


def make_consts():
    c = np.zeros((P, 1088), np.float32)
    i = np.arange(P)
    s, cc = i[:, None], i[None, :]
    c[:, 0:128] = np.eye(P)
    c[:, 128:256] = np.where(cc >= s, 0.0, -1e4)
    c[:, 256:384] = (cc > s)
    c[:, 384:512] = (s // 16 == cc // 16)
    cp_, sf = i[:, None], i[None, :]
    c[:, 512:640] = (cp_ // 32 == sf // 32) & (cp_ % 32 >= 16) & (sf % 32 < 16)
    c[:, 640:768] = (cp_ // 64 == sf // 64) & (cp_ % 64 >= 32) & (sf % 64 < 32)
    c[:, 768:896] = (cp_ >= 64) & (sf < 64)
    for g, w in enumerate(POOL_WINDOWS):
        tpos = np.arange(16)
        c[:, 896 + g * 16:896 + (g + 1) * 16] = 1.0 / np.minimum(tpos + 1, w)
    c[:, 960:1088] = 1.0
    return c


def prep_weights(w_in, w_out, w_gate_up, w_down, w_pool, norm_mix, norm_ffn, conv_w, pool_scale, dn_norm,
                 a_log, dt_bias, norm_final):
    DEPTH = w_in.shape[0]
    f = np.float32

    def tile_cols(w, col0, ncols, kcs=KC):
        return np.ascontiguousarray(w[:, :, col0:col0 + ncols].reshape(DEPTH, kcs, P, ncols).transpose(0, 2, 1, 3))

    cols = []
    for hb in range(2):
        for typ in range(4):
            for sub in range(2):
                cols.append(typ * 1024 + hb * 512 + sub * 256)
    cols += [4112 + i * 256 for i in range(4)]
    WIN = np.stack([tile_cols(w_in, c0, 256) for c0 in cols], axis=1)
    WBA = tile_cols(w_in, 4096, 16)
    wo = w_out.reshape(DEPTH, 4, 4, P, 2, 1024)
    WOUT = np.ascontiguousarray(wo.transpose(0, 1, 4, 3, 2, 5))
    WG = np.stack([tile_cols(w_gate_up, b * 256, 256) for b in range(22)], axis=1)
    WU = np.stack([tile_cols(w_gate_up, DFF + b * 256, 256) for b in range(22)], axis=1)
    wd = w_down.reshape(DEPTH, 4, 11, P, 8, 256)
    WD = np.ascontiguousarray(wd.transpose(0, 4, 1, 3, 2, 5))
    WP = np.ascontiguousarray(w_pool.reshape(DEPTH, 4, 2, P, 256).transpose(0, 3, 1, 2, 4))
    VEC = np.zeros((P, DEPTH, NV), f)
    VEC[:, :, O_NM:O_NM + 16] = norm_mix.reshape(DEPTH, KC, P).transpose(2, 0, 1)
    VEC[:, :, O_NF:O_NF + 16] = norm_ffn.reshape(DEPTH, KC, P).transpose(2, 0, 1)
    VEC[:, :, O_CW:O_CW + 96] = conv_w.reshape(DEPTH, 4, 24, P).transpose(3, 0, 2, 1).reshape(P, DEPTH, 96)
    VEC[:, :, O_PS:O_PS + 8] = pool_scale.reshape(DEPTH, 8, P).transpose(2, 0, 1)
    VEC[:, :, O_DN] = dn_norm.T
    VECF = np.ascontiguousarray(norm_final.reshape(KC, P).T)
    AB = np.ascontiguousarray(np.stack([a_log, dt_bias], axis=-1).transpose(1, 0, 2))
    return dict(WIN=WIN, WBA=WBA, WOUT=WOUT, WG=WG, WU=WU, WD=WD, WP=WP, VEC=VEC, VECF=VECF, AB=AB,
                CONST=make_consts())


_NC_CACHE = {}


def run(x_prompt, x_sample, state_delta, state_conv, state_pool, norm_mix, w_in, conv_w, a_log, dt_bias, dn_norm,
        w_pool, pool_scale, w_out, norm_ffn, w_gate_up, w_down, norm_final, n_cores=8):
    asf = lambda a: np.asarray(a, np.float32)
    x_prompt, x_sample, state_delta, state_conv, state_pool = map(asf, (x_prompt, x_sample, state_delta, state_conv, state_pool))
    BATCH, SEQ, _ = x_prompt.shape
    DEPTH = w_in.shape[0]
    DECB = x_sample.shape[0]
    NS = DECB // n_cores
    wts = prep_weights(*map(asf, (w_in, w_out, w_gate_up, w_down, w_pool, norm_mix, norm_ffn, conv_w, pool_scale, dn_norm,
                                  a_log, dt_bias, norm_final)))
    key = (SEQ, DEPTH, NS)
    if key not in _NC_CACHE:
        _NC_CACHE[key] = build_program(SEQ, DEPTH, NS)
    nc = _NC_CACHE[key]
    in_maps = []
    for c in range(n_cores):
        b = c % BATCH
        sl = slice(c * NS, (c + 1) * NS)
        m = dict(wts)
        m["xpT"] = np.ascontiguousarray(x_prompt[b].reshape(SEQ, KC, P).transpose(2, 1, 0))
        m["xsT"] = np.ascontiguousarray(x_sample[sl, 0].reshape(NS, KC, P).transpose(2, 1, 0))
        m["sdS"] = np.ascontiguousarray(state_delta[:, sl])
        m["scS"] = np.ascontiguousarray(state_conv[:, sl].reshape(DEPTH, NS, 3, 24, P).transpose(0, 4, 3, 1, 2))
        m["spS"] = np.ascontiguousarray(state_pool[:, sl].reshape(DEPTH, NS, 15, 8, P).transpose(0, 4, 3, 1, 2))
        in_maps.append(m)
    res = run_bass_kernel_spmd(nc, in_maps, core_ids=list(range(n_cores)))
    R = res.results
    y_prompt = np.stack([R[b]["ypT"].transpose(2, 1, 0).reshape(SEQ, D) for b in range(BATCH)])
    y_sample = np.concatenate([R[c]["ysT"].transpose(2, 1, 0).reshape(NS, 1, D) for c in range(n_cores)])
    ndp = np.stack([R[b]["ndp"] for b in range(BATCH)], axis=1)
    ncp = np.stack([R[b]["ncpT"].transpose(0, 3, 2, 1).reshape(DEPTH, 3, 3072) for b in range(BATCH)], axis=1)
    npp = np.stack([R[b]["nppT"].transpose(0, 3, 2, 1).reshape(DEPTH, 15, 1024) for b in range(BATCH)], axis=1)
    nds = np.concatenate([R[c]["nds"] for c in range(n_cores)], axis=1)
    ncs = np.concatenate([R[c]["ncsT"].transpose(0, 3, 4, 2, 1).reshape(DEPTH, NS, 3, 3072) for c in range(n_cores)], axis=1)
    nps = np.concatenate([R[c]["npsT"].transpose(0, 3, 4, 2, 1).reshape(DEPTH, NS, 15, 1024) for c in range(n_cores)], axis=1)
    f = np.float32
    return tuple(np.ascontiguousarray(a, dtype=f) for a in (y_prompt, y_sample, ndp, ncp, npp, nds, ncs, nps))


def kernel(**inputs):
    return run(**inputs)
```

```python
import contextlib
import numpy as np
import concourse.bass as bass
import concourse.mybir as mybir
from concourse.bass_utils import run_bass_kernel_spmd

F32 = mybir.dt.float32
BF16 = mybir.dt.bfloat16
ALU = mybir.AluOpType
AF = mybir.ActivationFunctionType
AX = mybir.AxisListType

P = 128
D = 2048
KC = 16
H = 8
HB = 4
CH = 128
DFF = 5632
FC = 44
EPS = 1e-6
NV = 137
O_NM, O_NF, O_CW, O_PS, O_DN = 0, 16, 32, 128, 136
POOL_WINDOWS = (2, 4, 8, 16)
ENGS = ("pe", "act", "dve", "pool", "sp")


class Buf:
    __slots__ = ("name", "lw", "rd", "tok")

    def __init__(self, name="", tok=()):
        self.name = name
        self.lw = None
        self.rd = {}
        self.tok = tuple(tok)


class Sched:
    def __init__(self, nc, stack):
        self.nc = nc
        self.stack = stack
        self.ops = {e: [] for e in ENGS}
        self.cnt = {e: 0 for e in ENGS}
        self.clock = {e: {} for e in ENGS}
        self.iclock = {}
        self.sems = {}
        self.dcnt = {}
        for e in ("pe", "act", "dve", "pool"):
            self.sems[e] = stack.enter_context(nc.semaphore("s_" + e))

    def dsem(self, key):
        if key not in self.sems:
            self.sems[key] = self.stack.enter_context(self.nc.semaphore("d_" + key))
            self.dcnt[key] = 0
        return key

    def _deps(self, reads, writes):
        deps = {}
        for b in reads:
            if b.lw is not None and deps.get(b.lw[0], 0) < b.lw[1]:
                deps[b.lw[0]] = b.lw[1]
        for b in writes:
            if b.lw is not None and deps.get(b.lw[0], 0) < b.lw[1]:
                deps[b.lw[0]] = b.lw[1]
            for k, i in b.rd.items():
                if deps.get(k, 0) < i:
                    deps[k] = i
        return deps

    def _emit_waits(self, eng, deps, skip_self=False):
        clk = self.clock[eng]
        for k, i in deps.items():
            if k == eng and skip_self:
                continue
            if clk.get(k, 0) >= i:
                continue
            self.ops[eng].append(("wait", self.sems[k], i))
            oc = self.iclock.get((k, i))
            if oc:
                for kk, vv in oc.items():
                    if clk.get(kk, 0) < vv:
                        clk[kk] = vv
            clk[k] = max(clk.get(k, 0), i)

    def _record(self, key, idx, eng, reads, writes):
        c = dict(self.clock[eng])
        c[key] = idx
        self.iclock[(key, idx)] = c
        for b in reads:
            if b.rd.get(key, 0) < idx:
                b.rd[key] = idx
        for b in writes:
            b.lw = (key, idx)
            b.rd = {}

    @staticmethod
    def _with_tok(reads, writes):
        extra = [t for b in list(reads) + list(writes) for t in b.tok]
        return (list(reads) + extra) if extra else reads

    def op(self, eng, fn, reads=(), writes=(), pe_acc=False):
        reads = self._with_tok(reads, writes)
        deps = self._deps(reads, writes)
        self._emit_waits(eng, deps, skip_self=pe_acc)
        self.cnt[eng] += 1
        idx = self.cnt[eng]
        self.ops[eng].append(("op", fn, self.sems[eng], 1))
        self._record(eng, idx, eng, reads, writes)

    def dma(self, q, dsem, fn, reads=(), writes=()):
        self.dsem(dsem)
        reads = self._with_tok(reads, writes)
        deps = self._deps(reads, writes)
        self._emit_waits(q, deps)
        self.dcnt[dsem] += 16
        idx = self.dcnt[dsem]
        self.ops[q].append(("op", fn, self.sems[dsem], 16))
        self._record(dsem, idx, q, reads, writes)

    def replay(self, eng, e):
        for item in self.ops[eng]:
            if item[0] == "wait":
                e.wait_ge(item[1], item[2])
            else:
                item[1](e).then_inc(item[2], item[3])


def build_program(SEQ, DEPTH, NS, with_prompt=True, with_sample=True):
    TT = 512
    NT = SEQ // TT
    NJ = TT // CH
    nc = bass.Bass("TRN2", target_bir_lowering=False)

    def din(name, shape, dt=F32):
        return nc.dram_tensor(name, list(shape), dt, kind="ExternalInput").ap()

    def dout(name, shape, dt=F32):
        return nc.dram_tensor(name, list(shape), dt, kind="ExternalOutput").ap()

    xpT = din("xpT", [P, KC, SEQ])
    xsT = din("xsT", [P, KC, NS])
    sdS = din("sdS", [DEPTH, NS, P, H, P])
    scS = din("scS", [DEPTH, P, 24, NS, 3])
    spS = din("spS", [DEPTH, P, 8, NS, 15])
    WIN = din("WIN", [DEPTH, 20, P, KC, 256])
    WBA = din("WBA", [DEPTH, P, KC, 16])
    WOUT = din("WOUT", [DEPTH, 4, 2, P, 4, 1024])
    WG = din("WG", [DEPTH, 22, P, KC, 256])
    WU = din("WU", [DEPTH, 22, P, KC, 256])
    WD = din("WD", [DEPTH, 8, 4, P, 11, 256])
    WP = din("WP", [DEPTH, P, 4, 2, 256])
    VEC = din("VEC", [P, DEPTH, NV])
    VECF = din("VECF", [P, KC])
    AB = din("AB", [8, DEPTH, 2])
    CONST = din("CONST", [P, 1088])

    ypT = dout("ypT", [P, KC, SEQ])
    ysT = dout("ysT", [P, KC, NS])
    ndp = dout("ndp", [DEPTH, H, P, P])
    ncpT = dout("ncpT", [DEPTH, P, 24, 3])
    nppT = dout("nppT", [DEPTH, P, 8, 15])
    nds = dout("nds", [DEPTH, NS, P, H, P])
    ncsT = dout("ncsT", [DEPTH, P, 24, NS, 3])
    npsT = dout("npsT", [DEPTH, P, 8, NS, 15])

    with contextlib.ExitStack() as st:
        S = Sched(nc, st)
        _n = [0]

        def sb(shape, dt=F32, name=None):
            _n[0] += 1
            return st.enter_context(nc.sbuf_tensor(name or f"t{_n[0]}", list(shape), dt))

        PS = st.enter_context(nc.psum_tensor("PS", [P, 7 * 512], F32))
        PSB = st.enter_context(nc.psum_tensor("PSB", [P, 1024], BF16))
        pbufs = [Buf(f"ps{i}") for i in range(7)]
        psb_buf = Buf("psb")
        _pb = [0]

        def pbank():
            i = _pb[0] % 7
            _pb[0] += 1
            return PS[:, i * 512:(i + 1) * 512], pbufs[i]

        def mm(out, lhsT, rhs, start=True, stop=True, reads=(), writes=()):
            S.op("pe", lambda e: e.matmul(out, lhsT, rhs, start=start, stop=stop), reads=reads, writes=writes,
                 pe_acc=not start)

        def tr(out, in_, ident, reads=(), writes=()):
            S.op("pe", lambda e: e.transpose(out, in_, ident), reads=reads, writes=writes)

        def act(out, in_, func, reads=(), writes=(), bias=0.0, scale=1.0):
            S.op("act", lambda e: e.activation(out=out, in_=in_, func=func, bias=bias, scale=scale),
                 reads=reads, writes=writes)

        def tt(eng, out, in0, in1, op, reads=(), writes=()):
            S.op(eng, lambda e: e.tensor_tensor(out=out, in0=in0, in1=in1, op=op), reads=reads, writes=writes)

        def ts(eng, out, in0, s1, op0, reads=(), writes=()):
            S.op(eng, lambda e: e.tensor_scalar(out=out, in0=in0, scalar1=s1, scalar2=None, op0=op0),
                 reads=reads, writes=writes)

        def stt(eng, out, in0, scalar, in1, op0, op1, reads=(), writes=()):
            S.op(eng, lambda e: e.scalar_tensor_tensor(out=out, in0=in0, scalar=scalar, in1=in1, op0=op0, op1=op1),
                 reads=reads, writes=writes)

        def cp(eng, out, in_, reads=(), writes=()):
            if eng == "act":
                S.op("act", lambda e: e.copy(out, in_), reads=reads, writes=writes)
            else:
                S.op(eng, lambda e: e.tensor_copy(out=out, in_=in_), reads=reads, writes=writes)

        def recip(out, in_, reads=(), writes=()):
            S.op("dve", lambda e: e.reciprocal(out=out, in_=in_), reads=reads, writes=writes)

        def memset(eng, ap, val, writes=()):
            S.op(eng, lambda e: e.memset(ap, val), writes=writes)

        def dma(q, key, out, in_, reads=(), writes=()):
            S.dma(q, key, lambda e: e.dma_start(out=out, in_=in_), reads=reads, writes=writes)

        const = sb([P, 1088], F32, "const")
        b_const = Buf("const")
        dma("sp", "c0", const[:], CONST[:, :], writes=[b_const])
        ident = const[:, 0:128]
        maskT = const[:, 128:256]
        strict01 = const[:, 256:384]
        bd16 = const[:, 384:512]
        Lm = [const[:, 512 + i * 128:640 + i * 128] for i in range(3)]
        invc = const[:, 896:960]
        ones_f = const[:, 960:1088]
        vec = sb([P, DEPTH, NV], F32, "vec")
        vecf = sb([P, KC], F32, "vecf")
        ab = sb([8, DEPTH, 2], F32, "ab")
        nexpA = sb([8, DEPTH], F32, "nexpA")
        scratch = sb([P, 8], F32, "scratch")
        dma("sp", "c1", vec[:], VEC[:, :, :], writes=[b_const])
        dma("sp", "c2", vecf[:], VECF[:, :], writes=[b_const])
        dma("sp", "c3", ab[:], AB[:, :, :], writes=[b_const])
        ident_b = sb([P, P], BF16, "ident_b")
        onesD_b = sb([P, P], BF16, "onesD_b")
        onesV_b = sb([P, P], BF16, "onesV_b")
        ones_b = sb([P, P], BF16, "ones_b")
        cp("dve", ident_b[:], ident, reads=[b_const], writes=[b_const])
        ts("dve", onesD_b[:], ones_f, 1.0 / D, ALU.mult, reads=[b_const], writes=[b_const])
        ts("dve", onesV_b[:], ones_f, 1.0 / P, ALU.mult, reads=[b_const], writes=[b_const])
        cp("dve", ones_b[:], ones_f, reads=[b_const], writes=[b_const])
        act(nexpA[:], ab[:, :, 0], AF.Exp, reads=[b_const], writes=[b_const])
        ts("dve", nexpA[:], nexpA[:], -1.0, ALU.mult, reads=[b_const], writes=[b_const])
        CB = [b_const]

        tokU = Buf("tokU")
        tokA = Buf("tokA")

        b_scr = Buf("scratch")

        def switch(tok):
            memset("dve", scratch[:, 0:1], 0.0, writes=[tok, b_scr])

        NSLOT = 3
        SLW = 4096
        wsl = [sb([P, SLW], BF16, f"wsl{i}") for i in range(NSLOT)]
        wbuf = [Buf(f"w{i}") for i in range(NSLOT)]
        _ws = [0]

        def wload(src, a, b):
            i = _ws[0] % NSLOT
            _ws[0] += 1
            view = wsl[i][:, 0:a * b].rearrange("p (a b) -> p a b", a=a)
            dma("pool", f"w{i}", view, src, writes=[wbuf[i]])
            return view, wbuf[i]

        wsm = sb([P, KC * 16 + 4 * 2 * 256], BF16, "wsm")
        b_wsm = Buf("wsm")
        wba = wsm[:, 0:KC * 16].rearrange("p (a b) -> p a b", a=KC)
        wp = wsm[:, KC * 16:].rearrange("p (g k n) -> p g k n", g=4, k=2)

        def load_small_w(l):
            dma("pool", "wsm", wba, WBA[l], writes=[b_wsm])
            dma("pool", "wsm", wp, WP[l], writes=[b_wsm])

        UW = 20736
        U = sb([P, UW], BF16, "U")

        def uview(off, n_bf16, dt, pattern=None, **kw):
            v = U[:, off:off + n_bf16]
            if dt == F32:
                v = v.bitcast(F32)
            if pattern:
                v = v.rearrange(pattern, **kw)
            return v

        b_sq2 = [Buf("sq0"), Buf("sq1")]
        b_rstd_t = Buf("rstd")

        def UB(name, tokA_=False):
            return Buf(name, tok=(tokU, tokA) if tokA_ else (tokU,))

        def make_act(N, hT, xn, sqs, rstd, actv, gtmps, b_act, b_gt):
            return dict(N=N, hT=hT, b_h=Buf("hT"), xn=xn, b_xn=Buf("xn"), sq=sqs, b_sq=b_sq2,
                        rstd=rstd, b_rstd=b_rstd_t, act=actv, b_act=b_act, gtmp=gtmps, b_gtmp=b_gt)

        def sumsq_rstd(A):
            N = A["N"]
            bank, bb = pbank()
            for kc in range(KC):
                sq, bsq = A["sq"][kc % 2], A["b_sq"][kc % 2]
                act(sq[:, 0:N], A["hT"][:, kc, :], AF.Square, reads=[A["b_h"]], writes=[bsq])
                mm(bank[:, 0:N], onesD_b[:], sq[:, 0:N], start=(kc == 0), stop=(kc == KC - 1), reads=[bsq] + CB, writes=[bb])
            act(A["rstd"][:, 0:N], bank[:, 0:N], AF.Ln, bias=EPS, reads=[bb], writes=[A["b_rstd"]])
            act(A["rstd"][:, 0:N], A["rstd"][:, 0:N], AF.Exp, scale=-0.5, reads=[A["b_rstd"]], writes=[A["b_rstd"]])

        def rmsnorm_to_xn(A, wcol):
            N = A["N"]
            sumsq_rstd(A)
            for kc in range(KC):
                stt("dve", A["xn"][:, kc, :], A["hT"][:, kc, :], wcol[:, kc:kc + 1], A["rstd"][:, 0:N], ALU.mult, ALU.mult,
                    reads=[A["b_h"], A["b_rstd"]] + CB, writes=[A["b_xn"]])

        def proj_chunks(A, wview, wb, nch, kcs, rhs_fn, rhs_bufs, handler):
            N = A["N"]
            for m in range(nch):
                bank, bb = pbank()
                for kc in range(kcs):
                    mm(bank[:, 0:N], wview[:, kc, m * 128:(m + 1) * 128], rhs_fn(kc), start=(kc == 0), stop=(kc == kcs - 1),
                       reads=[wb] + rhs_bufs, writes=[bb])
                handler(m, bank[:, 0:N], bb)

        def ffn(A, l):
            N = A["N"]
            rmsnorm_to_xn(A, vec[:, l, O_NF:O_NF + KC])
            xn, bxn = A["xn"], A["b_xn"]
            for g in range(2):
                for bi in range(11):
                    blk = g * 11 + bi
                    gt, bgt = A["gtmp"][bi % 2], A["b_gtmp"][bi % 2]
                    wv, wb = wload(WG[l, blk], KC, 256)

                    def h_gate(m, ps, bb, gt=gt, bgt=bgt):
                        act(gt[:, m, :], ps, AF.Silu, reads=[bb], writes=[bgt])
                    proj_chunks(A, wv, wb, 2, KC, lambda kc: xn[:, kc, :], [bxn], h_gate)
                    wv, wb = wload(WU[l, blk], KC, 256)

                    def h_up(m, ps, bb, gt=gt, bgt=bgt, bi=bi):
                        tt("dve", A["act"][:, bi * 2 + m, :], gt[:, m, :], ps, ALU.mult, reads=[bb, bgt], writes=[A["b_act"]])
                    proj_chunks(A, wv, wb, 2, KC, lambda kc: xn[:, kc, :], [bxn], h_up)
                for cb in range(8):
                    banks = [pbank() for _ in range(2)]
                    for kb in range(2):
                        wv, wb = wload(WD[l, cb, g * 2 + kb], 11, 256)
                        for m in range(2):
                            for kc in range(11):
                                mm(banks[m][0][:, 0:N], wv[:, kc, m * 128:(m + 1) * 128], A["act"][:, kb * 11 + kc, :],
                                   start=(kb == 0 and kc == 0), stop=(kb == 1 and kc == 10),
                                   reads=[wb, A["b_act"]], writes=[banks[m][1]])
                    for m in range(2):
                        oc = cb * 2 + m
                        tt("dve", A["hT"][:, oc, :], A["hT"][:, oc, :], banks[m][0][:, 0:N], ALU.add,
                           reads=[banks[m][1], A["b_h"]], writes=[A["b_h"]])

        def out_proj_part(A, l, q, mixh, bmix):
            for nb in range(2):
                wv, wb = wload(WOUT[l, q, nb], 4, 1024)

                def h_out(m, ps, bb, nb=nb):
                    oc = nb * 8 + m
                    tt("dve", A["hT"][:, oc, :], A["hT"][:, oc, :], ps, ALU.add, reads=[bb, A["b_h"]], writes=[A["b_h"]])
                proj_chunks(A, wv, wb, 8, 4, lambda kc: mixh[:, kc, :], [bmix], h_out)

        def final_norm_out(A, dst, key):
            N = A["N"]
            sumsq_rstd(A)
            for kc in range(KC):
                stt("dve", A["hT"][:, kc, :], A["hT"][:, kc, :], vecf[:, kc:kc + 1], A["rstd"][:, 0:N], ALU.mult, ALU.mult,
                    reads=[A["b_h"], A["b_rstd"]] + CB, writes=[A["b_h"]])
            dma("sp", key, dst, A["hT"][:], reads=[A["b_h"]])

        def gates_fm(A, l, braw, graw, b_g):
            N = A["N"]
            xn, bxn = A["xn"], A["b_xn"]
            bank, bb = pbank()
            for kc in range(KC):
                mm(bank[0:8, 0:N], wba[:, kc, 0:8], xn[:, kc, :], start=(kc == 0), stop=(kc == KC - 1), reads=[b_wsm, bxn], writes=[bb])
            act(braw, bank[0:8, 0:N], AF.Sigmoid, reads=[bb], writes=[b_g])
            bank, bb = pbank()
            for kc in range(KC):
                mm(bank[0:8, 0:N], wba[:, kc, 8:16], xn[:, kc, :], start=(kc == 0), stop=(kc == KC - 1), reads=[b_wsm, bxn], writes=[bb])
            act(graw, bank[0:8, 0:N], AF.Exp, bias=ab[:, l, 1:2], reads=[bb] + CB, writes=[b_g])
            act(graw, graw, AF.Ln, bias=1.0, reads=[b_g], writes=[b_g])
            ts("dve", graw, graw, nexpA[:, l:l + 1], ALU.mult, reads=[b_g] + CB, writes=[b_g])

        sq2 = [sb([P, TT], BF16) for _ in range(2)]
        rstd_t = sb([P, TT], F32)
        mixh = sb([P, 4, TT], BF16); b_mixh = Buf("mixh")
        g8a = sb([8, TT], F32); g8b = sb([8, TT], F32); g8c = sb([8, TT], F32); b_g8 = Buf("g8")

        if with_prompt:
            N = TT
            hT = sb([P, KC, N], F32, "hT")
            xn_t = sb([P, KC, N], BF16, "xn")
            pT = uview(0, 8432, F32, "p (c t) -> p c t", c=8); b_p = UB("p", True)
            OB = 8448
            qT = uview(OB, 2048, BF16, "p (h t) -> p h t", h=HB); b_q = UB("q")
            kT = uview(OB + 2048, 2048, BF16, "p (h t) -> p h t", h=HB); b_k = UB("k")
            vT = uview(OB + 4096, 2048, BF16, "p (h t) -> p h t", h=HB); b_v = UB("v")
            zT = uview(OB + 6144, 2048, BF16, "p (h t) -> p h t", h=HB); b_z = UB("z")
            OC = 16640
            itoff = [i * 1024 for i in range(8)] + [OC + i * 1024 for i in range(4)]

            def itmp(i, name):
                return uview(itoff[i], 512, BF16, "p (h c) -> p h c", h=HB), UB(name, True)
            Nn, b_N = itmp(0, "Nn"); N0, b_N0 = itmp(1, "N0"); M0, b_M0 = itmp(2, "M0")
            Ml = [itmp(3 + i, f"Ml{i}") for i in range(3)]
            Ra, b_Ra = itmp(6, "Ra"); Rb_, b_Rb = itmp(7, "Rb")
            N2, b_N2 = itmp(8, "N2"); M2, b_M2 = itmp(9, "M2"); Ta, b_Ta = itmp(10, "Ta"); Tb, b_Tb = itmp(11, "Tb")
            N4, b_N4, M4, b_M4 = N0, b_N0, M0, b_M0
            M8, b_M8 = N2, b_N2
            Yt, b_Y = Nn, b_N
            actv = uview(0, 11264, BF16, "p (c t) -> p c t", c=22); b_act = UB("act")
            gtmps = [uview(11264 + i * 2048, 2048, F32, "p (c t) -> p c t", c=2) for i in range(2)]
            b_gt = [UB("g0"), UB("g1")]
            A = make_act(N, hT, xn_t, sq2, rstd_t, actv, gtmps, b_act, b_gt)

            pre = [sb([P, 3 + N], F32) for _ in range(2)]; b_pre = [Buf("pre0"), Buf("pre1")]
            ycv = [sb([P, 15 + N], F32) for _ in range(2)]; b_ycv = [Buf("y0"), Buf("y1")]
            for _i in range(2):
                memset("dve", ycv[_i][:], 0.0, writes=[b_ycv[_i]])
            rn = [sb([P, N], F32) for _ in range(2)]; b_rn = [Buf("rn0"), Buf("rn1")]
            chalo = sb([P, DEPTH, 24, 3], F32); b_chalo = Buf("chalo")
            phalo = sb([P, DEPTH, 8, 15], F32); b_phalo = Buf("phalo")
            memset("dve", chalo[:], 0.0, writes=[b_chalo])
            memset("dve", phalo[:], 0.0, writes=[b_phalo])
            Sst = [sb([P, HB, P], F32) for _ in range(2)]; b_S = [Buf("S0"), Buf("S1")]
            Sb = sb([P, HB, P], BF16); b_Sb = Buf("Sb")
            Xg = sb([8, HB, CH], F32); b_X = Buf("X")
            gc_tok = sb([P, NJ, H], F32); be_tok = sb([P, NJ, H], F32); b_tok = Buf("tok")
            bg_tok = sb([P, NJ, H], F32); kd_tok = sb([P, NJ, H], F32); el_tok = sb([P, NJ, H], F32)
            glb = sb([8, P], F32); b_glb = Buf("glb")
            kbT = sb([P, HB, CH], BF16); qgT = sb([P, HB, CH], BF16); b_kq = Buf("kbqg")
            dm = sb([P, HB, CH], F32); decTs = sb([P, HB, CH], F32); b_dec = Buf("dec")
            kbg = sb([P, HB, CH], BF16); kdec = sb([P, HB, CH], BF16); vb = sb([P, HB, CH], BF16); b_tokm = Buf("tokm")
            qkTb = sb([P, HB, CH], BF16); b_qk = Buf("qk")
            oTc = sb([P, HB, CH], F32); b_oTc = Buf("oTc")
            sqo = sb([P, HB, CH], BF16); rno = sb([P, HB, CH], F32); b_no = Buf("no")
            dg = [sb([P, 2, N], BF16) for _ in range(2)]; b_dg = [Buf("dg0"), Buf("dg1")]
            Rbf = sb([P, HB, CH], BF16); b_Rbf = Buf("Rbf")
            wTn = sb([P, HB, CH], BF16); b_wTn = Buf("wTn")
            vnew = sb([P, HB, CH], BF16); b_vnew = Buf("vnew")

            HT = []
            for _sl in range(2):
                if _sl == 0:
                    _t = dict(dm=dm, decTs=decTs, kbT=kbT, qgT=qgT, kbg=kbg, kdec=kdec, vb=vb, qkTb=qkTb, Rbf=Rbf)
                else:
                    _t = dict(dm=sb([P, HB, CH], F32), decTs=sb([P, HB, CH], F32))
                    for _k in ("kbT", "qgT", "kbg", "kdec", "vb", "qkTb", "Rbf"):
                        _t[_k] = sb([P, HB, CH], BF16)
                _t.update(b_dec=Buf(f"dec{_sl}"), b_kq=Buf(f"kbqg{_sl}"), b_tokm=Buf(f"tokm{_sl}"), b_qk=Buf(f"qk{_sl}"), b_Rbf=Buf(f"Rbf{_sl}"))
                _t["it"] = [(uview(itoff[i] + 512 * _sl, 512, BF16, "p (h c) -> p h c", h=HB), UB(f"it{_sl}_{i}", True)) for i in range(12)]
                HT.append(_t)

            def bc_h(ap2d, nh):
                return ap2d.unsqueeze(1).to_broadcast([P, nh, CH])
            b_ndp = [[Buf(f"ndp{l}_{hb}") for hb in range(2)] for l in range(DEPTH)]

            v3 = lambda x: x.rearrange("p (h c) -> p h c", h=HB)
            f2 = lambda x: x[:].rearrange("p j h -> p (j h)")

            for t in range(NT):
                tok0 = t * TT
                dma("sp", "xin", hT[:], xpT[:, :, tok0:tok0 + TT], writes=[A["b_h"]])
                for l in range(DEPTH):
                    load_small_w(l)
                    rmsnorm_to_xn(A, vec[:, l, O_NM:O_NM + KC])
                    xn, bxn = A["xn"], A["b_xn"]
                    switch(tokU)
                    switch(tokA)
                    braw, gcA, gcB = g8a, g8b, g8c
                    gates_fm(A, l, braw[:], gcA[:], b_g8)
                    src, dst = gcA, gcB
                    sh = 1
                    while sh < CH:
                        s3 = src[:].rearrange("p (j c) -> p j c", c=CH)
                        d3 = dst[:].rearrange("p (j c) -> p j c", c=CH)
                        cp("dve", d3[:, :, 0:sh], s3[:, :, 0:sh], reads=[b_g8], writes=[b_g8])
                        tt("dve", d3[:, :, sh:], s3[:, :, sh:], s3[:, :, 0:CH - sh], ALU.add, reads=[b_g8], writes=[b_g8])
                        src, dst = dst, src
                        sh *= 2
                    gcT = src
                    bank, bb = pbank()
                    for j in range(NJ):
                        tr(bank[:, j * 8:(j + 1) * 8], gcT[:, j * CH:(j + 1) * CH], ident[0:8, 0:8], reads=[b_g8] + CB, writes=[bb])
                        tr(bank[:, 64 + j * 8:64 + (j + 1) * 8], braw[:, j * CH:(j + 1) * CH], ident[0:8, 0:8], reads=[b_g8] + CB, writes=[bb])
                    cp("dve", f2(gc_tok), bank[:, 0:NJ * 8], reads=[bb], writes=[b_tok])
                    cp("dve", f2(be_tok), bank[:, 64:64 + NJ * 8], reads=[bb], writes=[b_tok])
                    bank, bb = pbank()
                    for j in range(NJ):
                        col = j * CH + CH - 1
                        ts("dve", glb[:], ones_f[0:8, :], gcT[:, col:col + 1], ALU.mult, reads=[b_g8] + CB, writes=[b_glb])
                        mm(bank[:, j * 8:(j + 1) * 8], glb[:], ident[0:8, 0:8], reads=[b_glb] + CB, writes=[bb])
                    tt("dve", f2(kd_tok), bank[:, 0:NJ * 8], f2(gc_tok), ALU.subtract, reads=[bb, b_tok], writes=[b_tok])
                    act(f2(kd_tok), f2(kd_tok), AF.Exp, reads=[b_tok], writes=[b_tok])
                    act(f2(el_tok), bank[:, 0:NJ * 8], AF.Exp, reads=[bb], writes=[b_tok])
                    act(f2(bg_tok), f2(gc_tok), AF.Exp, reads=[b_tok], writes=[b_tok])
                    tt("dve", f2(bg_tok), f2(bg_tok), f2(be_tok), ALU.mult, reads=[b_tok], writes=[b_tok])

                    for pb in range(4):
                        wv, wb = wload(WIN[l, 16 + pb], KC, 256)

                        def h_p(m, ps, bb, pb=pb):
                            ci = pb * 2 + m
                            cp("dve", pT[:, ci, 0:15], phalo[:, l, ci, :], reads=[b_phalo], writes=[b_p])
                            cp("act", pT[:, ci, 15:15 + N], ps, reads=[bb], writes=[b_p])
                            cp("dve", phalo[:, l, ci, :], pT[:, ci, N:N + 15], reads=[b_p], writes=[b_phalo])
                        proj_chunks(A, wv, wb, 2, KC, lambda kc: xn[:, kc, :], [bxn], h_p)
                    if t == NT - 1:
                        dma("sp", "npp", nppT[l], phalo[:, l, :, :], reads=[b_phalo])
                    for half in range(2):
                        for gg in range(2):
                            g = half * 2 + gg
                            win = POOL_WINDOWS[g]
                            dgt, bdg = dg[g % 2], b_dg[g % 2]
                            L = 15 + N
                            for c2 in range(2):
                                ci = g * 2 + c2
                                cur, curb = pT[:, ci, :], b_p
                                sh = 1
                                k2 = 0
                                while sh < win:
                                    nxt, nb_ = ycv[k2 % 2], b_ycv[k2 % 2]
                                    tt("dve", nxt[:, sh:L], cur[:, sh:L], cur[:, 0:L - sh], ALU.add, reads=[curb], writes=[nb_])
                                    cur, curb = nxt[:, :], nb_
                                    k2 += 1
                                    sh *= 2
                                stt("dve", dgt[:, c2, :], cur[:, 15:15 + N], 1.0 / win, pT[:, ci, 15:15 + N], ALU.mult, ALU.subtract,
                                    reads=[curb, b_p], writes=[bdg])
                                if t == 0:
                                    o_, ob_ = ycv[k2 % 2], b_ycv[k2 % 2]
                                    tt("dve", o_[:, 0:15], cur[:, 15:30], invc[:, g * 16:g * 16 + 15], ALU.mult, reads=[curb] + CB, writes=[ob_])
                                    tt("dve", dgt[:, c2, 0:15], o_[:, 0:15], pT[:, ci, 15:30], ALU.subtract, reads=[ob_, b_p], writes=[bdg])
                            for m in range(2):
                                bank, bb = pbank()
                                for kc in range(2):
                                    mm(bank[:, 0:N], wp[:, g, kc, m * 128:(m + 1) * 128], dgt[:, kc, :], start=(kc == 0), stop=(kc == 1),
                                       reads=[b_wsm, bdg], writes=[bb])
                                ci = g * 2 + m
                                ts("dve", mixh[:, gg * 2 + m, :], bank[:, 0:N], vec[:, l, O_PS + ci:O_PS + ci + 1], ALU.mult,
                                   reads=[bb] + CB, writes=[b_mixh])
                        out_proj_part(A, l, 2 + half, mixh, b_mixh)
                    switch(tokA)

                    for hb in range(2):
                        cwv = vec[:, l, O_CW:O_CW + 96].rearrange("p (c i) -> p c i", i=4)
                        _ci = [0]
                        for typ in range(4):
                            for sub in range(2):
                                wv, wb = wload(WIN[l, hb * 8 + typ * 2 + sub], KC, 256)

                                def h_in(m, ps, bb, typ=typ, sub=sub):
                                    h4 = sub * 2 + m
                                    if typ == 3:
                                        act(zT[:, h4, :], ps, AF.Silu, reads=[bb], writes=[b_z])
                                        return
                                    ci = typ * 8 + hb * 4 + h4
                                    i2 = _ci[0] % 2
                                    _ci[0] += 1
                                    pr, bpr = pre[i2], b_pre[i2]
                                    cp("dve", pr[:, 0:3], chalo[:, l, ci, :], reads=[b_chalo], writes=[bpr])
                                    cp("act", pr[:, 3:3 + N], ps, reads=[bb], writes=[bpr])
                                    cp("dve", chalo[:, l, ci, :], pr[:, N:N + 3], reads=[bpr], writes=[b_chalo])
                                    y, by = ycv[i2], b_ycv[i2]
                                    ts("dve", y[:, 0:N], pr[:, 0:N], cwv[:, ci, 0:1], ALU.mult, reads=[bpr] + CB, writes=[by])
                                    for i in range(1, 4):
                                        stt("dve", y[:, 0:N], pr[:, i:i + N], cwv[:, ci, i:i + 1], y[:, 0:N], ALU.mult, ALU.add,
                                            reads=[bpr, by] + CB, writes=[by])
                                    dstT, bdst = ((qT, b_q), (kT, b_k), (vT, b_v))[typ]
                                    act(dstT[:, h4, :], y[:, 0:N], AF.Silu, reads=[by], writes=[bdst])
                                proj_chunks(A, wv, wb, 2, KC, lambda kc: xn[:, kc, :], [bxn], h_in)
                        for typ in (1, 0):
                            dstT, bdst = (qT, b_q) if typ == 0 else (kT, b_k)
                            nbanks = []
                            for h4 in range(HB):
                                sqb, bsqb = A["sq"][h4 % 2], A["b_sq"][h4 % 2]
                                act(sqb[:, 0:N], dstT[:, h4, :], AF.Square, reads=[bdst], writes=[bsqb])
                                bk2, bb2 = pbank()
                                mm(bk2[:, 0:N], ones_b[:], sqb[:, 0:N], reads=[bsqb] + CB, writes=[bb2])
                                nbanks.append((bk2, bb2))

                            def scale(h4, typ=typ, dstT=dstT, bdst=bdst):
                                i2 = h4 % 2
                                if typ == 0:
                                    stt("dve", dstT[:, h4, :], dstT[:, h4, :], float(P) ** -0.5, rn[i2][:], ALU.mult, ALU.mult,
                                        reads=[bdst, b_rn[i2]], writes=[bdst])
                                else:
                                    tt("dve", dstT[:, h4, :], dstT[:, h4, :], rn[i2][:], ALU.mult, reads=[bdst, b_rn[i2]], writes=[bdst])
                            for h4 in range(HB):
                                i2 = h4 % 2
                                bk2, bb2 = nbanks[h4]
                                act(rn[i2][:], bk2[:, 0:N], AF.Ln, bias=EPS, reads=[bb2], writes=[b_rn[i2]])
                                act(rn[i2][:], rn[i2][:], AF.Exp, scale=-0.5, reads=[b_rn[i2]], writes=[b_rn[i2]])
                                if h4 >= 1:
                                    scale(h4 - 1)
                            scale(HB - 1)
                        if t == NT - 1 and hb == 1:
                            dma("sp", "ncp", ncpT[l], chalo[:, l, :, :], reads=[b_chalo])
                        Sh, bSh = Sst[hb], b_S[hb]
                        if t == 0:
                            memset("dve", Sh[:], 0.0, writes=[bSh])
                            memset("dve", Sb[:], 0.0, writes=[b_Sb])
                        else:
                            dma("sp", f"sld{hb}", Sh[:], ndp[l, hb * HB:(hb + 1) * HB].rearrange("h d v -> d h v"), reads=[b_ndp[l][hb]], writes=[bSh])
                            cp("act", Sb[:], Sh[:], reads=[bSh], writes=[b_Sb])
                        hs = slice(hb * HB, (hb + 1) * HB)
                        i8b = ident[0:8, hb * HB:(hb + 1) * HB].unsqueeze(2).to_broadcast([8, HB, CH])
                        k3 = PSB[:, 0:512].rearrange("p (h c) -> p h c", h=HB)
                        v3b = PSB[:, 512:1024].rearrange("p (h c) -> p h c", h=HB)

                        def mm4(lh, blh, rh, brh):
                            bank, bb = pbank()
                            for h4 in range(HB):
                                mm(bank[:, h4 * CH:(h4 + 1) * CH], lh[:, h4, :], rh[:, h4, :], reads=[blh, brh], writes=[bb])
                            return v3(bank), bb

                        def head(j, T):
                            cs = slice(j * CH, (j + 1) * CH)
                            dm, decTs, b_dec = T["dm"], T["decTs"], T["b_dec"]
                            kbT, qgT, b_kq = T["kbT"], T["qgT"], T["b_kq"]
                            kbg, kdec, vb, b_tokm = T["kbg"], T["kdec"], T["vb"], T["b_tokm"]
                            qkTb, b_qk, Rbf, b_Rbf = T["qkTb"], T["b_qk"], T["Rbf"], T["b_Rbf"]
                            (Nn, b_N), (N0, b_N0), (M0, b_M0) = T["it"][0], T["it"][1], T["it"][2]
                            Ml = [T["it"][3 + i] for i in range(3)]
                            (Ra, b_Ra), (Rb_, b_Rb) = T["it"][6], T["it"][7]
                            (N2, b_N2), (M2, b_M2), (Ta, b_Ta), (Tb, b_Tb) = T["it"][8], T["it"][9], T["it"][10], T["it"][11]
                            N4, b_N4, M4, b_M4 = N0, b_N0, M0, b_M0
                            M8, b_M8 = N2, b_N2
                            Yt, b_Y = Nn, b_N
                            tt("dve", Xg[:], gcT[:, cs].unsqueeze(1).to_broadcast([8, HB, CH]), i8b, ALU.mult, reads=[b_g8] + CB, writes=[b_X])
                            GBk, GBb = pbank()
                            mm(GBk, ones_f[0:8, :], Xg[:].rearrange("p h c -> p (h c)"), reads=[b_X] + CB, writes=[GBb])
                            g3 = v3(GBk)
                            act(dm[:], g3, AF.Exp, reads=[GBb], writes=[b_dec])
                            tt("dve", qgT[:], qT[:, :, cs], dm[:], ALU.mult, reads=[b_q, b_dec], writes=[b_kq])
                            for h4 in range(HB):
                                h = hb * HB + h4
                                stt("dve", dm[:, h4, :], g3[:, h4, :], gc_tok[:, j, h:h + 1], maskT, ALU.subtract, ALU.min,
                                    reads=[GBb, b_tok] + CB, writes=[b_dec])
                            yield
                            act(dm[:], dm[:], AF.Exp, reads=[b_dec], writes=[b_dec])
                            tt("dve", decTs[:], dm[:], bc_h(strict01, HB), ALU.mult, reads=[b_dec] + CB, writes=[b_dec])
                            yield
                            tt("dve", Xg[:], braw[:, cs].unsqueeze(1).to_broadcast([8, HB, CH]), i8b, ALU.mult, reads=[b_g8] + CB, writes=[b_X])
                            BBk, BBb = pbank()
                            mm(BBk, ones_f[0:8, :], Xg[:].rearrange("p h c -> p (h c)"), reads=[b_X] + CB, writes=[BBb])
                            tt("dve", kbT[:], kT[:, :, cs], v3(BBk), ALU.mult, reads=[b_k, BBb], writes=[b_kq])
                            yield
                            bank, bb = pbank()
                            for h4 in range(HB):
                                mm(bank[:, h4 * CH:(h4 + 1) * CH], kT[:, h4, cs], kbT[:, h4, :], reads=[b_k, b_kq], writes=[bb])
                            tt("dve", Nn[:], v3(bank), decTs[:], ALU.mult, reads=[bb, b_dec], writes=[b_N])
                            tt("dve", N0[:], Nn[:], bc_h(bd16, HB), ALU.mult, reads=[b_N] + CB, writes=[b_N0])
                            yield
                            for h4 in range(HB):
                                tr(k3[:, h4, :], Nn[:, h4, :], ident_b[:], reads=[b_N] + CB, writes=[psb_buf])
                            tt("dve", M0[:], k3, bc_h(bd16, HB), ALU.mult, reads=[psb_buf] + CB, writes=[b_M0])
                            for i in range(3):
                                tt("dve", Ml[i][0][:], k3, bc_h(Lm[i], HB), ALU.mult, reads=[psb_buf] + CB, writes=[Ml[i][1]])
                            tt("dve", Ra[:], bc_h(ident, HB), N0[:], ALU.subtract, reads=[b_N0] + CB, writes=[b_Ra])
                            yield
                            Rc, bRc, Rn, bRn = Ra, b_Ra, Rb_, b_Rb
                            p1, bb1 = mm4(M0, b_M0, N0, b_N0)
                            p2, bb2 = mm4(N0, b_N0, M0, b_M0)
                            cp("act", N2[:], p1, reads=[bb1], writes=[b_N2])
                            cp("act", M2[:], p2, reads=[bb2], writes=[b_M2])
                            yield
                            p3, bb3 = mm4(M2, b_M2, Rc, bRc)
                            tt("dve", Rn[:], Rc[:], p3, ALU.add, reads=[bb3, bRc], writes=[bRn])
                            Rc, bRc, Rn, bRn = Rn, bRn, Rc, bRc
                            p1, bb1 = mm4(M2, b_M2, N2, b_N2)
                            p2, bb2 = mm4(N2, b_N2, M2, b_M2)
                            cp("act", N4[:], p1, reads=[bb1], writes=[b_N4])
                            cp("act", M4[:], p2, reads=[bb2], writes=[b_M4])
                            yield
                            p3, bb3 = mm4(M4, b_M4, Rc, bRc)
                            tt("dve", Rn[:], Rc[:], p3, ALU.add, reads=[bb3, bRc], writes=[bRn])
                            Rc, bRc, Rn, bRn = Rn, bRn, Rc, bRc
                            p2, bb2 = mm4(N4, b_N4, M4, b_M4)
                            cp("act", M8[:], p2, reads=[bb2], writes=[b_M8])
                            yield
                            p3, bb3 = mm4(M8, b_M8, Rc, bRc)
                            tt("dve", Rn[:], Rc[:], p3, ALU.add, reads=[bb3, bRc], writes=[bRn])
                            Rc, bRc, Rn, bRn = Rn, bRn, Rc, bRc
                            yield
                            for h4 in range(HB):
                                tr(k3[:, h4, :], Rc[:, h4, :], ident_b[:], reads=[bRc] + CB, writes=[psb_buf])
                            Tc, bTc, Tn, bTn = Ta, b_Ta, Tb, b_Tb
                            cp("act", Tc[:], k3, reads=[psb_buf], writes=[bTc])
                            yield
                            for i in range(3):
                                py, bby = mm4(Ml[i][0], Ml[i][1], Rc, bRc)
                                cp("act", Yt[:], py, reads=[bby], writes=[b_Y])
                                yield
                                pz, bbz = mm4(Tc, bTc, Yt, b_Y)
                                if i < 2:
                                    pzt, bbzt = mm4(Yt, b_Y, Tc, bTc)
                                    tt("dve", Rn[:], Rc[:], pz, ALU.subtract, reads=[bbz, bRc], writes=[bRn])
                                    tt("dve", Tn[:], Tc[:], pzt, ALU.subtract, reads=[bbzt, bTc], writes=[bTn])
                                    Rc, bRc, Rn, bRn = Rn, bRn, Rc, bRc
                                    Tc, bTc, Tn, bTn = Tn, bTn, Tc, bTc
                                else:
                                    tt("dve", Rbf[:], Rc[:], pz, ALU.subtract, reads=[bbz, bRc], writes=[b_Rbf])
                                yield
                            for h4 in range(HB):
                                tr(k3[:, h4, :], kT[:, h4, cs], ident_b[:], reads=[b_k] + CB, writes=[psb_buf])
                                tr(v3b[:, h4, :], vT[:, h4, cs], ident_b[:], reads=[b_v] + CB, writes=[psb_buf])
                            tt("dve", kbg[:], k3, bg_tok[:, j, hs].unsqueeze(2).to_broadcast([P, HB, CH]), ALU.mult, reads=[psb_buf, b_tok], writes=[b_tokm])
                            tt("dve", kdec[:], k3, kd_tok[:, j, hs].unsqueeze(2).to_broadcast([P, HB, CH]), ALU.mult, reads=[psb_buf, b_tok], writes=[b_tokm])
                            tt("dve", vb[:], v3b, be_tok[:, j, hs].unsqueeze(2).to_broadcast([P, HB, CH]), ALU.mult, reads=[psb_buf, b_tok], writes=[b_tokm])
                            bank, bb = pbank()
                            for h4 in range(HB):
                                mm(bank[:, h4 * CH:(h4 + 1) * CH], kT[:, h4, cs], qT[:, h4, cs], reads=[b_k, b_q], writes=[bb])
                            tt("dve", qkTb[:], v3(bank), dm[:], ALU.mult, reads=[bb, b_dec], writes=[b_qk])

                        def tail(j, T):
                            cs = slice(j * CH, (j + 1) * CH)
                            qgT, b_kq = T["qgT"], T["b_kq"]
                            kbg, kdec, vb, b_tokm = T["kbg"], T["kdec"], T["vb"], T["b_tokm"]
                            qkTb, b_qk, Rbf, b_Rbf = T["qkTb"], T["b_qk"], T["Rbf"], T["b_Rbf"]
                            bank, bb = pbank()
                            for h4 in range(HB):
                                mm(bank[:, h4 * CH:(h4 + 1) * CH], kbg[:, h4, :], Rbf[:, h4, :], reads=[b_tokm, b_Rbf], writes=[bb])
                            act(wTn[:], v3(bank), AF.Copy, scale=-1.0, reads=[bb], writes=[b_wTn])
                            bank, bb = pbank()
                            for h4 in range(HB):
                                mm(bank[:, h4 * CH:(h4 + 1) * CH], Rbf[:, h4, :], vb[:, h4, :], start=True, stop=False, reads=[b_Rbf, b_tokm], writes=[bb])
                                mm(bank[:, h4 * CH:(h4 + 1) * CH], wTn[:, h4, :], Sb[:, h4, :], start=False, stop=True, reads=[b_wTn, b_Sb], writes=[bb])
                            cp("act", vnew[:], v3(bank), reads=[bb], writes=[b_vnew])
                            bank, bb = pbank()
                            for h4 in range(HB):
                                mm(bank[:, h4 * CH:(h4 + 1) * CH], Sb[:, h4, :], qgT[:, h4, :], start=True, stop=False, reads=[b_Sb, b_kq], writes=[bb])
                                mm(bank[:, h4 * CH:(h4 + 1) * CH], vnew[:, h4, :], qkTb[:, h4, :], start=False, stop=True, reads=[b_vnew, b_qk], writes=[bb])
                            cp("act", oTc[:], v3(bank), reads=[bb], writes=[b_oTc])
                            bank, bb = pbank()
                            for h4 in range(HB):
                                mm(bank[:, h4 * CH:(h4 + 1) * CH], kdec[:, h4, :], vnew[:, h4, :], reads=[b_tokm, b_vnew], writes=[bb])
                            tt("dve", Sh[:], Sh[:], el_tok[:, j, hs].unsqueeze(2).to_broadcast([P, HB, CH]), ALU.mult, reads=[bSh, b_tok], writes=[bSh])
                            tt("dve", Sh[:], Sh[:], v3(bank), ALU.add, reads=[bb, bSh], writes=[bSh])
                            cp("act", Sb[:], Sh[:], reads=[bSh], writes=[b_Sb])
                            act(sqo[:], oTc[:], AF.Square, reads=[b_oTc], writes=[b_no])
                            bank, bb = pbank()
                            mm(bank, onesV_b[:], sqo[:].rearrange("p h c -> p (h c)"), reads=[b_no] + CB, writes=[bb])
                            act(rno[:], v3(bank), AF.Ln, bias=EPS, reads=[bb], writes=[b_no])
                            act(rno[:], rno[:], AF.Exp, scale=-0.5, reads=[b_no], writes=[b_no])
                            stt("dve", rno[:], oTc[:], vec[:, l, O_DN:O_DN + 1], rno[:], ALU.mult, ALU.mult, reads=[b_oTc, b_no] + CB, writes=[b_no])
                            tt("dve", mixh[:, :, cs], rno[:], zT[:, :, cs], ALU.mult, reads=[b_no, b_z], writes=[b_mixh])
                        for jp in range(0, NJ, 2):
                            gens = [head(jp, HT[0]), head(jp + 1, HT[1])]
                            while gens:
                                for g in list(gens):
                                    try:
                                        next(g)
                                    except StopIteration:
                                        gens.remove(g)
                            tail(jp, HT[0])
                            tail(jp + 1, HT[1])
                        dma("sp", f"sst{hb}", ndp[l, hb * HB:(hb + 1) * HB].rearrange("h d v -> d h v"), Sh[:], reads=[bSh], writes=[b_ndp[l][hb]])
                        out_proj_part(A, l, hb, mixh, b_mixh)
                    switch(tokU)
                    ffn(A, l)
                final_norm_out(A, ypT[:, :, tok0:tok0 + TT], "yout")

        if with_sample:
            N = NS
            NP = NS * H
            switch(tokU)
            hTs = sb([P, KC, N], F32, "hTs"); xns = sb([P, KC, N], BF16, "xns")
            acts = sb([P, 22, N], BF16, "acts"); gts = [sb([P, 2, N], F32) for _ in range(2)]
            B = make_act(N, hTs, xns, sq2, rstd_t, acts, gts, Buf("acts"), [Buf("gs0"), Buf("gs1")])
            Sin = [uview(i * 2048, 2048, F32, "p (h v) -> p h v", h=H) for i in range(2)]; b_Sin = [UB("sin0"), UB("sin1")]
            Sout = [uview(4096 + i * 2048, 2048, F32, "p (h v) -> p h v", h=H) for i in range(2)]; b_Sout = [UB("so0"), UB("so1")]
            s_pre = uview(8192, 3072, F32, "p (c b i) -> p c b i", c=24, i=4); b_spre = UB("spre")
            s_p = uview(11264, 4096, F32, "p (c b i) -> p c b i", c=8, i=16); b_sp = UB("spool")
            s_y = uview(15360, 768, F32, "p (c b) -> p c b", c=24); s_y2 = uview(16128, 768, F32, "p (c b) -> p c b", c=24); b_sy = UB("sy")
            o = [16896]

            def u128(n=128):
                v = uview(o[0], 2 * n, F32)
                o[0] += 2 * n
                return v
            kqf = u128(256); kq = kqf.rearrange("p (j t) -> p j t", t=2); b_kqs = UB("kq")
            kvqsf = u128(256); kvqs = kvqsf.rearrange("p (j t) -> p j t", t=2); b_kvqs = UB("kvqs")
            vTs = u128(); b_vs = UB("vs")
            zTs = u128(); b_zs = UB("zs")
            BBs = u128(); EGs = u128(); QKs = u128(); b_bc = UB("bcs")
            dvT = u128(); oTs = u128(); b_dv = UB("dv")
            s_rno = u128(); b_sno = UB("sno")
            assert o[0] <= UW
            Ktok = sb([P, P], F32); DVtok = sb([P, P], F32); b_ktok = Buf("ktok")
            Km = [sb([P, P], F32) for _ in range(2)]; b_Km = [Buf("km0"), Buf("km1")]
            s_sq = sb([P, 16 * NS], BF16); s_rn = sb([P, 16 * NS], F32); b_srn = Buf("srn")
            s_sum = sb([P, 8, NS], F32); s_d = sb([P, 8, NS], BF16); b_sd = Buf("sd")
            s_sqo = sb([P, NP], BF16)
            Xs = sb([8, NS, H], F32); b_Xs = Buf("Xs")
            mixs = sb([P, 4, N], BF16); b_mixs = Buf("mixs")

            dma("sp", "xin_s", hTs[:], xsT[:, :, :], writes=[B["b_h"]])
            for l in range(DEPTH):
                load_small_w(l)
                rmsnorm_to_xn(B, vec[:, l, O_NM:O_NM + KC])
                xn, bxn = B["xn"], B["b_xn"]
                dma("sp", "scl", s_pre[:, :, :, 0:3], scS[l], writes=[b_spre])
                dma("sp", "spl", s_p[:, :, :, 0:15], spS[l], writes=[b_sp])
                s_b8, s_g8, s_eg8 = g8a[:, 0:N], g8b[:, 0:N], g8c[:, 0:N]
                gates_fm(B, l, s_b8, s_g8, b_g8)
                act(s_eg8, s_g8, AF.Exp, reads=[b_g8], writes=[b_g8])
                i8 = ident[0:8, 0:8].unsqueeze(1).to_broadcast([8, NS, H])
                for srcv, dstv in ((s_b8, BBs), (s_eg8, EGs)):
                    tt("dve", Xs[:], srcv.unsqueeze(2).to_broadcast([8, NS, H]), i8, ALU.mult, reads=[b_g8] + CB, writes=[b_Xs])
                    bank, bb = pbank()
                    mm(bank[:, 0:NP], ones_f[0:8, :], Xs[:].rearrange("p b h -> p (b h)"), reads=[b_Xs] + CB, writes=[bb])
                    cp("act", dstv, bank[:, 0:NP], reads=[bb], writes=[b_bc])
                for pb in range(4):
                    wv, wb = wload(WIN[l, 16 + pb], KC, 256)

                    def h_ps(m, ps, bb, pb=pb):
                        cp("act", s_p[:, pb * 2 + m, :, 15], ps, reads=[bb], writes=[b_sp])
                    proj_chunks(B, wv, wb, 2, KC, lambda kc: xn[:, kc, :], [bxn], h_ps)
                dma("sp", "spo", npsT[l], s_p[:, :, :, 1:16], reads=[b_sp])
                for g in range(4):
                    win = POOL_WINDOWS[g]
                    S.op("dve", (lambda e, g=g, win=win: e.tensor_reduce(out=s_sum[:, 2 * g:2 * g + 2, :], in_=s_p[:, 2 * g:2 * g + 2, :, 16 - win:16],
                                                                      axis=AX.X, op=ALU.add)), reads=[b_sp], writes=[b_sd])
                    stt("dve", s_d[:, 2 * g:2 * g + 2, :], s_sum[:, 2 * g:2 * g + 2, :], 1.0 / win, s_p[:, 2 * g:2 * g + 2, :, 15], ALU.mult, ALU.subtract,
                        reads=[b_sd, b_sp], writes=[b_sd])
                for half in range(2):
                    for gg in range(2):
                        g = half * 2 + gg
                        for m in range(2):
                            bank, bb = pbank()
                            for kc in range(2):
                                mm(bank[:, 0:N], wp[:, g, kc, m * 128:(m + 1) * 128], s_d[:, g * 2 + kc, :], start=(kc == 0), stop=(kc == 1),
                                   reads=[b_wsm, b_sd], writes=[bb])
                            ci = g * 2 + m
                            ts("dve", mixs[:, gg * 2 + m, :], bank[:, 0:N], vec[:, l, O_PS + ci:O_PS + ci + 1], ALU.mult, reads=[bb] + CB, writes=[b_mixs])
                    out_proj_part(B, l, 2 + half, mixs, b_mixs)
                zT3 = zTs.rearrange("p (b h) -> p b h", h=H)
                for hb in range(2):
                    for typ in range(4):
                        for sub in range(2):
                            wv, wb = wload(WIN[l, hb * 8 + typ * 2 + sub], KC, 256)

                            def h_in_s(m, ps, bb, typ=typ, sub=sub, hb=hb):
                                hh = hb * 4 + sub * 2 + m
                                if typ == 3:
                                    act(zT3[:, :, hh], ps, AF.Silu, reads=[bb], writes=[b_zs])
                                else:
                                    cp("act", s_pre[:, typ * 8 + hh, :, 3], ps, reads=[bb], writes=[b_spre])
                            proj_chunks(B, wv, wb, 2, KC, lambda kc: xn[:, kc, :], [bxn], h_in_s)
                dma("sp", "sco", ncsT[l], s_pre[:, :, :, 1:4], reads=[b_spre])
                cwv = vec[:, l, O_CW:O_CW + 96].rearrange("p (c i) -> p c i", i=4)
                cwb = lambda i: cwv[:, :, i:i + 1].to_broadcast([P, 24, NS])
                tt("dve", s_y, s_pre[:, :, :, 0], cwb(0), ALU.mult, reads=[b_spre] + CB, writes=[b_sy])
                for i in range(1, 4):
                    tt("dve", s_y2, s_pre[:, :, :, i], cwb(i), ALU.mult, reads=[b_spre] + CB, writes=[b_sy])
                    tt("dve", s_y, s_y, s_y2, ALU.add, reads=[b_sy], writes=[b_sy])
                act(s_y, s_y, AF.Silu, reads=[b_sy], writes=[b_sy])
                act(s_sq[:].rearrange("p (c b) -> p c b", b=NS), s_y[:, 0:16, :], AF.Square, reads=[b_sy], writes=[b_srn])
                bank, bb = pbank()
                mm(bank[:, 0:16 * NS], ones_b[:], s_sq[:], reads=[b_srn] + CB, writes=[bb])
                act(s_rn[:], bank[:, 0:16 * NS], AF.Ln, bias=EPS, reads=[bb], writes=[b_srn])
                act(s_rn[:], s_rn[:], AF.Exp, scale=-0.5, reads=[b_srn], writes=[b_srn])
                rn3 = s_rn[:].rearrange("p (c b) -> p c b", b=NS)
                kq4 = kq.rearrange("p (b h) t -> p b h t", h=H)
                vT3 = vTs.rearrange("p (b h) -> p b h", h=H)
                for hh in range(H):
                    stt("dve", kq4[:, :, hh, 1], s_y[:, hh, :], float(P) ** -0.5, rn3[:, hh, :], ALU.mult, ALU.mult, reads=[b_sy, b_srn], writes=[b_kqs])
                    tt("dve", kq4[:, :, hh, 0], s_y[:, 8 + hh, :], rn3[:, 8 + hh, :], ALU.mult, reads=[b_sy, b_srn], writes=[b_kqs])
                    cp("dve", vT3[:, :, hh], s_y[:, 16 + hh, :], reads=[b_sy], writes=[b_vs])
                tt("dve", dvT, kq[:, :, 0], kq[:, :, 1], ALU.mult, reads=[b_kqs], writes=[b_dv])
                bank, bb = pbank()
                mm(bank[:, 0:NP], ones_f, dvT, reads=[b_dv] + CB, writes=[bb])
                cp("act", QKs, bank[:, 0:NP], reads=[bb], writes=[b_bc])
                kvbank, kvbb = pbank()
                for b in range(NS):
                    si = b % 2
                    dma(("sp", "act")[si], f"sin{si}", Sin[si], sdS[l, b], writes=[b_Sin[si]])
                    for hh in range(H):
                        jj = b * H + hh
                        mm(kvbank[:, jj * 2:jj * 2 + 2], Sin[si][:, hh, :], kq[:, jj, :], reads=[b_Sin[si], b_kqs], writes=[kvbb])
                cp("act", kvqsf, kvbank[:, 0:NP * 2], reads=[kvbb], writes=[b_kvqs])
                tt("dve", dvT, EGs, kvqs[:, :, 0], ALU.mult, reads=[b_bc, b_kvqs], writes=[b_dv])
                tt("dve", dvT, vTs, dvT, ALU.subtract, reads=[b_vs, b_dv], writes=[b_dv])
                tt("dve", dvT, dvT, BBs, ALU.mult, reads=[b_bc, b_dv], writes=[b_dv])
                tt("dve", oTs, EGs, kvqs[:, :, 1], ALU.mult, reads=[b_bc, b_kvqs], writes=[b_dv])
                tt("dve", s_rno, QKs, dvT, ALU.mult, reads=[b_bc, b_dv], writes=[b_sno])
                tt("dve", oTs, oTs, s_rno, ALU.add, reads=[b_sno, b_dv], writes=[b_dv])
                bank, bb = pbank()
                tr(bank[:, 0:P], kq[:, :, 0], ident, reads=[b_kqs] + CB, writes=[bb])
                tr(bank[:, P:2 * P], dvT, ident, reads=[b_dv] + CB, writes=[bb])
                cp("act", Ktok[:], bank[:, 0:P], reads=[bb], writes=[b_ktok])
                cp("act", DVtok[:], bank[:, P:2 * P], reads=[bb], writes=[b_ktok])
                for b in range(NS):
                    si = b % 2
                    dma("act", f"sin{si}", Sin[si], sdS[l, b], writes=[b_Sin[si]])
                    for hh in range(H):
                        jj = b * H + hh
                        ki = jj % 2
                        ts("dve", Km[ki][:], Ktok[:], ident[:, jj:jj + 1], ALU.mult, reads=[b_ktok] + CB, writes=[b_Km[ki]])
                        bank, bb = pbank()
                        mm(bank[:, 0:P], Km[ki][:], DVtok[:], reads=[b_Km[ki], b_ktok], writes=[bb])
                        stt("dve", Sout[si][:, hh, :], Sin[si][:, hh, :], EGs[:, jj:jj + 1], bank[:, 0:P], ALU.mult, ALU.add,
                            reads=[b_Sin[si], b_bc, bb], writes=[b_Sout[si]])
                    dma("sp", f"sout{si}", nds[l, b], Sout[si], reads=[b_Sout[si]])
                act(s_sqo[:], oTs, AF.Square, reads=[b_dv], writes=[b_sno])
                bank, bb = pbank()
                mm(bank[:, 0:NP], onesV_b[:], s_sqo[:], reads=[b_sno] + CB, writes=[bb])
                act(s_rno, bank[:, 0:NP], AF.Ln, bias=EPS, reads=[bb], writes=[b_sno])
                act(s_rno, s_rno, AF.Exp, scale=-0.5, reads=[b_sno], writes=[b_sno])
                stt("dve", s_rno, oTs, vec[:, l, O_DN:O_DN + 1], s_rno, ALU.mult, ALU.mult, reads=[b_dv, b_sno] + CB, writes=[b_sno])
                tt("dve", s_rno, s_rno, zTs, ALU.mult, reads=[b_sno, b_zs], writes=[b_sno])
                r3 = s_rno.rearrange("p (b h) -> p b h", h=H)
                for hb in range(2):
                    for h4 in range(HB):
                        cp("dve", mixs[:, h4, :], r3[:, :, hb * HB + h4], reads=[b_sno], writes=[b_mixs])
                    out_proj_part(B, l, hb, mixs, b_mixs)
                ffn(B, l)
            final_norm_out(B, ysT[:, :, :], "yout_s")

        for key, cntv in S.dcnt.items():
            if cntv > 0:
                S.ops["sp"].append(("wait", S.sems[key], cntv))
        build_program.stats = {e: len(S.ops[e]) for e in ENGS}

        with nc.Block() as block:
            @block.tensor
            def _(e):
                S.replay("pe", e)

            @block.scalar
            def _(e):
                S.replay("act", e)

            @block.vector
            def _(e):
                S.replay("dve", e)

            @block.gpsimd
            def _(e):
                S.replay("pool", e)

            @block.sync
            def _(e):
                S.replay("sp", e)
    return nc


def make_consts():
    c = np.zeros((P, 1088), np.float32)
    i = np.arange(P)
    s, cc = i[:, None], i[None, :]
    c[:, 0:128] = np.eye(P)
    c[:, 128:256] = np.where(cc >= s, 0.0, -1e4)
    c[:, 256:384] = (cc > s)
    c[:, 384:512] = (s // 16 == cc // 16)
    cp_, sf = i[:, None], i[None, :]
    c[:, 512:640] = (cp_ // 32 == sf // 32) & (cp_ % 32 >= 16) & (sf % 32 < 16)
    c[:, 640:768] = (cp_ // 64 == sf // 64) & (cp_ % 64 >= 32) & (sf % 64 < 32)
    c[:, 768:896] = (cp_ >= 64) & (sf < 64)
    for g, w in enumerate(POOL_WINDOWS):
        tpos = np.arange(16)
        c[:, 896 + g * 16:896 + (g + 1) * 16] = 1.0 / np.minimum(tpos + 1, w)
    c[:, 960:1088] = 1.0
    return c


def prep_weights(w_in, w_out, w_gate_up, w_down, w_pool, norm_mix, norm_ffn, conv_w, pool_scale, dn_norm,
                 a_log, dt_bias, norm_final):
    DEPTH = w_in.shape[0]
    f = np.float32

    def tile_cols(w, col0, ncols, kcs=KC):
        return np.ascontiguousarray(w[:, :, col0:col0 + ncols].reshape(DEPTH, kcs, P, ncols).transpose(0, 2, 1, 3))

    cols = []
    for hb in range(2):
        for typ in range(4):
            for sub in range(2):
                cols.append(typ * 1024 + hb * 512 + sub * 256)
    cols += [4112 + i * 256 for i in range(4)]
    WIN = np.stack([tile_cols(w_in, c0, 256) for c0 in cols], axis=1)
    WBA = tile_cols(w_in, 4096, 16)
    wo = w_out.reshape(DEPTH, 4, 4, P, 2, 1024)
    WOUT = np.ascontiguousarray(wo.transpose(0, 1, 4, 3, 2, 5))
    WG = np.stack([tile_cols(w_gate_up, b * 256, 256) for b in range(22)], axis=1)
    WU = np.stack([tile_cols(w_gate_up, DFF + b * 256, 256) for b in range(22)], axis=1)
    wd = w_down.reshape(DEPTH, 4, 11, P, 8, 256)
    WD = np.ascontiguousarray(wd.transpose(0, 4, 1, 3, 2, 5))
    WP = np.ascontiguousarray(w_pool.reshape(DEPTH, 4, 2, P, 256).transpose(0, 3, 1, 2, 4))
    VEC = np.zeros((P, DEPTH, NV), f)
    VEC[:, :, O_NM:O_NM + 16] = norm_mix.reshape(DEPTH, KC, P).transpose(2, 0, 1)
    VEC[:, :, O_NF:O_NF + 16] = norm_ffn.reshape(DEPTH, KC, P).transpose(2, 0, 1)
    VEC[:, :, O_CW:O_CW + 96] = conv_w.reshape(DEPTH, 4, 24, P).transpose(3, 0, 2, 1).reshape(P, DEPTH, 96)
    VEC[:, :, O_PS:O_PS + 8] = pool_scale.reshape(DEPTH, 8, P).transpose(2, 0, 1)
    VEC[:, :, O_DN] = dn_norm.T
    VECF = np.ascontiguousarray(norm_final.reshape(KC, P).T)
    AB = np.ascontiguousarray(np.stack([a_log, dt_bias], axis=-1).transpose(1, 0, 2))
    return dict(WIN=WIN, WBA=WBA, WOUT=WOUT, WG=WG, WU=WU, WD=WD, WP=WP, VEC=VEC, VECF=VECF, AB=AB,
                CONST=make_consts())


_NC_CACHE = {}


def run(x_prompt, x_sample, state_delta, state_conv, state_pool, norm_mix, w_in, conv_w, a_log, dt_bias, dn_norm,
        w_pool, pool_scale, w_out, norm_ffn, w_gate_up, w_down, norm_final, n_cores=8):
    asf = lambda a: np.asarray(a, np.float32)
    x_prompt, x_sample, state_delta, state_conv, state_pool = map(asf, (x_prompt, x_sample, state_delta, state_conv, state_pool))
    BATCH, SEQ, _ = x_prompt.shape
    DEPTH = w_in.shape[0]
    DECB = x_sample.shape[0]
    NS = DECB // n_cores
    wts = prep_weights(*map(asf, (w_in, w_out, w_gate_up, w_down, w_pool, norm_mix, norm_ffn, conv_w, pool_scale, dn_norm,
                                  a_log, dt_bias, norm_final)))
    key = (SEQ, DEPTH, NS)
    if key not in _NC_CACHE:
        _NC_CACHE[key] = build_program(SEQ, DEPTH, NS)
    nc = _NC_CACHE[key]
    in_maps = []
    for c in range(n_cores):
        b = c % BATCH
        sl = slice(c * NS, (c + 1) * NS)
        m = dict(wts)
        m["xpT"] = np.ascontiguousarray(x_prompt[b].reshape(SEQ, KC, P).transpose(2, 1, 0))
        m["xsT"] = np.ascontiguousarray(x_sample[sl, 0].reshape(NS, KC, P).transpose(2, 1, 0))
        m["sdS"] = np.ascontiguousarray(state_delta[:, sl].transpose(0, 1, 3, 2, 4))
        m["scS"] = np.ascontiguousarray(state_conv[:, sl].reshape(DEPTH, NS, 3, 24, P).transpose(0, 4, 3, 1, 2))
        m["spS"] = np.ascontiguousarray(state_pool[:, sl].reshape(DEPTH, NS, 15, 8, P).transpose(0, 4, 3, 1, 2))
        in_maps.append(m)
    res = run_bass_kernel_spmd(nc, in_maps, core_ids=list(range(n_cores)))
    R = res.results
    y_prompt = np.stack([R[b]["ypT"].transpose(2, 1, 0).reshape(SEQ, D) for b in range(BATCH)])
    y_sample = np.concatenate([R[c]["ysT"].transpose(2, 1, 0).reshape(NS, 1, D) for c in range(n_cores)])
    ndp = np.stack([R[b]["ndp"] for b in range(BATCH)], axis=1)
    ncp = np.stack([R[b]["ncpT"].transpose(0, 3, 2, 1).reshape(DEPTH, 3, 3072) for b in range(BATCH)], axis=1)
    npp = np.stack([R[b]["nppT"].transpose(0, 3, 2, 1).reshape(DEPTH, 15, 1024) for b in range(BATCH)], axis=1)
    nds = np.concatenate([R[c]["nds"].transpose(0, 1, 3, 2, 4) for c in range(n_cores)], axis=1)
    ncs = np.concatenate([R[c]["ncsT"].transpose(0, 3, 4, 2, 1).reshape(DEPTH, NS, 3, 3072) for c in range(n_cores)], axis=1)
    nps = np.concatenate([R[c]["npsT"].transpose(0, 3, 4, 2, 1).reshape(DEPTH, NS, 15, 1024) for c in range(n_cores)], axis=1)
    f = np.float32
    return tuple(np.ascontiguousarray(a, dtype=f) for a in (y_prompt, y_sample, ndp, ncp, npp, nds, ncs, nps))


def kernel(**inputs):
    return run(**inputs)
```
